# Optimizing a Trainium2 kernel written in Bass

```python
import math
import jax, jax.numpy as jnp
from jax import lax
import numpy as np

D_MODEL = 1024
BATCH = 8
SEQ = 4096
DEPTH = 4

N_MIXERS = 3
DA_HEADS = 8
DA_HEAD_DIM = 64
Q_BLOCK = 128
DIL_GROUPS = ((128, 1), (512, 4), (2048, 16))
DIL_HEADS = 8
DIL_HEAD_DIM = 128
DIL_BLOCK = 128
GLA_HEADS = 4
GLA_DK = D_MODEL // 2
GLA_DV = D_MODEL
GLA_GATE_RANK = 16
GLA_TAU = 16.0
GLA_CHUNK = 64
D_FF = 4 * D_MODEL
ALPHA = (2 * DEPTH) ** 0.25
BETA = (8 * DEPTH) ** -0.25
LN_EPS = 1e-5
RMS_EPS = 1e-6

kernel_name = "hybrid_diffattn_dilated_gla_deepnorm"


def layer_norm(x, g, b):
    xf = x.astype(jnp.float32)
    mu = jnp.mean(xf, axis=-1, keepdims=True)
    var = jnp.mean(jnp.square(xf - mu), axis=-1, keepdims=True)
    return ((xf - mu) * lax.rsqrt(var + LN_EPS) * g + b).astype(x.dtype)


def rms_norm(x, g):
    xf = x.astype(jnp.float32)
    return (xf * lax.rsqrt(jnp.mean(xf * xf, axis=-1, keepdims=True) + RMS_EPS) * g).astype(x.dtype)


def alibi_slopes(n_heads):
    return 2.0 ** (-8.0 * jnp.arange(1, n_heads + 1, dtype=jnp.float32) / n_heads)


def diff_lambda_init(layer_idx):
    return 0.8 - 0.6 * math.exp(-0.3 * layer_idx)


def diff_attention(x, w_in, lam_q1, lam_k1, lam_q2, lam_k2, subln_g, w_out, lambda_init):
    B, S, _ = x.shape
    H, d = DA_HEADS, DA_HEAD_DIM
    q, k, v = jnp.split(x @ w_in, 3, axis=-1)
    q = q.reshape(B, S, H, 2, d).transpose(0, 2, 3, 1, 4)
    k = k.reshape(B, S, H, 2, d).transpose(0, 2, 3, 1, 4)
    v = v.reshape(B, S, H, 2 * d).transpose(0, 2, 1, 3)
    lam = (jnp.exp(jnp.sum(lam_q1.astype(jnp.float32) * lam_k1.astype(jnp.float32)))
           - jnp.exp(jnp.sum(lam_q2.astype(jnp.float32) * lam_k2.astype(jnp.float32)))
           + lambda_init)
    slopes = alibi_slopes(H)
    scale = d ** -0.5
    outs = []
    for blk in range(S // Q_BLOCK):
        q0 = blk * Q_BLOCK
        kv_len = q0 + Q_BLOCK
        s = jnp.einsum('bhmqd,bhmkd->bhmqk', q[:, :, :, q0:kv_len],
                       k[:, :, :, :kv_len]).astype(jnp.float32) * scale
        dist = (q0 + jnp.arange(Q_BLOCK))[:, None] - jnp.arange(kv_len)[None, :]
        bias = -slopes[:, None, None] * dist.astype(jnp.float32)
        s = jnp.where(dist >= 0, s + bias[None, :, None], -jnp.inf)
        p = jax.nn.softmax(s, axis=-1)
        a = p[:, :, 0] - lam * p[:, :, 1]
        outs.append(jnp.einsum('bhqk,bhkd->bhqd', a.astype(v.dtype), v[:, :, :kv_len]))
    o = jnp.concatenate(outs, axis=2)
    o = rms_norm(o, subln_g) * (1.0 - lambda_init)
    return o.transpose(0, 2, 1, 3).reshape(B, S, H * 2 * d) @ w_out


def strided_window_attention(q, k, v, window, dil, slopes):
    B, S, H, dh = q.shape
    L = S // dil
    W = window // dil
    BLK = DIL_BLOCK
    nb = -(-L // BLK)
    Lp = nb * BLK

    def to_phase(t):
        t = t.reshape(B, L, dil, H, dh).transpose(0, 2, 3, 1, 4)
        return jnp.pad(t, ((0, 0), (0, 0), (0, 0), (0, Lp - L), (0, 0)))

    def with_prev(t):
        t = jnp.pad(t, ((0, 0), (0, 0), (0, 0), (BLK, 0), (0, 0)))
        prev = t[:, :, :, :Lp].reshape(B, dil, H, nb, BLK, dh)
        cur = t[:, :, :, BLK:].reshape(B, dil, H, nb, BLK, dh)
        return jnp.concatenate([prev, cur], axis=4)

    qb = to_phase(q).reshape(B, dil, H, nb, BLK, dh)
    kb = with_prev(to_phase(k))
    vb = with_prev(to_phase(v))
    s = jnp.einsum('brhnqd,brhnkd->brhnqk', qb, kb).astype(jnp.float32) * (dh ** -0.5)
    qi = jnp.arange(BLK)
    kj = jnp.arange(2 * BLK)
    dist = qi[:, None] + BLK - kj[None, :]
    kpos = jnp.arange(nb)[:, None] * BLK + kj[None, :] - BLK
    valid = ((dist >= 0) & (dist <= W))[None] & (kpos >= 0)[:, None, :]
    bias = -slopes[:, None, None] * (dist * dil).astype(jnp.float32)
    s = jnp.where(valid, s + bias[:, None], -jnp.inf)
    lse = jax.nn.logsumexp(s, axis=-1)
    p = jnp.exp(s - lse[..., None])
    o = jnp.einsum('brhnqk,brhnkd->brhnqd', p.astype(vb.dtype), vb)
    o = o.reshape(B, dil, H, Lp, dh)[:, :, :, :L].transpose(0, 3, 1, 2, 4).reshape(B, S, H, dh)
    lse = lse.reshape(B, dil, H, Lp)[..., :L].transpose(0, 3, 1, 2).reshape(B, S, H)
    return o, lse


def dilated_attention(x, w_in, w_out):
    B, S, _ = x.shape
    H, dh = DIL_HEADS, DIL_HEAD_DIM
    G = len(DIL_GROUPS)
    qkv = (x @ w_in).reshape(B, S, G, 3, H, dh)
    slopes = alibi_slopes(H)
    outs, lses = [], []
    for g, (window, dil) in enumerate(DIL_GROUPS):
        o, lse = strided_window_attention(qkv[:, :, g, 0], qkv[:, :, g, 1], qkv[:, :, g, 2],
                                          window, dil, slopes)
        outs.append(o)
        lses.append(lse)
    wts = jax.nn.softmax(jnp.stack(lses, axis=0), axis=0)
    o = jnp.sum(wts[..., None].astype(x.dtype) * jnp.stack(outs, axis=0), axis=0)
    return o.reshape(B, S, H * dh) @ w_out


def gla_attention(x, w_in, w_gate2, b_gate, gnorm_g, w_out):
    B, S, _ = x.shape
    H, C = GLA_HEADS, GLA_CHUNK
    dk, dv = GLA_DK // H, GLA_DV // H
    nc = S // C
    proj = x @ w_in
    q, k, v, r, g_low = jnp.split(
        proj, [GLA_DK, 2 * GLA_DK, 2 * GLA_DK + GLA_DV, 2 * GLA_DK + 2 * GLA_DV], axis=-1)
    log_a = jax.nn.log_sigmoid((g_low @ w_gate2 + b_gate).astype(jnp.float32)) / GLA_TAU

    def chunked(t, d):
        return t.astype(jnp.float32).reshape(B, nc, C, H, d).transpose(1, 0, 3, 2, 4)

    qc = chunked(q, dk) * (dk ** -0.5)
    kc = chunked(k, dk)
    vc = chunked(v, dv)
    bcum = jnp.cumsum(chunked(log_a, dk), axis=3)
    b_last = bcum[:, :, :, -1:]
    q_dec = qc * jnp.exp(bcum)
    k_intra = kc * jnp.exp(-bcum)
    k_state = kc * jnp.exp(b_last - bcum)
    causal = jnp.tril(jnp.ones((C, C), dtype=bool))
    s = jnp.where(causal, jnp.einsum('nbhqd,nbhkd->nbhqk', q_dec, k_intra), 0.0)
    o_intra = jnp.einsum('nbhqk,nbhkd->nbhqd', s, vc)

    def step(state, inp):
        q_c, k_c, v_c, decay = inp
        o = jnp.einsum('bhqd,bhde->bhqe', q_c, state)
        state = state * decay[:, :, 0, :, None] + jnp.einsum('bhkd,bhke->bhde', k_c, v_c)
        return state, o

    state0 = jnp.zeros((B, H, dk, dv), jnp.float32)
    _, o_inter = lax.scan(step, state0, (q_dec, k_state, vc, jnp.exp(b_last)))
    o = rms_norm(o_intra + o_inter, gnorm_g)
    o = o.transpose(1, 0, 3, 2, 4).reshape(B, S, H * dv).astype(x.dtype)
    o = o * jax.nn.silu(r)
    return o @ w_out


def squared_relu_mlp(x, w1, w2):
    return jnp.square(jax.nn.relu(x @ w1)) @ w2


def setup_inputs(seed: int = 0) -> dict:
    key = jax.random.key(seed)
    keys = iter(jax.random.split(key, 96))

    def dense(fan_in, fan_out, scale=1.0):
        return jax.random.normal(next(keys), (fan_in, fan_out), jnp.float32) * (scale * fan_in ** -0.5)

    def gain(n):
        return 1.0 + 0.02 * jax.random.normal(next(keys), (n,), jnp.float32)

    def small(n, scale=0.02):
        return scale * jax.random.normal(next(keys), (n,), jnp.float32)

    inputs = {"x": jax.random.normal(next(keys), (BATCH, SEQ, D_MODEL), jnp.float32)}
    for i in range(DEPTH):
        p = f"l{i}_"
        kind = i % N_MIXERS
        if kind == 0:
            inputs[p + "w_in"] = dense(D_MODEL, 3 * DA_HEADS * 2 * DA_HEAD_DIM)
            inputs[p + "lam_q1"] = small(DA_HEAD_DIM, 0.1)
            inputs[p + "lam_k1"] = small(DA_HEAD_DIM, 0.1)
            inputs[p + "lam_q2"] = small(DA_HEAD_DIM, 0.1)
            inputs[p + "lam_k2"] = small(DA_HEAD_DIM, 0.1)
            inputs[p + "subln_g"] = gain(2 * DA_HEAD_DIM)
            inputs[p + "w_out"] = dense(DA_HEADS * 2 * DA_HEAD_DIM, D_MODEL, BETA)
        elif kind == 1:
            inputs[p + "w_in"] = dense(D_MODEL, len(DIL_GROUPS) * 3 * DIL_HEADS * DIL_HEAD_DIM)
            inputs[p + "w_out"] = dense(DIL_HEADS * DIL_HEAD_DIM, D_MODEL, BETA)
        else:
            inputs[p + "w_in"] = dense(D_MODEL, 2 * GLA_DK + 2 * GLA_DV + GLA_GATE_RANK)
            inputs[p + "w_gate2"] = dense(GLA_GATE_RANK, GLA_DK)
            inputs[p + "b_gate"] = small(GLA_DK, 0.5)
            inputs[p + "gnorm_g"] = gain(GLA_DV // GLA_HEADS)
            inputs[p + "w_out"] = dense(GLA_DV, D_MODEL, BETA)
        inputs[p + "ln1_g"] = gain(D_MODEL)
        inputs[p + "ln1_b"] = small(D_MODEL)
        inputs[p + "w_ff1"] = dense(D_MODEL, D_FF)
        inputs[p + "w_ff2"] = dense(D_FF, D_MODEL, BETA)
        inputs[p + "ln2_g"] = gain(D_MODEL)
        inputs[p + "ln2_b"] = small(D_MODEL)
    return inputs


def reference(x,
              l0_w_in, l0_lam_q1, l0_lam_k1, l0_lam_q2, l0_lam_k2, l0_subln_g, l0_w_out,
              l0_ln1_g, l0_ln1_b, l0_w_ff1, l0_w_ff2, l0_ln2_g, l0_ln2_b,
              l1_w_in, l1_w_out,
              l1_ln1_g, l1_ln1_b, l1_w_ff1, l1_w_ff2, l1_ln2_g, l1_ln2_b,
              l2_w_in, l2_w_gate2, l2_b_gate, l2_gnorm_g, l2_w_out,
              l2_ln1_g, l2_ln1_b, l2_w_ff1, l2_w_ff2, l2_ln2_g, l2_ln2_b,
              l3_w_in, l3_lam_q1, l3_lam_k1, l3_lam_q2, l3_lam_k2, l3_subln_g, l3_w_out,
              l3_ln1_g, l3_ln1_b, l3_w_ff1, l3_w_ff2, l3_ln2_g, l3_ln2_b):
    layer_params = (
        ((l0_w_in, l0_lam_q1, l0_lam_k1, l0_lam_q2, l0_lam_k2, l0_subln_g, l0_w_out),
         (l0_ln1_g, l0_ln1_b, l0_w_ff1, l0_w_ff2, l0_ln2_g, l0_ln2_b)),
        ((l1_w_in, l1_w_out),
         (l1_ln1_g, l1_ln1_b, l1_w_ff1, l1_w_ff2, l1_ln2_g, l1_ln2_b)),
        ((l2_w_in, l2_w_gate2, l2_b_gate, l2_gnorm_g, l2_w_out),
         (l2_ln1_g, l2_ln1_b, l2_w_ff1, l2_w_ff2, l2_ln2_g, l2_ln2_b)),
        ((l3_w_in, l3_lam_q1, l3_lam_k1, l3_lam_q2, l3_lam_k2, l3_subln_g, l3_w_out),
         (l3_ln1_g, l3_ln1_b, l3_w_ff1, l3_w_ff2, l3_ln2_g, l3_ln2_b)),
    )
    for i in range(DEPTH):
        mix_p, (g1, b1, w1, w2, g2, b2) = layer_params[i]
        kind = i % N_MIXERS
        if kind == 0:
            y = diff_attention(x, *mix_p, lambda_init=diff_lambda_init(i))
        elif kind == 1:
            y = dilated_attention(x, *mix_p)
        else:
            y = gla_attention(x, *mix_p)
        x = layer_norm(ALPHA * x + y, g1, b1)
        x = layer_norm(ALPHA * x + squared_relu_mlp(x, w1, w2), g2, b2)
    return x
```

```python
import numpy as np
from contextlib import ExitStack

import concourse.bass as bass
import concourse.mybir as mybir
from concourse.bass_utils import run_bass_kernel_spmd

F32 = mybir.dt.float32
BF16 = mybir.dt.bfloat16
I32 = mybir.dt.int32
AF = mybir.ActivationFunctionType
ALU = mybir.AluOpType

D = 1024
S = 4096
DFF = 4096
DEPTH = 4
ALPHA = (2 * DEPTH) ** 0.25
LN_EPS = 1e-5
RMS_EPS = 1e-6
NCORES = 8
TT = 512
NT = S // TT
P = 128

ENGS = ("pe", "dve", "act", "pool", "sp")


class Tile:
    __slots__ = ("name", "writers", "readers", "sem")

    def __init__(self, name):
        self.name = name
        self.writers = []
        self.readers = []
        self.sem = None


class Op:
    __slots__ = ("eng", "fn", "deps", "is_dma", "semid", "sig", "sigval", "waits", "needed")

    def __init__(self, eng, fn, is_dma=False):
        self.eng = eng
        self.fn = fn
        self.deps = []
        self.is_dma = is_dma
        self.semid = None
        self.sig = False
        self.sigval = 0
        self.waits = []
        self.needed = False


class Sched:
    N_DMA_SEMS = 80

    def __init__(self, nc, stack):
        self.nc = nc
        self.eng_sem = {e: stack.enter_context(nc.semaphore("s_" + e)) for e in ENGS}
        self.dma_sems = [stack.enter_context(nc.semaphore("d%d" % i)) for i in range(self.N_DMA_SEMS)]
        self.eng_cnt = {e: 0 for e in ENGS}
        self.dma_cnt = [0] * self.N_DMA_SEMS
        self.waited = {}
        self.ops = []
        self.fence = None
        self.free_sems = list(range(24, self.N_DMA_SEMS))
        self.free_sems_sw = list(range(24))
        self.phase_tiles = []
        self.n_inst = 0

    def tile(self, name):
        t = Tile(name)
        self.phase_tiles.append(t)
        return t

    def tiles(self, name, n):
        return [self.tile("%s%d" % (name, i)) for i in range(n)]

    def _track(self, op, r, w):
        deps = op.deps
        if self.fence is not None:
            deps.append(self.fence)
        for t in r:
            deps.extend(t.writers)
            t.readers.append(op)
        for t in w:
            if t.readers:
                deps.extend(x for x in t.readers if x is not op)
                deps.extend(t.writers)
                t.writers = [op]
                t.readers = []
            else:
                if op.is_dma and t.writers and all(x.is_dma for x in t.writers):
                    t.writers.append(op)
                else:
                    deps.extend(t.writers)
                    t.writers = [op]
        self.ops.append(op)

    def op(self, eng, fn, r=(), w=()):
        o = Op(eng, fn)
        self._track(o, r, w)
        return o

    def dma(self, q, fn, r=(), w=(), key=None):
        o = Op(q, fn, is_dma=True)
        assert key is not None
        if key.sem is None:
            key.sem = self.free_sems_sw.pop() if q == "pool" else self.free_sems.pop()
        o.semid = key.sem
        self._track(o, r, w)
        return o

    def barrier(self):
        b = Op("sp", lambda e: e.nop())
        b.needed = True
        last = {}
        for o in self.ops:
            if o.is_dma:
                last[("d", o.semid)] = o
            else:
                last[o.eng] = o
        b.deps = list(last.values())
        if self.fence is not None:
            b.deps.append(self.fence)
        self.ops.append(b)
        self.fence = b
        for t in self.phase_tiles:
            t.writers = []
            t.readers = []

    def release_phase(self):
        for t in self.phase_tiles:
            if t.sem is not None:
                (self.free_sems_sw if t.sem < 24 else self.free_sems).append(t.sem)
                t.sem = None
        self.phase_tiles = []

    def emit(self):
        ops = self.ops
        self.ops = []
        for o in ops:
            for d in o.deps:
                if d.eng == "pe" and o.eng == "pe" and not d.is_dma and not o.is_dma:
                    continue
                d.needed = True
        for o in ops:
            if o.is_dma:
                self.dma_cnt[o.semid] += 16
                o.sigval = self.dma_cnt[o.semid]
            elif o.needed:
                self.eng_cnt[o.eng] += 1
                o.sigval = self.eng_cnt[o.eng]
                o.sig = True
            req = {}
            for d in o.deps:
                if d.is_dma:
                    k = ("d", d.semid)
                else:
                    if d.eng == "pe" and o.eng == "pe" and not o.is_dma:
                        continue
                    k = ("e", d.eng)
                if d.sigval > req.get(k, 0):
                    req[k] = d.sigval
            for k, v in req.items():
                wk = (o.eng, k)
                if self.waited.get(wk, 0) >= v:
                    continue
                self.waited[wk] = v
                o.waits.append((k, v))
        per = {e: [] for e in ENGS}
        for o in ops:
            per[o.eng].append(o)
        self.n_inst += len(ops)

        def run(e, lst):
            for o in lst:
                for (k, v) in o.waits:
                    sem = self.dma_sems[k[1]] if k[0] == "d" else self.eng_sem[k[1]]
                    e.wait_ge(sem, v)
                ins = o.fn(e)
                if o.is_dma:
                    ins.then_inc(self.dma_sems[o.semid], 16)
                elif o.sig:
                    ins.then_inc(self.eng_sem[o.eng], 1)

        with self.nc.Block() as block:
            @block.tensor
            def _(e):
                run(e, per["pe"])

            @block.vector
            def _(e):
                run(e, per["dve"])

            @block.scalar
            def _(e):
                run(e, per["act"])

            @block.gpsimd
            def _(e):
                run(e, per["pool"])

            @block.sync
            def _(e):
                run(e, per["sp"])


class Prog:
    def __init__(self, nc, sched, stack):
        self.nc = nc
        self.s = sched
        self.stack = stack
        self.dram = {}

    def sb(self, stack, name, shape, dt):
        return stack.enter_context(self.nc.sbuf_tensor(name, list(shape), dt))

    def ps(self, stack, name, shape, dt):
        return stack.enter_context(self.nc.psum_tensor(name, list(shape), dt))


def setup_consts(pg):
    nc, s = pg.nc, pg.s
    st = pg.stack
    pg.ident = pg.sb(st, "ident", [P, P], BF16)
    pg.ones_bf = pg.sb(st, "ones_bf", [P, P], BF16)
    pg.identf = pg.sb(st, "identf", [P, P], F32)
    t_id = s.tile("ident")
    s.op("pool", lambda e: e.memset(pg.identf[:], 1.0), w=[t_id])
    s.op("pool", lambda e: e.affine_select(pg.identf[:], pg.identf[:], [[-1, P]], ALU.is_equal, 0.0,
                                           base=0, channel_multiplier=1), r=[t_id], w=[t_id])
    s.op("pool", lambda e: e.tensor_copy(pg.ident[:], pg.identf[:]), r=[t_id], w=[t_id])
    s.op("pool", lambda e: e.memset(pg.ones_bf[:], 1.0), w=[t_id])
    pg.t_const = t_id
    pg.tri_le3 = pg.sb(st, "tri_le3", [P, 1, P], F32)
    pg.tri_ge3 = pg.sb(st, "tri_ge3", [P, 1, P], F32)
    s.op("pool", lambda e: e.memset(pg.tri_le3[:], 0.0), w=[t_id])
    s.op("pool", lambda e: e.affine_select(pg.tri_le3[:], pg.tri_le3[:], [[1, P]], ALU.is_ge, NEG,
                                           base=0, channel_multiplier=-1), r=[t_id], w=[t_id])
    s.op("pool", lambda e: e.memset(pg.tri_ge3[:], 0.0), w=[t_id])
    s.op("pool", lambda e: e.affine_select(pg.tri_ge3[:], pg.tri_ge3[:], [[-1, P]], ALU.is_ge, NEG,
                                           base=0, channel_multiplier=1), r=[t_id], w=[t_id])
    pg.tri_bf = pg.sb(st, "tri_bf", [P, 2, P], BF16)
    s.op("pool", lambda e: e.tensor_copy(pg.tri_bf[:, 0:1, :], pg.tri_le3[:]), r=[t_id], w=[t_id])
    s.op("pool", lambda e: e.tensor_copy(pg.tri_bf[:, 1:2, :], pg.tri_ge3[:]), r=[t_id], w=[t_id])
    pg.one_col = pg.sb(st, "one_col", [P, 1], F32)
    s.op("pool", lambda e: e.memset(pg.one_col[:], 1.0), w=[t_id])
    pg.neg_half = pg.sb(st, "neg_half", [P, 8], F32)
    s.op("pool", lambda e: e.memset(pg.neg_half[:], -0.5), w=[t_id])
    pg.eps_rms = pg.sb(st, "eps_rms", [P, 1], F32)
    s.op("pool", lambda e: e.memset(pg.eps_rms[:], RMS_EPS), w=[t_id])


def ln_tail(pg, xw, t_xw, gb, t_gb, xo_dram_rows, t_xo_dram, xTo, t_xTo, col0, bufs, i):
    s = pg.s
    st6 = bufs["st6"][i % 2]
    t_st = bufs["t_st"][i % 2]
    mv = bufs["mv"][i % 2]
    xbf = bufs["xbf"][i % 2]
    t_xbf = bufs["t_xbf"][i % 2]
    psT = bufs["psT"][i % 2]
    t_psT = bufs["t_psT"][i % 2]
    s.op("dve", lambda e: e.bn_stats(st6[:, 0:6], xw[:, 0:512]), r=[t_xw], w=[t_st])
    s.op("dve", lambda e: e.bn_stats(st6[:, 6:12], xw[:, 512:1024]), r=[t_xw], w=[t_st])
    s.op("dve", lambda e: e.bn_aggr(mv[:, 0:2], st6[:, 0:12]), r=[t_st], w=[t_st])
    s.op("act", lambda e: e.activation(mv[:, 2:3], mv[:, 1:2], AF.Sqrt, bias=pg.eps_ln[:, 0:1], scale=1.0),
         r=[t_st, pg.t_const], w=[t_st])
    s.op("dve", lambda e: e.reciprocal(mv[:, 3:4], mv[:, 2:3]), r=[t_st], w=[t_st])
    s.op("dve", lambda e: e.scalar_tensor_tensor(mv[:, 4:5], mv[:, 0:1], -1.0, mv[:, 3:4], ALU.mult, ALU.mult),
         r=[t_st], w=[t_st])
    s.op("act", lambda e: e.activation(xw[:], xw[:], AF.Identity, bias=mv[:, 4:5], scale=mv[:, 3:4]),
         r=[t_xw, t_st], w=[t_xw])
    g_bc, b_bc = gb
    s.op("pool", lambda e: e.tensor_tensor(xw[:], xw[:], g_bc[:], ALU.mult), r=[t_xw, t_gb], w=[t_xw])
    s.op("pool", lambda e: e.tensor_tensor(xw[:], xw[:], b_bc[:], ALU.add), r=[t_xw, t_gb], w=[t_xw])
    s.dma("sp", lambda e: e.dma_start(out=xo_dram_rows, in_=xw[:]), r=[t_xw], w=[t_xo_dram], key=t_xw)
    s.op("act", lambda e: e.copy(xbf[:], xw[:]), r=[t_xw], w=[t_xbf])

    def back():
        for j in range(8):
            s.op("pe", lambda e, j=j: e.transpose(psT[:, j, :], xbf[:, j * P:(j + 1) * P], pg.ident[:]),
                 r=[t_xbf, pg.t_const], w=[t_psT])
        s.op("dve", lambda e: e.tensor_copy(xTo[:, :, col0:col0 + P], psT[:]), r=[t_psT], w=[t_xTo])
    return back


def alloc_ln_bufs(pg, st, tag):
    s = pg.s
    b = {}
    b["st6"] = [pg.sb(st, "st6%s%d" % (tag, i), [P, 12], F32) for i in range(2)]
    b["mv"] = [pg.sb(st, "mv%s%d" % (tag, i), [P, 8], F32) for i in range(2)]
    b["t_st"] = s.tiles("t_st" + tag, 2)
    b["xbf"] = [pg.sb(st, "xbf%s%d" % (tag, i), [P, D], BF16) for i in range(2)]
    b["t_xbf"] = s.tiles("t_xbf" + tag, 2)
    b["psT"] = [pg.ps(st, "psT%s%d" % (tag, i), [P, 8, P], BF16) for i in range(1)] * 2
    b["t_psT"] = s.tiles("t_psT" + tag, 1) * 2
    return b


def load_gb(pg, st, g_ap, b_ap, tag):
    s = pg.s
    g_bc = pg.sb(st, "g_bc" + tag, [P, D], F32)
    b_bc = pg.sb(st, "b_bc" + tag, [P, D], F32)
    t_gb = s.tile("t_gb" + tag)
    s.dma("sp", lambda e: e.dma_start(out=g_bc[:], in_=g_ap.partition_broadcast(P)), w=[t_gb], key=t_gb)
    s.dma("sp", lambda e: e.dma_start(out=b_bc[:], in_=b_ap.partition_broadcast(P)), w=[t_gb], key=t_gb)
    return (g_bc, b_bc), t_gb


def phase_pre(pg, x_in, xT_dram):
    nc, s = pg.nc, pg.s
    with ExitStack() as st:
        xin = [pg.sb(st, "pre_x%d" % i, [P, D], F32) for i in range(3)]
        t_xin = s.tiles("pre_tx", 3)
        xbf = [pg.sb(st, "pre_xbf%d" % i, [P, D], BF16) for i in range(2)]
        t_xbf = s.tiles("pre_txbf", 2)
        psT = [pg.ps(st, "pre_psT%d" % i, [P, 8, P], BF16) for i in range(2)]
        t_psT = s.tiles("pre_tps", 2)
        xTo = [pg.sb(st, "pre_xTo%d" % i, [P, 8, TT], BF16) for i in range(2)]
        t_xTo = s.tiles("pre_txTo", 2)
        xT_v = xT_dram.rearrange("(c p) t -> p c t", p=P)
        nblk = S // P
        t_xT = pg.dram["xT"]

        def load(b):
            s.dma("sp", lambda e: e.dma_start(out=xin[b % 3][:], in_=x_in[b * P:(b + 1) * P, :]),
                  w=[t_xin[b % 3]], key=t_xin[b % 3])
        load(0)
        load(1)
        for b in range(nblk):
            if b + 2 < nblk:
                load(b + 2)
            tt, sub = divmod(b, 4)
            cur_xbf, cur_t = xbf[b % 2], t_xbf[b % 2]
            s.op("act", lambda e, b=b, o=cur_xbf: e.copy(o[:], xin[b % 3][:]), r=[t_xin[b % 3]], w=[cur_t])
            pt, tpt = psT[b % 2], t_psT[b % 2]
            for j in range(8):
                s.op("pe", lambda e, j=j, pt=pt, o=cur_xbf: e.transpose(pt[:, j, :], o[:, j * P:(j + 1) * P],
                                                                       pg.ident[:]),
                     r=[cur_t, pg.t_const], w=[tpt])
            xo, txo = xTo[tt % 2], t_xTo[tt % 2]
            s.op("dve", lambda e, pt=pt, xo=xo, sub=sub: e.tensor_copy(xo[:, :, sub * P:(sub + 1) * P], pt[:]),
                 r=[tpt], w=[txo])
            if sub == 3:
                s.dma("sp", lambda e, xo=xo, tt=tt: e.dma_start(out=xT_v[:, :, tt * TT:(tt + 1) * TT], in_=xo[:]),
                      r=[txo], w=[t_xT[tt]], key=txo)
        s.barrier()
        s.emit()
    s.release_phase()


def phase_ffn(pg, L, x_old_dram, xT_dram, w1, w2, g_ap, b_ap, x_new_dram, xT_new_dram):
    nc, s = pg.nc, pg.s
    tag = "f%d" % L
    with ExitStack() as st:
        w1_sb = pg.sb(st, "w1" + tag, [P, 8, DFF], BF16)
        w2_sb = pg.sb(st, "w2" + tag, [P, 32, D], BF16)
        t_w1 = s.tiles("t_w1", 8)
        t_w2 = s.tiles("t_w2", 8)
        w1_v = w1.rearrange("(c p) f -> p c f", p=P)
        w2_v = w2.rearrange("(c p) d -> p c d", p=P)
        for c in range(8):
            s.dma("pool", lambda e, c=c: e.dma_start(out=w1_sb[:, :, c * 512:(c + 1) * 512],
                                                     in_=w1_v[:, :, c * 512:(c + 1) * 512]),
                  w=[t_w1[c]], key=t_w1[c])
        for c in range(8):
            s.dma("pool", lambda e, c=c: e.dma_start(out=w2_sb[:, 4 * c:4 * c + 4, :], in_=w2_v[:, 4 * c:4 * c + 4, :]),
                  w=[t_w2[c]], key=t_w2[c])
        gb, t_gb = load_gb(pg, st, g_ap, b_ap, tag)
        xT = [pg.sb(st, "xT%s%d" % (tag, i), [P, 8, TT], BF16) for i in range(2)]
        t_xTb = s.tiles("t_xTb", 2)
        hT = pg.sb(st, "hT" + tag, [P, 32, TT], BF16)
        t_hT = s.tiles("t_hT", 32)
        rl = [pg.sb(st, "rl%s%d" % (tag, i), [P, TT], F32) for i in range(2)]
        t_rl = s.tiles("t_rl", 2)
        xw = [pg.sb(st, "xw%s%d" % (tag, i), [P, D], F32) for i in range(3)]
        t_xw = s.tiles("t_xw", 3)
        ps_h = [pg.ps(st, "psh%s%d" % (tag, i), [P, TT], F32) for i in range(3)]
        t_psh = s.tiles("t_psh", 3)
        ps_y = [pg.ps(st, "psy%s%d" % (tag, i), [P, D], F32) for i in range(2)]
        t_psy = s.tiles("t_psy", 2)
        lnb = alloc_ln_bufs(pg, st, tag)
        xT_v = xT_dram.rearrange("(c p) t -> p c t", p=P)
        xTn_v = xT_new_dram.rearrange("(c p) t -> p c t", p=P)
        t_xT = pg.dram["xT"]
        t_xo = pg.dram["xo"]
        t_xi = pg.dram["xi"]

        def load_xT(t):
            s.dma("sp", lambda e: e.dma_start(out=xT[t % 2][:], in_=xT_v[:, :, t * TT:(t + 1) * TT]),
                  r=[t_xT[t]], w=[t_xTb[t % 2]], key=t_xTb[t % 2])

        def load_xold(b):
            s.dma("sp", lambda e: e.dma_start(out=xw[b % 3][:], in_=x_old_dram[b * P:(b + 1) * P, :]),
                  r=[t_xi[b // 4]], w=[t_xw[b % 3]], key=t_xw[b % 3])

        load_xT(0)
        if NT > 1:
            load_xT(1)
        hcnt = 0
        pend = []

        def flush():
            while pend:
                pend.pop(0)()

        for t in range(NT):
            xTt, txT = xT[t % 2], t_xTb[t % 2]
            for c in range(32):
                ph, tph = ps_h[hcnt % 3], t_psh[hcnt % 3]
                r_, tr_ = rl[hcnt % 2], t_rl[hcnt % 2]
                hcnt += 1
                for kc in range(8):
                    s.op("pe", lambda e, c=c, kc=kc, ph=ph, xTt=xTt: e.matmul(
                        ph[:], w1_sb[:, kc, c * P:(c + 1) * P], xTt[:, kc, :], start=(kc == 0), stop=(kc == 7)),
                        r=[t_w1[c // 4], txT], w=[tph])
                s.op("act", lambda e, ph=ph, r_=r_: e.activation(r_[:], ph[:], AF.Relu), r=[tph], w=[tr_])
                s.op("dve", lambda e, c=c, r_=r_: e.tensor_tensor(hT[:, c, :], r_[:], r_[:], ALU.mult),
                     r=[tr_], w=[t_hT[c]])
                if c == 5:
                    flush()
            for sub in range(4):
                b = t * 4 + sub
                load_xold(b)
                py, tpy = ps_y[b % 2], t_psy[b % 2]
                for c in range(32):
                    for half in range(2):
                        s.op("pe", lambda e, c=c, half=half, py=py, sub=sub: e.matmul(
                            py[:, half * 512:(half + 1) * 512], hT[:, c, sub * P:(sub + 1) * P],
                            w2_sb[:, c, half * 512:(half + 1) * 512], start=(c == 0), stop=(c == 31)),
                            r=[t_hT[c], t_w2[c // 4]], w=[tpy])
                xw_, txw_ = xw[b % 3], t_xw[b % 3]
                s.op("dve", lambda e, xw_=xw_, py=py: e.scalar_tensor_tensor(
                    xw_[:], xw_[:], float(ALPHA), py[:], ALU.mult, ALU.add), r=[txw_, tpy], w=[txw_])
                back = ln_tail(pg, xw_, txw_, gb, t_gb, x_new_dram[b * P:(b + 1) * P, :], t_xo[t],
                               xTt, txT, sub * P, lnb, b)
                flush()
                pend.append(back)
                if sub == 3:
                    def fin(t=t, xTt=xTt, txT=txT):
                        s.dma("sp", lambda e: e.dma_start(out=xTn_v[:, :, t * TT:(t + 1) * TT], in_=xTt[:]),
                              r=[txT], w=[t_xT[t]], key=txT)
                        if t + 2 < NT:
                            load_xT(t + 2)
                    pend.append(fin)
        flush()
        s.barrier()
        s.emit()
    s.release_phase()


NEG = -30000.0
DA_H = 8


def slopes(n):
    return [2.0 ** (-8.0 * (h + 1) / n) for h in range(n)]


def dram_tile(pg, name):
    if name not in pg.dram:
        pg.dram[name] = Tile(name)
    return pg.dram[name]


def phase_proj(pg, tag, xT_dram, w_ap, ncols, jobs):
    s = pg.s
    with ExitStack() as st:
        xT = pg.sb(st, "pjx" + tag, [P, 8, S], BF16)
        xT_v = xT_dram.rearrange("(c p) t -> p c t", p=P)
        NXT = S // 512
        t_x = s.tiles("pj_tx", NXT)
        for c in range(NXT):
            s.dma("sp", lambda e, c=c: e.dma_start(out=xT[:, :, c * 512:(c + 1) * 512],
                                                   in_=xT_v[:, :, c * 512:(c + 1) * 512]),
                  r=pg.dram["xT"], w=[t_x[c]], key=t_x[c])
        w_sb = pg.sb(st, "pjw" + tag, [P, 8, ncols], BF16)
        NWB = (ncols + 511) // 512
        t_w = s.tiles("pj_tw", NWB)
        w_v = w_ap.rearrange("(c p) n -> p c n", p=P)
        worder = []
        for job in jobs:
            for wb in range(job["c0"] // 512, (job["c0"] + job["n"] + 511) // 512):
                if wb not in worder:
                    worder.append(wb)
        for wb in worder:
            hi = min(ncols, (wb + 1) * 512)
            s.dma("pool", lambda e, wb=wb, hi=hi: e.dma_start(out=w_sb[:, :, wb * 512:hi], in_=w_v[:, :, wb * 512:hi]),
                  w=[t_w[wb]], key=t_w[wb])
        ps = [pg.ps(st, "pjps%s%d" % (tag, i), [P, 512], F32) for i in range(4)]
        t_ps = s.tiles("pj_tps", 4)
        stg = {}
        cnt = {"ps": 0, "ev": 0}

        def staging(kind, dt, n):
            key = (kind, dt, n)
            if key not in stg:
                nm = "pjs%s%d" % (tag, len(stg))
                bufs = [pg.sb(st, nm + "_%d" % i, [P, n], dt) for i in range(2)]
                stg[key] = [bufs, s.tiles(nm, 2), 0]
            ent = stg[key]
            i = ent[2] % 2
            ent[2] += 1
            return ent[0][i], ent[1][i]

        def evac(dst_ap, src_ap, scale, rt, wt):
            i = cnt["ev"]
            cnt["ev"] += 1
            if i % 2 == 0:
                if scale == 1.0:
                    s.op("act", lambda e: e.copy(dst_ap, src_ap), r=[rt], w=[wt])
                else:
                    s.op("act", lambda e: e.mul(dst_ap, src_ap, float(scale)), r=[rt], w=[wt])
            else:
                if scale == 1.0:
                    s.op("dve", lambda e: e.tensor_copy(dst_ap, src_ap), r=[rt], w=[wt])
                else:
                    s.op("dve", lambda e: e.tensor_scalar(dst_ap, src_ap, float(scale), None, ALU.mult),
                         r=[rt], w=[wt])

        for job in jobs:
            c0, n, dst, dt = job["c0"], job["n"], job["dst"], job["dt"]
            scale = job.get("scale", 1.0)
            t_dst = dram_tile(pg, job["name"])
            if job["kind"] == "fm":
                dil = job.get("dil", 1)
                for cb in range((n + P - 1) // P):
                    m = min(P, n - cb * P)
                    stage, t_stage = staging("fm", dt, S)
                    for t in range(S // 512):
                        bank, tb = ps[cnt["ps"] % 4], t_ps[cnt["ps"] % 4]
                        cnt["ps"] += 1
                        for kc in range(8):
                            lhs = w_sb[:, kc, c0 + cb * P:c0 + cb * P + m]
                            rhs = xT[:, kc, t * 512:(t + 1) * 512]
                            s.op("pe", lambda e, kc=kc, bank=bank, m=m, lhs=lhs, rhs=rhs: e.matmul(
                                bank[:m, :], lhs, rhs, start=(kc == 0), stop=(kc == 7)),
                                r=[t_w[(c0 + cb * P) // 512], t_x[t]], w=[tb])
                        if dil == 1:
                            evac(stage[:m, t * 512:(t + 1) * 512], bank[:m, :], scale, tb, t_stage)
                        else:
                            wd = 512 // dil
                            src_ap = bank[:m, :].rearrange("p (l r) -> p r l", r=dil)
                            dst_ap = stage[:m, :].rearrange("p (r l) -> p r l", r=dil)[:, :, t * wd:(t + 1) * wd]
                            evac(dst_ap, src_ap, scale, tb, t_stage)
                    s.dma("sp", lambda e, o_=dst[cb * P:cb * P + m, :], i_=stage[:m, :]: e.dma_start(out=o_, in_=i_),
                          r=[t_stage], w=[t_dst], key=t_stage)
            else:
                blocks = job.get("blocks") or [(b * P, 1) for b in range(S // P)]
                for bi, (start, step) in enumerate(blocks):
                    stage, t_stage = staging("tm", dt, n)
                    xdeps = [t_x[j] for j in range(start // 512, (start + (P - 1) * step) // 512 + 1)]
                    for cg in range((n + 511) // 512):
                        wcol = min(512, n - cg * 512)
                        bank, tb = ps[cnt["ps"] % 4], t_ps[cnt["ps"] % 4]
                        cnt["ps"] += 1
                        for kc in range(8):
                            if step == 1:
                                lhs = xT[:, kc, start:start + P]
                            else:
                                lhs = xT[:, kc, start:start + (P - 1) * step + 1:step]
                            rhs = w_sb[:, kc, c0 + cg * 512:c0 + cg * 512 + wcol]
                            s.op("pe", lambda e, kc=kc, bank=bank, lhs=lhs, rhs=rhs, wcol=wcol: e.matmul(
                                bank[:, :wcol], lhs, rhs, start=(kc == 0), stop=(kc == 7)),
                                r=[t_w[(c0 + cg * 512) // 512]] + xdeps, w=[tb])
                        evac(stage[:, cg * 512:cg * 512 + wcol], bank[:, :wcol], scale, tb, t_stage)
                    s.dma("sp", lambda e, o_=dst[bi * P:(bi + 1) * P, :], i_=stage[:]: e.dma_start(out=o_, in_=i_),
                          r=[t_stage], w=[t_dst], key=t_stage)
        s.barrier()
        s.emit()
    s.release_phase()


def phase_outproj(pg, L, o_dram, w_out, x_old_dram, g_ap, b_ap, x_new_dram, xT_new_dram, nd_drams=None):
    s = pg.s
    tag = "o%d" % L
    NBX = 8
    with ExitStack() as st:
        w_sb = pg.sb(st, "wo" + tag, [P, 8, D], BF16)
        t_w = s.tile("t_wo")
        w_v = w_out.rearrange("(c p) n -> p c n", p=P)
        for c in range(2):
            s.dma("pool", lambda e, c=c: e.dma_start(out=w_sb[:, 4 * c:4 * c + 4, :], in_=w_v[:, 4 * c:4 * c + 4, :]),
                  w=[t_w], key=t_w)
        (g_bc, b_bc), t_gb = load_gb(pg, st, g_ap, b_ap, tag)
        NOB = 4
        ob = [pg.sb(st, "ob%s%d" % (tag, i), [P, D], BF16) for i in range(NOB)]
        t_ob = s.tiles("t_ob", NOB)
        if nd_drams is not None:
            ndt = [[pg.sb(st, "ndt%s%d%d" % (tag, i, g), [P, 8, 129], F32) for g in range(3)] for i in range(NOB)]
            t_ndt = [[s.tile("t_ndt") for g in range(3)] for i in range(NOB)]
            rec = [pg.sb(st, "rec%s%d" % (tag, i), [P, 8], F32) for i in range(NOB)]
            t_nd = dram_tile(pg, "nd")
        oT = [pg.sb(st, "oT%s%d" % (tag, i), [P, 8, P], BF16) for i in range(2)]
        t_oT = s.tiles("t_oT", 2)
        psT = [pg.ps(st, "psTo%s%d" % (tag, i), [P, 8, P], BF16) for i in range(2)]
        t_psT = s.tiles("t_psTo", 2)
        ps_y = [pg.ps(st, "psy%s%d" % (tag, i), [P, D], F32) for i in range(2)]
        t_psy = s.tiles("t_psy", 2)
        xw = [pg.sb(st, "xw%s%d" % (tag, i), [P, D], F32) for i in range(NBX)]
        t_xw = s.tiles("t_xw", NBX)
        st6 = [pg.sb(st, "st6%s%d" % (tag, i), [P, 12], F32) for i in range(NBX)]
        mv = [pg.sb(st, "mv%s%d" % (tag, i), [P, 8], F32) for i in range(NBX)]
        t_st = s.tiles("t_st", NBX)
        xbf = [pg.sb(st, "xbf%s%d" % (tag, i), [P, D], BF16) for i in range(3)]
        t_xbf = s.tiles("t_xbf", 3)
        psX = pg.ps(st, "psX" + tag, [P, 8, P], BF16)
        t_psX = s.tile("t_psX")
        xTo = [pg.sb(st, "xTo%s%d" % (tag, i), [P, 8, TT], BF16) for i in range(3)]
        t_xTo = s.tiles("t_xTo", 3)
        xTn_v = xT_new_dram.rearrange("(c p) t -> p c t", p=P)
        t_o = dram_tile(pg, "o_tm")
        t_xT = pg.dram["xT"]
        nblk = S // P

        def load(b):
            i = b % NOB
            if nd_drams is None:
                DMA(s, "sp", ob[i][:], o_dram[b * P:(b + 1) * P, :], r=[t_o], w=[t_ob[i]], key=t_ob[i])
            else:
                for g in range(3):
                    DMA(s, "sp", ndt[i][g][:], nd_drams[g][b * P:(b + 1) * P, :].rearrange("p (h c) -> p h c", h=8),
                        r=[t_nd], w=[t_ndt[i][g]], key=t_ndt[i][g])
            DMA(s, "sp", xw[b % NBX][:], x_old_dram[b * P:(b + 1) * P, :], r=[pg.dram["xi"][b // 4]],
                w=[t_xw[b % NBX]], key=t_xw[b % NBX])

        def stage_pre(b):
            if nd_drams is None:
                return
            i = b % NOB
            n0, n1, n2 = ndt[i]
            t0, t1, t2 = t_ndt[i]
            OP(s, "pool", "tensor_tensor", n0[:], n0[:], n1[:], ALU.add, r=[t0, t1], w=[t0])
            OP(s, "pool", "tensor_tensor", n0[:], n0[:], n2[:], ALU.add, r=[t0, t2], w=[t0])
            OP(s, "dve", "reciprocal", rec[i][:], n0[:, :, 128], r=[t0], w=[t0])
            OP(s, "dve", "tensor_tensor", ob[i][:].rearrange("p (h c) -> p h c", h=8), n0[:, :, 0:128],
               rec[i][:].unsqueeze(2).to_broadcast([P, 8, P]), ALU.mult, r=[t0], w=[t_ob[i]])

        def stage_a(b):
            pt, tpt = psT[b % 2], t_psT[b % 2]
            o_, to_ = ob[b % NOB], t_ob[b % NOB]
            for j in range(8):
                OP(s, "pe", "transpose", pt[:, j, :], o_[:, j * P:(j + 1) * P], pg.ident[:], r=[to_, pg.t_const],
                   w=[tpt])
            oT_, toT_ = oT[b % 2], t_oT[b % 2]
            OP(s, "act", "copy", oT_[:], pt[:], r=[tpt], w=[toT_])
            py, tpy = ps_y[b % 2], t_psy[b % 2]
            for kc in range(8):
                for half in range(2):
                    OP(s, "pe", "matmul", py[:, half * 512:(half + 1) * 512], oT_[:, kc, :],
                       w_sb[:, kc, half * 512:(half + 1) * 512], start=(kc == 0), stop=(kc == 7),
                       r=[toT_, t_w], w=[tpy])

        def stages(b):
            t, sub = divmod(b, 4)
            py, tpy = ps_y[b % 2], t_psy[b % 2]
            x_, tx_ = xw[b % NBX], t_xw[b % NBX]
            s6_, mv_, ts_ = st6[b % NBX], mv[b % NBX], t_st[b % NBX]
            xb_, txb_ = xbf[b % 3], t_xbf[b % 3]
            xo_, txo_ = xTo[t % 3], t_xTo[t % 3]

            def s1():
                OP(s, "dve", "scalar_tensor_tensor", x_[:], x_[:], float(ALPHA), py[:], ALU.mult, ALU.add,
                   r=[tx_, tpy], w=[tx_])
                OP(s, "dve", "bn_stats", s6_[:, 0:6], x_[:, 0:512], r=[tx_], w=[ts_])
                OP(s, "dve", "bn_stats", s6_[:, 6:12], x_[:, 512:1024], r=[tx_], w=[ts_])
                OP(s, "dve", "bn_aggr", mv_[:, 0:2], s6_[:, 0:12], r=[ts_], w=[ts_])

            def s2():
                OP(s, "act", "activation", mv_[:, 2:3], mv_[:, 1:2], AF.Sqrt, bias=pg.eps_ln[:, 0:1], scale=1.0,
                   r=[ts_, pg.t_const], w=[ts_])
                OP(s, "dve", "reciprocal", mv_[:, 3:4], mv_[:, 2:3], r=[ts_], w=[ts_])
                OP(s, "dve", "scalar_tensor_tensor", mv_[:, 4:5], mv_[:, 0:1], -1.0, mv_[:, 3:4], ALU.mult, ALU.mult,
                   r=[ts_], w=[ts_])

            def s3():
                OP(s, "act", "activation", x_[:], x_[:], AF.Identity, bias=mv_[:, 4:5], scale=mv_[:, 3:4],
                   r=[tx_, ts_], w=[tx_])

            def s4():
                OP(s, "pool", "tensor_tensor", x_[:], x_[:], g_bc[:], ALU.mult, r=[tx_, t_gb], w=[tx_])
                OP(s, "dve", "tensor_tensor", x_[:], x_[:], b_bc[:], ALU.add, r=[tx_, t_gb], w=[tx_])

            def s5():
                DMA(s, "sp", x_new_dram[b * P:(b + 1) * P, :], x_[:], r=[tx_], w=[pg.dram["xo"][t]], key=tx_)
                OP(s, "act", "copy", xb_[:], x_[:], r=[tx_], w=[txb_])

            def s6():
                for j in range(8):
                    OP(s, "pe", "transpose", psX[:, j, :], xb_[:, j * P:(j + 1) * P], pg.ident[:],
                       r=[txb_, pg.t_const], w=[t_psX])
                OP(s, "act", "copy", xo_[:, :, sub * P:(sub + 1) * P], psX[:], r=[t_psX], w=[txo_])
                if sub == 3:
                    DMA(s, "sp", xTn_v[:, :, t * TT:(t + 1) * TT], xo_[:], r=[txo_], w=[t_xT[t]], key=txo_)
            return [s1, s2, s3, s4, s5, s6]

        for b in range(min(3, nblk)):
            load(b)
        for b in range(min(2, nblk)):
            stage_pre(b)
        stage_a(0)
        NST = 6
        all_st = {}
        for i in range(nblk + NST - 1):
            if i + 3 < nblk:
                load(i + 3)
            if i + 2 < nblk:
                stage_pre(i + 2)
            if i + 1 < nblk:
                stage_a(i + 1)
            if i < nblk:
                all_st[i] = stages(i)
            for k in range(NST):
                b = i - k
                if 0 <= b < nblk:
                    all_st[b][k]()
        s.barrier()
        s.emit()
    s.release_phase()


def build_aux(pg, specs):
    s = pg.s
    with ExitStack() as st:
        ia = pg.sb(st, "aux_ia", [1, S], I32)
        bB = pg.sb(st, "aux_bB", [1, S], BF16)
        bO = pg.sb(st, "aux_bO", [1, S], BF16)
        t_b = s.tile("t_auxb")
        OP(s, "pool", "iota", ia[:], [[0, S // P], [1, P]], base=0, channel_multiplier=0, w=[t_b])
        OP(s, "pool", "tensor_copy", bB[:], ia[:], r=[t_b], w=[t_b])
        OP(s, "pool", "memset", bO[:], 1.0, w=[t_b])
        ps = [pg.ps(st, "aux_ps%d" % i, [64, 512], F32) for i in range(2)]
        t_ps = s.tiles("t_auxps", 2)
        t_aq = dram_tile(pg, "auxda")
        cnt = 0
        for si, (aux_dram, mults, Lseq) in enumerate(specs):
            ib = pg.sb(st, "aux_ib%d" % si, [1, S], I32)
            bA = pg.sb(st, "aux_bA%d" % si, [1, S], BF16)
            mm = pg.sb(st, "aux_m%d" % si, [1, 3, 64], BF16)
            outb = pg.sb(st, "aux_o%d" % si, [64, S], BF16)
            t_s = s.tile("t_auxs%d" % si)
            t_o = s.tile("t_auxo%d" % si)
            nrep = S // Lseq
            OP(s, "pool", "iota", ib[:], [[0, nrep], [128, Lseq // P], [0, P]], base=0, channel_multiplier=0, w=[t_s])
            OP(s, "pool", "tensor_copy", bA[:], ib[:], r=[t_s], w=[t_s])
            OP(s, "pool", "memset", mm[:], 0.0, w=[t_s])
            mo = mm[:, 2, :].rearrange("p (h c) -> p h c", c=8)
            OP(s, "pool", "memset", mo[:, :, 2:6], 1.0, w=[t_s])
            for h in range(8):
                m = float(mults[h])
                OP(s, "pool", "memset", mm[:, 0, h * 8 + 0:h * 8 + 1], -m, w=[t_s])
                OP(s, "pool", "memset", mm[:, 0, h * 8 + 6:h * 8 + 7], m, w=[t_s])
                OP(s, "pool", "memset", mm[:, 1, h * 8 + 1:h * 8 + 2], -m, w=[t_s])
                OP(s, "pool", "memset", mm[:, 1, h * 8 + 7:h * 8 + 8], m, w=[t_s])
            for j in range(S // 512):
                p_, tp_ = ps[cnt % 2], t_ps[cnt % 2]
                cnt += 1
                cs = slice(j * 512, (j + 1) * 512)
                OP(s, "pe", "matmul", p_[:], mm[:, 0, :], bA[:, cs], start=True, stop=False, r=[t_s], w=[tp_])
                OP(s, "pe", "matmul", p_[:], mm[:, 1, :], bB[:, cs], start=False, stop=False, r=[t_s, t_b], w=[tp_])
                OP(s, "pe", "matmul", p_[:], mm[:, 2, :], bO[:, cs], start=False, stop=True, r=[t_s, t_b], w=[tp_])
                OP(s, "dve" if j % 2 else "act", "tensor_copy" if j % 2 else "copy", outb[:, cs], p_[:],
                   r=[tp_], w=[t_o])
            DMA(s, "sp", aux_dram.rearrange("h q r s -> (h q r) s"), outb[:], r=[t_o], w=[t_aq], key=t_o)
        s.barrier()
        s.emit()
    s.release_phase()


def phase_diffattn(pg, L, qT_dram, kT_dram, v_dram, aux, lam_aps, subln_g, lambda_init, o_dram):
    s = pg.s
    tag = "a%d" % L
    H = DA_H
    with ExitStack() as st:
        lamt = pg.sb(st, "lamt" + tag, [1, 4, 64], F32)
        lams = pg.sb(st, "lams" + tag, [1, 8], F32)
        junk = pg.sb(st, "lamj" + tag, [1, 64], F32)
        neglam = pg.sb(st, "neglam" + tag, [P, 1], F32)
        t_lam = s.tile("t_lam")
        t_nl = s.tile("t_nl")
        for i, ap in enumerate(lam_aps):
            s.dma("sp", lambda e, i=i, ap=ap: e.dma_start(out=lamt[:, i, :], in_=ap.rearrange("(o n) -> o n", o=1)),
                  w=[t_lam], key=t_lam)
        s.op("dve", lambda e: e.scalar_tensor_tensor(junk[:], lamt[:, 0, :], 1.0, lamt[:, 1, :], ALU.mult, ALU.mult,
                                                     accum_out=lams[:, 0:1]), r=[t_lam], w=[t_nl])
        s.op("dve", lambda e: e.scalar_tensor_tensor(junk[:], lamt[:, 2, :], 1.0, lamt[:, 3, :], ALU.mult, ALU.mult,
                                                     accum_out=lams[:, 1:2]), r=[t_lam], w=[t_nl])
        s.op("act", lambda e: e.activation(lams[:, 2:4], lams[:, 0:2], AF.Exp), r=[t_nl], w=[t_nl])
        s.op("dve", lambda e: e.scalar_tensor_tensor(lams[:, 4:5], lams[:, 3:4], float(-lambda_init), lams[:, 2:3],
                                                     ALU.add, ALU.subtract), r=[t_nl], w=[t_nl])
        t_lamd = dram_tile(pg, "lam_d")
        lam_d = pg.lam_dram
        s.dma("sp", lambda e: e.dma_start(out=lam_d[L:L + 1, :], in_=lams[:, 4:5]), r=[t_nl], w=[t_lamd], key=t_nl)
        s.dma("sp", lambda e: e.dma_start(out=neglam[:], in_=lam_d[L, :].partition_broadcast(P)),
              r=[t_lamd], w=[t_nl], key=t_nl)
        g_bc = pg.sb(st, "sg" + tag, [P, P], F32)
        t_g = s.tile("t_sg")
        s.dma("sp", lambda e: e.dma_start(out=g_bc[:], in_=subln_g.partition_broadcast(P)), w=[t_g], key=t_g)
        s.op("pool", lambda e: e.tensor_scalar(g_bc[:], g_bc[:], float(1.0 - lambda_init), None, ALU.mult),
             r=[t_g], w=[t_g])
        qa = [[pg.sb(st, "qa%s%d%d" % (tag, i, m), [P, S], BF16) for m in range(2)] for i in range(2)]
        ka = [[pg.sb(st, "ka%s%d%d" % (tag, i, m), [P, S], BF16) for m in range(2)] for i in range(2)]
        vaug = [pg.sb(st, "va%s%d" % (tag, i), [P, S // P, 129], BF16) for i in range(2)]
        t_qkv = s.tiles("t_qkv", 2)
        for i in range(2):
            for m in range(2):
                s.op("pool", lambda e, i=i, m=m: e.memset(qa[i][m][64:128, :], 0.0), w=[t_qkv[i]])
                s.op("pool", lambda e, i=i, m=m: e.memset(ka[i][m][64:128, :], 0.0), w=[t_qkv[i]])
            s.op("pool", lambda e, i=i: e.memset(vaug[i][:, :, 128:129], 1.0), w=[t_qkv[i]])
        t_q = dram_tile(pg, "qT")
        t_aux = dram_tile(pg, "auxda")
        v_v = v_dram.rearrange("(n p) d -> p n d", p=P)

        def load_head(h):
            i = h % 2
            for m in range(2):
                r0 = h * 128 + m * 64
                s.dma("sp", lambda e, m=m, r0=r0: e.dma_start(out=qa[i][m][0:64, :], in_=qT_dram[r0:r0 + 64, :]),
                      r=[t_q], w=[t_qkv[i]], key=t_qkv[i])
                s.dma("sp", lambda e, m=m, r0=r0: e.dma_start(out=ka[i][m][0:64, :], in_=kT_dram[r0:r0 + 64, :]),
                      r=[t_q], w=[t_qkv[i]], key=t_qkv[i])
                s.dma("sp", lambda e, m=m: e.dma_start(out=qa[i][m][64:68, :], in_=aux[h, 0, :, :]),
                      r=[t_aux], w=[t_qkv[i]], key=t_qkv[i])
                s.dma("sp", lambda e, m=m: e.dma_start(out=ka[i][m][64:68, :], in_=aux[h, 1, :, :]),
                      r=[t_aux], w=[t_qkv[i]], key=t_qkv[i])
            s.dma("sp", lambda e: e.dma_start(out=vaug[i][:, :, 0:128], in_=v_v[:, :, h * 128:(h + 1) * 128]),
                  r=[t_q], w=[t_qkv[i]], key=t_qkv[i])

        oacc = [pg.sb(st, "oacc%s%d" % (tag, i), [P, 2, 4, 129], F32) for i in range(2)]
        t_oacc = s.tiles("t_oacc", 2)
        sm = [pg.sb(st, "sm%s%d" % (tag, i), [P, 32], F32) for i in range(2)]
        t_sm = s.tiles("t_sm", 2)
        otmp = [pg.sb(st, "otmp%s%d" % (tag, i), [P, 2, 4, P], F32) for i in range(2)]
        t_otmp = s.tiles("t_otmp", 2)
        ost = [pg.sb(st, "ost%s%d" % (tag, i), [P, 4, P], BF16) for i in range(2)]
        t_ost = s.tiles("t_ost", 2)
        t_o = dram_tile(pg, "o_tm")
        o_v = o_dram.rearrange("(n p) d -> p n d", p=P)
        NTL = S // 512
        pending = []
        AX = mybir.AxisListType

        def post(h, t, ob):
            oa, sm_, ot = oacc[ob], sm[ob], otmp[ob]
            toa, tsm, tot = t_oacc[ob], t_sm[ob], t_otmp[ob]
            bc = lambda ap: ap.unsqueeze(2).to_broadcast([P, 4, P])
            OP(s, "dve", "reciprocal", sm_[:, 0:8].rearrange("p (a b) -> p a b", a=2), oa[:, :, :, 128],
               r=[toa], w=[tsm])
            OP(s, "dve", "tensor_scalar", sm_[:, 8:12], sm_[:, 4:8], neglam[:, 0:1], None, ALU.mult,
               r=[tsm, t_nl], w=[tsm])
            OP(s, "dve", "tensor_tensor", ot[:, 0], oa[:, 0, :, 0:128], bc(sm_[:, 0:4]), ALU.mult,
               r=[toa, tsm], w=[tot])
            OP(s, "dve", "tensor_tensor", ot[:, 1], oa[:, 1, :, 0:128], bc(sm_[:, 8:12]), ALU.mult,
               r=[toa, tsm], w=[tot])
            OP(s, "dve", "tensor_tensor", ot[:, 0], ot[:, 0], ot[:, 1], ALU.add, r=[tot], w=[tot])
            OP(s, "dve", "tensor_tensor", ot[:, 1], ot[:, 0], ot[:, 0], ALU.mult, r=[tot], w=[tot])
            OP(s, "dve", "tensor_reduce", sm_[:, 12:16], ot[:, 1], AX.X, ALU.add, r=[tot], w=[tsm])
            OP(s, "dve", "tensor_scalar", sm_[:, 16:20], sm_[:, 12:16], 1.0 / 128.0, float(RMS_EPS), ALU.mult,
               ALU.add, r=[tsm], w=[tsm])
            OP(s, "pool", "tensor_tensor", sm_[:, 20:24], sm_[:, 16:20], pg.neg_half[:, 0:4], ALU.pow,
               r=[tsm, pg.t_const], w=[tsm])
            OP(s, "dve", "tensor_tensor", ot[:, 0], ot[:, 0], bc(sm_[:, 20:24]), ALU.mult, r=[tot, tsm], w=[tot])
            OP(s, "dve", "tensor_tensor", ost[ob][:], ot[:, 0], g_bc[:].unsqueeze(1).to_broadcast([P, 4, P]),
               ALU.mult, r=[tot, t_g], w=[t_ost[ob]])
            DMA(s, "sp", o_v[:, t * 4:(t + 1) * 4, h * 128:(h + 1) * 128], ost[ob][:], r=[t_ost[ob]], w=[t_o],
                key=t_ost[ob])

        def make_end(h, t, m):
            ob = (h * NTL + t) % 2

            def end(acc, t_acc):
                for qb in range(4):
                    OP(s, "dve", "tensor_copy", oacc[ob][:, m, qb, :], acc[qb][:, 0:129], r=[t_acc[qb]],
                       w=[t_oacc[ob]])
                if m == 0:
                    while pending:
                        pending.pop(0)()
                else:
                    pending.append(lambda: post(h, t, ob))
            return end

        units = []
        head_first_unit = {}
        sl_h = slopes(H)
        for h in range(H):
            i = h % 2
            head_first_unit[h] = len(units)
            wkeep = 0
            while sl_h[h] * (128 * (wkeep + 1) - 127) < 88.0:
                wkeep += 1
            for t in range(NTL):
                for m in range(2):
                    q_, k_, v_ = qa[i][m], ka[i][m], vaug[i]
                    reads = [t_qkv[i]]
                    kb_first = max(0, 4 * t - wkeep)
                    nd = list(range(kb_first, 4 * t))
                    groups = []
                    if len(nd) % 2 == 1:
                        groups.append(nd[:1])
                        nd = nd[1:]
                    for x0 in range(0, len(nd), 2):
                        groups.append(nd[x0:x0 + 2])
                    first_kb = kb_first if 4 * t > kb_first else 4 * t
                    for grp in groups:
                        u = dict(reads=reads, mm=[], exp=[(0, len(grp), 0, 512)], pv=[])
                        for bi, kb in enumerate(grp):
                            u["mm"].append((bi, 0, 512, [(k_[:, kb * P:(kb + 1) * P], q_[:, t * 512:(t + 1) * 512])]))
                            for qb in range(4):
                                u["pv"].append((qb, bi, qb * P, v_[:, kb, :], kb == first_kb, False))
                        units.append(u)
                    tri2 = pg.tri_le3[:, 0, :].unsqueeze(1).to_broadcast([P, 2, P])
                    for j0 in (0, 2):
                        w0 = 512 - P * j0
                        u = dict(reads=reads, mm=[], exp=[(0, 2, 0, w0)], pv=[])
                        for bi in range(2):
                            j = j0 + bi
                            kb = 4 * t + j
                            u["mm"].append((bi, 0, 512 - P * j, [(k_[:, kb * P:(kb + 1) * P],
                                                                  q_[:, t * 512 + P * j:(t + 1) * 512]),
                                                                 (pg.ident[:], pg.tri_bf[:, 0, :], 0, P)]))
                            for qb in range(j, 4):
                                u["pv"].append((qb, bi, (qb - j) * P, v_[:, kb, :], kb == first_kb, qb == j))
                        if j0 == 2:
                            u["end"] = make_end(h, t, m)
                        units.append(u)
        load_head(0)
        bounds = [head_first_unit[h] for h in range(H)] + [len(units)]
        for h in range(H):
            if h + 1 < H:
                load_head(h + 1)
            attn_core_run(pg, st, tag, units[bounds[h]:bounds[h + 1]], h == 0)
        while pending:
            pending.pop(0)()
        s.barrier()
        s.emit()
    s.release_phase()


DIL_GROUPS = ((128, 1), (512, 4), (2048, 16))


def phase_dilattn(pg, L, g, dil, qT_dram, kT_dram, v_dram, nd_dram):
    s = pg.s
    tag = "d%d%d" % (L, g)
    H = 8
    Lg = S // dil
    nb = Lg // P
    NB = S // P
    sl8 = slopes(8)
    with ExitStack() as st:
        qa = [pg.sb(st, "qa%s%d" % (tag, i), [P, S], BF16) for i in range(2)]
        ka = [pg.sb(st, "ka%s%d" % (tag, i), [P, S], BF16) for i in range(2)]
        vaug = [pg.sb(st, "va%s%d" % (tag, i), [P, NB, 129], BF16) for i in range(2)]
        t_qkv = s.tiles("t_qkv", 2)
        for i in range(2):
            s.op("pool", lambda e, i=i: e.memset(vaug[i][:, :, 128:129], 1.0), w=[t_qkv[i]])
        idist = pg.sb(st, "idist" + tag, [P, 256], I32)
        dist = pg.sb(st, "dist" + tag, [P, 256], F32)
        mbase = pg.sb(st, "mbase" + tag, [P, 2, P], F32)
        t_dm = s.tile("t_dm")
        OP(s, "pool", "iota", idist[:], [[1, 256]], base=0, channel_multiplier=-1, w=[t_dm])
        OP(s, "pool", "tensor_copy", dist[:], idist[:], r=[t_dm], w=[t_dm])
        OP(s, "pool", "tensor_copy", mbase[:, 0:1, :], pg.tri_le3[:], r=[pg.t_const], w=[t_dm])
        OP(s, "pool", "tensor_copy", mbase[:, 1:2, :], pg.tri_ge3[:], r=[pg.t_const], w=[t_dm])
        tbl = [pg.sb(st, "tbl%s%d" % (tag, i), [P, 2, 512], F32) for i in range(2)]
        t_tbl = s.tiles("t_tbl", 2)
        t_q = dram_tile(pg, "qT")
        v_v = v_dram.rearrange("(n p) d -> p n d", p=P)

        def load_head(h):
            i = h % 2
            DMA(s, "sp", qa[i][:], qT_dram[h * P:(h + 1) * P, :], r=[t_q], w=[t_qkv[i]], key=t_qkv[i])
            DMA(s, "sp", ka[i][:], kT_dram[h * P:(h + 1) * P, :], r=[t_q], w=[t_qkv[i]], key=t_qkv[i])
            DMA(s, "sp", vaug[i][:, :, 0:128], v_v[:, :, h * P:(h + 1) * P], r=[t_q], w=[t_qkv[i]], key=t_qkv[i])
            m = -float(sl8[h] * dil)
            OP(s, "dve", "scalar_tensor_tensor", tbl[i][:].rearrange("p a (b c) -> p (a b) c", c=256),
               dist[:].unsqueeze(1).to_broadcast([P, 4, 256]), m,
               mbase[:].rearrange("p a b -> p (a b)").unsqueeze(1).to_broadcast([P, 4, 256]), ALU.mult, ALU.add,
               r=[t_dm], w=[t_tbl[i]])

        ndh = [pg.sb(st, "ndh%s%d" % (tag, i), [P, NB, 129], F32) for i in range(2)]
        t_ndh = s.tiles("t_ndh", 2)
        t_nd = dram_tile(pg, "nd")
        nd_v = nd_dram.rearrange("(n p r) c -> p r n c", p=P, r=dil)

        def make_evac(h, beta):
            a = beta % 4

            def evac(acc, t_acc):
                k = h % 2
                OP(s, "dve", "tensor_copy", ndh[k][:, beta, :], acc[a][:, 0:129], r=[t_acc[a]], w=[t_ndh[k]])
                if beta == NB - 1:
                    src_ap = ndh[k][:].rearrange("p (r n) c -> p r n c", r=dil)
                    dst_ap = nd_v[:, :, :, h * 129:(h + 1) * 129]
                    DMA(s, "sp", dst_ap, src_ap, r=[t_ndh[k]], w=[t_nd], key=t_ndh[k])
            return evac

        load_head(0)
        for h in range(H):
            if h + 1 < H:
                load_head(h + 1)
            i = h % 2
            q_, k_, v_ = qa[i], ka[i], vaug[i]
            units = []
            for b0 in range(0, NB, 4):
                u = dict(reads=[t_qkv[i]], mm=[], mask=[(0, 2, 0, 512, tbl[i][:], t_tbl[i])],
                         exp=[(0, 2, 0, 512)], pv=[])
                for sl in range(4):
                    beta = b0 + sl
                    r_, n_ = divmod(beta, nb)
                    has_next = (n_ + 1 < nb)
                    width = 256 if has_next else 128
                    bank, off = sl // 2, (sl % 2) * 256
                    u["mm"].append((bank, off, width, [
                        (k_[:, beta * P:(beta + 1) * P], q_[:, beta * P:beta * P + width])]))
                    u["pv"].append((beta % 4, bank, off, v_[:, beta, :], n_ == 0, True, make_evac(h, beta)))
                    if has_next:
                        u["pv"].append(((beta + 1) % 4, bank, off + P, v_[:, beta, :], True, False, None))
                units.append(u)
            attn_core_run(pg, st, tag, units, h == 0)
        s.barrier()
        s.emit()
    s.release_phase()


def phase_dilcombine(pg, L, nd_drams, o_dram):
    s = pg.s
    tag = "c%d" % L
    with ExitStack() as st:
        nd = [[pg.sb(st, "nd%s%d%d" % (tag, i, g), [P, 8, 129], F32) for g in range(3)] for i in range(2)]
        t_ndb = [[s.tile("t_ndb") for g in range(3)] for i in range(2)]
        rec = [pg.sb(st, "rec%s%d" % (tag, i), [P, 8], F32) for i in range(2)]
        ob = [pg.sb(st, "ob%s%d" % (tag, i), [P, D], BF16) for i in range(2)]
        t_ob = s.tiles("t_ob", 2)
        t_nd = dram_tile(pg, "nd")
        t_o = dram_tile(pg, "o_tm")
        nblk = S // P

        def load(b):
            for g in range(3):
                s.dma("sp", lambda e, g=g: e.dma_start(
                    out=nd[b % 2][g][:], in_=nd_drams[g][b * P:(b + 1) * P, :].rearrange("p (h c) -> p h c", h=8)),
                    r=[t_nd], w=[t_ndb[b % 2][g]], key=t_ndb[b % 2][g])
        load(0)
        for b in range(nblk):
            if b + 1 < nblk:
                load(b + 1)
            n0, n1, n2 = nd[b % 2]
            t0, t1, t2 = t_ndb[b % 2]
            s.op("pool", lambda e, n0=n0, n1=n1: e.tensor_tensor(n0[:], n0[:], n1[:], ALU.add), r=[t0, t1], w=[t0])
            s.op("pool", lambda e, n0=n0, n2=n2: e.tensor_tensor(n0[:], n0[:], n2[:], ALU.add), r=[t0, t2], w=[t0])
            rc = rec[b % 2]
            s.op("dve", lambda e, n0=n0, rc=rc: e.reciprocal(rc[:], n0[:, :, 128]), r=[t0], w=[t0])
            for h in range(8):
                eng = "dve" if h % 2 == 0 else "act"
                if eng == "dve":
                    s.op("dve", lambda e, h=h, n0=n0, rc=rc, b=b: e.tensor_scalar(
                        ob[b % 2][:, h * P:(h + 1) * P], n0[:, h, 0:128], rc[:, h:h + 1], None, ALU.mult),
                        r=[t0], w=[t_ob[b % 2]])
                else:
                    s.op("act", lambda e, h=h, n0=n0, rc=rc, b=b: e.activation(
                        ob[b % 2][:, h * P:(h + 1) * P], n0[:, h, 0:128], AF.Copy, scale=rc[:, h:h + 1]),
                        r=[t0], w=[t_ob[b % 2]])
            s.dma("sp", lambda e, b=b: e.dma_start(out=o_dram[b * P:(b + 1) * P, :], in_=ob[b % 2][:]),
                  r=[t_ob[b % 2]], w=[t_o], key=t_ob[b % 2])
        s.barrier()
        s.emit()
    s.release_phase()


def OP(s, eng, name, *args, r=(), w=(), **kw):
    return s.op(eng, lambda e: getattr(e, name)(*args, **kw), r=list(r), w=list(w))


def DMA(s, q, out, in_, r=(), w=(), key=None):
    return s.dma(q, lambda e: e.dma_start(out=out, in_=in_), r=list(r), w=list(w), key=key)


def phase_gla(pg, L, gqT, gkT, gk_tm, v_dram, gr_tm, glT_dram, w_gate2, b_gate, gnorm_g, o_dram):
    s = pg.s
    tag = "g%d" % L
    HG, DK, DV = 4, 128, 256
    NCH = S // P
    I16 = 1.0 / 16.0
    with ExitStack() as st:
        sb = lambda name, shape, dt: pg.sb(st, name + tag, shape, dt)
        tri_incl = sb("tri_incl", [P, 4, P], F32)
        sgt = sb("sgt", [P, P], F32)
        ones_row = sb("ones_row", [1, P], F32)
        bg = sb("bg", [1, 512], F32)
        wg2 = sb("wg2", [16, 512], F32)
        glT = sb("glT", [16, S], F32)
        gn_bc = sb("gn_bc", [P, DV], F32)
        t_c = s.tile("t_glac")
        OP(s, "pool", "memset", tri_incl[:], 1.0, w=[t_c])
        OP(s, "pool", "affine_select", tri_incl[:], tri_incl[:], [[0, 4], [1, P]], ALU.is_ge, 0.0,
           base=0, channel_multiplier=-1, r=[t_c], w=[t_c])
        OP(s, "pool", "memset", sgt[:], 1.0, w=[t_c])
        OP(s, "pool", "affine_select", sgt[:], sgt[:], [[-1, P]], ALU.is_gt, 0.0,
           base=0, channel_multiplier=1, r=[t_c], w=[t_c])
        OP(s, "pool", "memset", ones_row[:], 1.0, w=[t_c])
        t_ld = s.tile("t_glald")
        DMA(s, "sp", bg[:], b_gate.rearrange("(o n) -> o n", o=1), w=[t_ld], key=t_ld)
        DMA(s, "sp", wg2[:], w_gate2, w=[t_ld], key=t_ld)
        DMA(s, "sp", glT[:], glT_dram, r=[dram_tile(pg, "qT")], w=[t_ld], key=t_ld)
        DMA(s, "sp", gn_bc[:], gnorm_g.partition_broadcast(P), w=[t_ld], key=t_ld)
        def dbl(name, shape, dt):
            return [sb("%s%d" % (name, i), shape, dt) for i in range(2)], s.tiles("t_" + name, 2)
        qT, t_qT = dbl("qT", [P, 4, P], F32)
        kT, t_kT = dbl("kT", [P, 4, P], F32)
        ktm, t_ktm = dbl("ktm", [P, 512], F32)
        vv, t_vv = dbl("vv", [P, D], BF16)
        rr, t_rr = dbl("rr", [P, D], F32)
        e1, t_e1 = dbl("e1", [P, 512], F32)
        eq, t_eq = dbl("eq", [P, 4, P], F32)
        ek, t_ek = dbl("ek", [P, 4, P], F32)
        es, t_es = dbl("es", [P, 512], F32)
        qd, t_qd = dbl("qd", [P, 4, P], BF16)
        ki, t_ki = dbl("ki", [P, 4, P], BF16)
        kst, t_kst = dbl("kst", [P, 512], BF16)
        sT, t_sT = dbl("sT", [P, 4, P], BF16)
        sg, t_sg = dbl("sg", [P, D], F32)
        ot, t_ot = dbl("ot", [P, D], F32)
        ost, t_ost = dbl("ost", [P, D], BF16)
        sm, t_sm = dbl("sm", [P, 16], F32)
        junk, t_junk = dbl("junk", [P, DV], F32)
        state = sb("state", [P, 4, DV], F32)
        state_bf = sb("state_bf", [P, 4, DV], BF16)
        t_state = s.tiles("t_state", 4)
        t_sbf = s.tiles("t_sbf", 4)
        ps_z = pg.ps(st, "psz" + tag, [P, 512], F32)
        ps_c = pg.ps(st, "psc" + tag, [P, 4, P], F32)
        ps_r = ps_z
        ps_s = ps_c
        ps_o2 = [pg.ps(st, "pso%s%d" % (tag, i), [P, 4, DV], F32) for i in range(2)]
        ps_kv = pg.ps(st, "pskv" + tag, [P, 4, DV], F32)
        t_psz, t_psc = s.tile("t_psz"), s.tile("t_psc")
        t_psr, t_pss = t_psz, t_psc
        t_pso2 = [s.tiles("t_pso", 2) for i in range(2)]
        t_pskv = [t for t in s.tiles("t_pskv", 2) for _ in range(2)]
        t_q = dram_tile(pg, "qT")
        t_o = dram_tile(pg, "o_tm")
        gq_v = gqT.rearrange("(h p) t -> p h t", p=P)
        gk_v = gkT.rearrange("(h p) t -> p h t", p=P)

        def load(c):
            i = c % 2
            cs = slice(c * P, (c + 1) * P)
            DMA(s, "sp", qT[i][:], gq_v[:, :, cs], r=[t_q], w=[t_qT[i]], key=t_qT[i])
            DMA(s, "sp", kT[i][:], gk_v[:, :, cs], r=[t_q], w=[t_kT[i]], key=t_kT[i])
            DMA(s, "sp", ktm[i][:], gk_tm[cs, :], r=[t_q], w=[t_ktm[i]], key=t_ktm[i])
            DMA(s, "sp", vv[i][:], v_dram[cs, :], r=[t_q], w=[t_vv[i]], key=t_vv[i])
            DMA(s, "sp", rr[i][:], gr_tm[cs, :], r=[t_q], w=[t_rr[i]], key=t_rr[i])

        def prep(c):
            i = c % 2
            cs = slice(c * P, (c + 1) * P)
            OP(s, "pe", "matmul", ps_z[:], glT[:, cs], wg2[:], start=True, stop=False, r=[t_ld], w=[t_psz])
            OP(s, "pe", "matmul", ps_z[:], ones_row[:], bg[:], start=False, stop=True, r=[t_ld, t_c], w=[t_psz])
            OP(s, "act", "activation", e1[i][:], ps_z[:], AF.Exp, scale=-1.0, r=[t_psz], w=[t_e1[i]])
            OP(s, "act", "activation", e1[i][:], e1[i][:], AF.Ln, bias=pg.one_col[:, 0:1], scale=1.0,
               r=[t_e1[i], pg.t_const], w=[t_e1[i]])
            for h in range(HG):
                OP(s, "pe", "matmul", ps_c[:, h, :], e1[i][:, h * P:(h + 1) * P], tri_incl[:, 0, :],
                   start=True, stop=True, r=[t_e1[i], t_c], w=[t_psc])
            OP(s, "pe", "matmul", ps_r[:], sgt[:], e1[i][:], start=True, stop=True, r=[t_e1[i], t_c], w=[t_psr])
            OP(s, "act", "activation", eq[i][:], ps_c[:], AF.Exp, scale=-I16, r=[t_psc], w=[t_eq[i]])
            OP(s, "act", "activation", ek[i][:], ps_c[:], AF.Exp, scale=I16, r=[t_psc], w=[t_ek[i]])
            OP(s, "act", "activation", es[i][:], ps_r[:], AF.Exp, scale=-I16, r=[t_psr], w=[t_es[i]])
            OP(s, "dve", "tensor_tensor", qd[i][:], qT[i][:], eq[i][:], ALU.mult, r=[t_qT[i], t_eq[i]], w=[t_qd[i]])
            OP(s, "pool", "tensor_tensor", ki[i][:], kT[i][:], ek[i][:], ALU.mult, r=[t_kT[i], t_ek[i]], w=[t_ki[i]])
            OP(s, "pool", "tensor_tensor", kst[i][:], ktm[i][:], es[i][:], ALU.mult, r=[t_ktm[i], t_es[i]],
               w=[t_kst[i]])
            for h in range(HG):
                OP(s, "pe", "matmul", ps_s[:, h, :], ki[i][:, h, :], qd[i][:, h, :], start=True, stop=True,
                   r=[t_ki[i], t_qd[i]], w=[t_pss])
            OP(s, "dve", "tensor_tensor", sT[i][:], ps_s[:], tri_incl[:], ALU.mult, r=[t_pss, t_c], w=[t_sT[i]])
            OP(s, "act", "activation", sg[i][:], rr[i][:], AF.Exp, scale=-1.0, r=[t_rr[i]], w=[t_sg[i]])
            OP(s, "pool", "tensor_scalar", sg[i][:], sg[i][:], 1.0, None, ALU.add, r=[t_sg[i]], w=[t_sg[i]])
            OP(s, "dve", "reciprocal", sg[i][:], sg[i][:], r=[t_sg[i]], w=[t_sg[i]])
            OP(s, "pool", "tensor_tensor", sg[i][:], sg[i][:], rr[i][:], ALU.mult, r=[t_sg[i], t_rr[i]], w=[t_sg[i]])

        def recur(c):
            i = c % 2
            ps_o = ps_o2[c % 2]
            t_pso = t_pso2[c % 2]
            for h in range(HG):
                tpo = t_pso[h // 2]
                vs = vv[i][:, h * DV:(h + 1) * DV]
                if c > 0:
                    OP(s, "pe", "matmul", ps_o[:, h, :], qd[i][:, h, :], state_bf[:, h, :], start=True, stop=False,
                       r=[t_qd[i], t_sbf[h]], w=[tpo])
                OP(s, "pe", "matmul", ps_o[:, h, :], sT[i][:, h, :], vs, start=(c == 0), stop=True,
                   r=[t_sT[i], t_vv[i]], w=[tpo])
            for h in range(HG):
                vs = vv[i][:, h * DV:(h + 1) * DV]
                OP(s, "pe", "matmul", ps_kv[:, h, :], kst[i][:, h * P:(h + 1) * P], vs, start=True, stop=True,
                   r=[t_kst[i], t_vv[i]], w=[t_pskv[h]])
                if c == 0:
                    OP(s, "dve", "tensor_copy", state[:, h, :], ps_kv[:, h, :], r=[t_pskv[h]], w=[t_state[h]])
                else:
                    OP(s, "dve", "scalar_tensor_tensor", state[:, h, :], state[:, h, :], eq[i][:, h, P - 1:P],
                       ps_kv[:, h, :], ALU.mult, ALU.add, r=[t_pskv[h], t_eq[i], t_state[h]], w=[t_state[h]])
                if c + 1 < NCH:
                    OP(s, "act", "copy", state_bf[:, h, :], state[:, h, :], r=[t_state[h]], w=[t_sbf[h]])
            for h in range(HG):
                OP(s, "act", "activation", junk[i][:], ps_o[:, h, :], AF.Square, accum_out=sm[i][:, h:h + 1],
                   r=[t_pso[h // 2]], w=[t_junk[i], t_sm[i]])
            OP(s, "dve", "tensor_scalar", sm[i][:, 4:8], sm[i][:, 0:4], 1.0 / DV, float(RMS_EPS), ALU.mult, ALU.add,
               r=[t_sm[i]], w=[t_sm[i]])
            OP(s, "pool", "tensor_tensor", sm[i][:, 8:12], sm[i][:, 4:8], pg.neg_half[:, 0:4], ALU.pow,
               r=[t_sm[i], pg.t_const], w=[t_sm[i]])
            for h in range(HG):
                OP(s, "dve", "scalar_tensor_tensor", ot[i][:, h * DV:(h + 1) * DV], ps_o[:, h, :],
                   sm[i][:, 8 + h:9 + h], gn_bc[:], ALU.mult, ALU.mult, r=[t_pso[h // 2], t_sm[i], t_ld],
                   w=[t_ot[i]])
            OP(s, "pool", "tensor_tensor", ost[i][:], ot[i][:], sg[i][:], ALU.mult, r=[t_ot[i], t_sg[i]],
               w=[t_ost[i]])
            DMA(s, "sp", o_dram[c * P:(c + 1) * P, :], ost[i][:], r=[t_ost[i]], w=[t_o], key=t_ost[i])

        load(0)
        prep(0)
        for c in range(NCH):
            if c + 1 < NCH:
                load(c + 1)
                prep(c + 1)
            recur(c)
        s.barrier()
        s.emit()
    s.release_phase()


_CORE_STATE = {}


def attn_core_run(pg, st, tag, units, first):
    s = pg.s
    if first:
        cs = {}
        cs["ps_s"] = [pg.ps(st, "pss%s%d" % (tag, i), [P, 2, 512], F32) for i in range(2)]
        cs["t_pss"] = s.tiles("t_pss", 2)
        cs["pT"] = [pg.sb(st, "pT%s%d" % (tag, i), [P, 2, 512], BF16) for i in range(3)]
        cs["t_pT"] = s.tiles("t_pT", 3)
        cs["acc"] = [pg.ps(st, "acc%s%d" % (tag, i), [P, 512], F32) for i in range(4)]
        cs["t_acc"] = s.tiles("t_acc", 4)
        cs["cnt"] = 0
        for i in range(2):
            s.op("dve", lambda e, i=i: e.memset(cs["ps_s"][i][:], 0.0), w=[cs["t_pss"][i]])
        _CORE_STATE[tag] = cs
    cs = _CORE_STATE[tag]
    ps_s, t_pss, pT, t_pT, acc, t_acc = cs["ps_s"], cs["t_pss"], cs["pT"], cs["t_pT"], cs["acc"], cs["t_acc"]

    def s_stage(u, i):
        ps, tps = ps_s[i % 2], t_pss[i % 2]
        p_, tp_ = pT[i % 3], t_pT[i % 3]
        for (bank, off, width, pairs) in u["mm"]:
            for pi, pr in enumerate(pairs):
                lhsT, rhs = pr[0], pr[1]
                o0, wd = (off + pr[2], pr[3]) if len(pr) > 2 else (off, width)
                s.op("pe", lambda e, bank=bank, o0=o0, wd=wd, lhsT=lhsT, rhs=rhs, ps=ps, pi=pi,
                     np_=len(pairs): e.matmul(ps[:, bank, o0:o0 + wd], lhsT, rhs, start=(pi == 0),
                                              stop=(pi == np_ - 1)), r=u["reads"] + [pg.t_const], w=[tps])
        for (bank0, nb, off, width, tab, ttab) in u.get("mask", []):
            s.op("dve", lambda e, bank0=bank0, nb=nb, off=off, width=width, tab=tab, ps=ps: e.tensor_tensor(
                ps[:, bank0:bank0 + nb, off:off + width], ps[:, bank0:bank0 + nb, off:off + width], tab, ALU.add),
                r=[tps, ttab], w=[tps])
        for (bank0, nb, off, width) in u["exp"]:
            s.op("act", lambda e, bank0=bank0, nb=nb, off=off, width=width, ps=ps, p_=p_: e.activation(
                p_[:, bank0:bank0 + nb, off:off + width], ps[:, bank0:bank0 + nb, off:off + width], AF.Exp),
                r=[tps], w=[tp_])

    def pv_stage(u, i):
        p_, tp_ = pT[i % 3], t_pT[i % 3]
        for ent in u["pv"]:
            (a, bank, off, vap, start, stop) = ent[:6]
            s.op("pe", lambda e, a=a, bank=bank, off=off, vap=vap, start=start, stop=stop, p_=p_: e.matmul(
                acc[a][:, 0:129], p_[:, bank, off:off + P], vap, start=start, stop=stop),
                r=[tp_] + u["reads"], w=[t_acc[a]])
            if len(ent) > 6 and ent[6] is not None:
                ent[6](acc, t_acc)
        if u.get("end") is not None:
            u["end"](acc, t_acc)

    n = len(units)
    base = cs["cnt"]
    for i in range(min(2, n)):
        s_stage(units[i], base + i)
    for i in range(n):
        if i + 2 < n:
            s_stage(units[i + 2], base + i + 2)
        pv_stage(units[i], base + i)
    cs["cnt"] = base + n


import math


def diff_lambda_init(layer_idx):
    return 0.8 - 0.6 * math.exp(-0.3 * layer_idx)


DIFF_IN = ["w_in", "lam_q1", "lam_k1", "lam_q2", "lam_k2", "subln_g", "w_out"]
DIFF_SHAPES = {"w_in": [D, 3072], "lam_q1": [64], "lam_k1": [64], "lam_q2": [64], "lam_k2": [64],
               "subln_g": [128], "w_out": [D, D]}
DIL_SHAPES = {"w_in": [D, 9216], "w_out": [D, D]}
GLA_SHAPES = {"w_in": [D, 3088], "w_gate2": [16, 512], "b_gate": [512], "gnorm_g": [256], "w_out": [D, D]}
FFN_SHAPES = {"ln1_g": [D], "ln1_b": [D], "w_ff1": [D, DFF], "w_ff2": [DFF, D], "ln2_g": [D], "ln2_b": [D]}


def layer_shapes(L):
    kind = L % 3
    d = dict([DIFF_SHAPES, DIL_SHAPES, GLA_SHAPES][kind])
    d.update(FFN_SHAPES)
    return d


def build_program(mode="full"):
    nc = bass.Bass("TRN2", target_bir_lowering=False)
    ins = {}

    def din(name, shape):
        ins[name] = nc.dram_tensor(name, list(shape), F32, kind="ExternalInput").ap()
        return ins[name]

    if mode == "full":
        layers = list(range(DEPTH))
    else:
        layers = [int(mode[-1])]
    x = din("x", [S, D])
    for L in layers:
        for k, shp in layer_shapes(L).items():
            din("l%d_%s" % (L, k), shp)
    out = nc.dram_tensor("out", [S, D], F32, kind="ExternalOutput").ap()

    def scratch(name, shape, dt):
        return nc.dram_tensor(name, list(shape), dt, kind="Internal").ap()

    xTa = scratch("xTa", [D, S], BF16)
    xa = scratch("xa", [S, D], F32)
    xb = scratch("xb", [S, D], F32)
    qT = scratch("qT", [D, S], BF16)
    kT = scratch("kT", [D, S], BF16)
    v_tm = scratch("v_tm", [S, D], BF16)
    o_tm = scratch("o_tm", [S, D], BF16)
    aux_da = scratch("aux_da", [8, 2, 4, S], BF16)

    with ExitStack() as stack:
        sched = Sched(nc, stack)
        pg = Prog(nc, sched, stack)
        pg.lam_dram = scratch("lam_d", [DEPTH, 1], F32)
        pg.eps_ln = pg.sb(stack, "eps_ln", [P, 1], F32)
        setup_consts(pg)
        sched.op("pool", lambda e: e.memset(pg.eps_ln[:], LN_EPS), w=[pg.t_const])
        pg.dram["xT"] = [Tile("xT%d" % i) for i in range(NT)]
        pg.dram["xo"] = [Tile("xo%d" % i) for i in range(NT)]
        pg.dram["xi"] = [Tile("xi%d" % i) for i in range(NT)]
        phase_pre(pg, x, xTa)
        if mode.startswith("ffn"):
            L = layers[0]
            p = "l%d_" % L
            phase_ffn(pg, L, x, xTa, ins[p + "w_ff1"], ins[p + "w_ff2"], ins[p + "ln2_g"], ins[p + "ln2_b"],
                      out, xTa)
            return nc, list(ins.keys())
        specs = []
        if any(L % 3 == 0 for L in layers):
            specs.append((aux_da, slopes(DA_H), S))
        if specs:
            build_aux(pg, specs)
        x_cur = x
        for li, L in enumerate(layers):
            p = "l%d_" % L
            kind = L % 3
            last = (li == len(layers) - 1)
            if kind == 0:
                jobs = [dict(kind="fm", name="qT", c0=0, n=1024, dst=qT, dt=BF16, scale=0.125),
                        dict(kind="fm", name="qT", c0=1024, n=1024, dst=kT, dt=BF16),
                        dict(kind="tm", name="qT", c0=2048, n=1024, dst=v_tm, dt=BF16)]
                phase_proj(pg, "p%d" % L, xTa, ins[p + "w_in"], 3072, jobs)
                phase_diffattn(pg, L, qT, kT, v_tm, aux_da,
                               [ins[p + "lam_q1"], ins[p + "lam_k1"], ins[p + "lam_q2"], ins[p + "lam_k2"]],
                               ins[p + "subln_g"], diff_lambda_init(L), o_tm)
            elif kind == 1:
                nds = []
                for g, (window, dil) in enumerate(DIL_GROUPS):
                    nd_g = scratch("nd%d" % g, [S, 8 * 129], F32)
                    nds.append(nd_g)
                    Lg = S // dil
                    blocks = [(r + dil * P * n, dil) for r in range(dil) for n in range(Lg // P)]
                    jobs = [dict(kind="fm", name="qT", c0=0, n=1024, dst=qT, dt=BF16, scale=128.0 ** -0.5, dil=dil),
                            dict(kind="fm", name="qT", c0=1024, n=1024, dst=kT, dt=BF16, dil=dil),
                            dict(kind="tm", name="qT", c0=2048, n=1024, dst=v_tm, dt=BF16, blocks=blocks)]
                    phase_proj(pg, "p%d%d" % (L, g), xTa, ins[p + "w_in"][:, g * 3072:(g + 1) * 3072], 3072, jobs)
                    phase_dilattn(pg, L, g, dil, qT, kT, v_tm, nd_g)
            else:
                gqT = scratch("gqT", [512, S], F32)
                gkT = scratch("gkT", [512, S], F32)
                gk_tm = scratch("gk_tm", [S, 512], F32)
                gr_tm = scratch("gr_tm", [S, D], F32)
                glT = scratch("glT", [16, S], F32)
                jobs = [dict(kind="fm", name="qT", c0=0, n=512, dst=gqT, dt=F32, scale=128.0 ** -0.5),
                        dict(kind="fm", name="qT", c0=512, n=512, dst=gkT, dt=F32),
                        dict(kind="fm", name="qT", c0=3072, n=16, dst=glT, dt=F32),
                        dict(kind="tm", name="qT", c0=512, n=512, dst=gk_tm, dt=F32),
                        dict(kind="tm", name="qT", c0=1024, n=1024, dst=v_tm, dt=BF16),
                        dict(kind="tm", name="qT", c0=2048, n=1024, dst=gr_tm, dt=F32)]
                phase_proj(pg, "p%d" % L, xTa, ins[p + "w_in"], 3088, jobs)
                phase_gla(pg, L, gqT, gkT, gk_tm, v_tm, gr_tm, glT, ins[p + "w_gate2"], ins[p + "b_gate"],
                          ins[p + "gnorm_g"], o_tm)
            mix_only = mode.startswith("mix")
            phase_outproj(pg, L, o_tm, ins[p + "w_out"], x_cur, ins[p + "ln1_g"], ins[p + "ln1_b"],
                          out if mix_only else xa, xTa, nd_drams=(nds if kind == 1 else None))
            if mix_only:
                break
            phase_ffn(pg, L, xa, xTa, ins[p + "w_ff1"], ins[p + "w_ff2"], ins[p + "ln2_g"], ins[p + "ln2_b"],
                      out if last else xb, xTa)
            x_cur = xb
    return nc, list(ins.keys())


_CACHE = {}
LAST_EXEC_NS = None


def kernel(**inputs):
    mode = inputs.pop("_mode", "full")
    if mode not in _CACHE:
        _CACHE[mode] = build_program(mode)
    nc, names = _CACHE[mode]
    x = np.ascontiguousarray(inputs["x"], dtype=np.float32)
    in_maps = []
    for c in range(NCORES):
        m = {}
        for n in names:
            if n == "x":
                m[n] = np.ascontiguousarray(x[c])
            else:
                m[n] = np.ascontiguousarray(inputs[n], dtype=np.float32)
        in_maps.append(m)
    import os
    tr = bool(os.environ.get("KTRACE"))
    res = run_bass_kernel_spmd(nc, in_maps, core_ids=list(range(NCORES)), trace=tr)
    global LAST_EXEC_NS
    LAST_EXEC_NS = res.exec_time_ns
    return np.stack([r["out"] for r in res.results], axis=0)
```

```python
import numpy as np
from contextlib import ExitStack

import concourse.bass as bass
import concourse.mybir as mybir
from concourse.bass_utils import run_bass_kernel_spmd

F32 = mybir.dt.float32
BF16 = mybir.dt.bfloat16
I32 = mybir.dt.int32
AF = mybir.ActivationFunctionType
ALU = mybir.AluOpType

D = 1024
S = 4096
DFF = 4096
DEPTH = 4
ALPHA = (2 * DEPTH) ** 0.25
LN_EPS = 1e-5
RMS_EPS = 1e-6
NCORES = 8
TT = 512
NT = S // TT
P = 128

ENGS = ("pe", "dve", "act", "pool", "sp")


class Tile:
    __slots__ = ("name", "writers", "readers", "sem")

    def __init__(self, name):
        self.name = name
        self.writers = []
        self.readers = []
        self.sem = None


class Op:
    __slots__ = ("eng", "fn", "deps", "is_dma", "semid", "sig", "sigval", "waits", "needed")

    def __init__(self, eng, fn, is_dma=False):
        self.eng = eng
        self.fn = fn
        self.deps = []
        self.is_dma = is_dma
        self.semid = None
        self.sig = False
        self.sigval = 0
        self.waits = []
        self.needed = False


class Sched:
    N_DMA_SEMS = 80

    def __init__(self, nc, stack):
        self.nc = nc
        self.eng_sem = {e: stack.enter_context(nc.semaphore("s_" + e)) for e in ENGS}
        self.dma_sems = [stack.enter_context(nc.semaphore("d%d" % i)) for i in range(self.N_DMA_SEMS)]
        self.eng_cnt = {e: 0 for e in ENGS}
        self.dma_cnt = [0] * self.N_DMA_SEMS
        self.waited = {}
        self.ops = []
        self.fence = None
        self.free_sems = list(range(24, self.N_DMA_SEMS))
        self.free_sems_sw = list(range(24))
        self.phase_tiles = []
        self.n_inst = 0

    def tile(self, name):
        t = Tile(name)
        self.phase_tiles.append(t)
        return t

    def tiles(self, name, n):
        return [self.tile("%s%d" % (name, i)) for i in range(n)]

    def _track(self, op, r, w):
        deps = op.deps
        if self.fence is not None:
            deps.append(self.fence)
        for t in r:
            deps.extend(t.writers)
            t.readers.append(op)
        for t in w:
            if t.readers:
                deps.extend(x for x in t.readers if x is not op)
                deps.extend(t.writers)
                t.writers = [op]
                t.readers = []
            else:
                if op.is_dma and t.writers and all(x.is_dma for x in t.writers):
                    t.writers.append(op)
                else:
                    deps.extend(t.writers)
                    t.writers = [op]
        self.ops.append(op)

    def op(self, eng, fn, r=(), w=()):
        o = Op(eng, fn)
        self._track(o, r, w)
        return o

    def dma(self, q, fn, r=(), w=(), key=None):
        o = Op(q, fn, is_dma=True)
        assert key is not None
        if key.sem is None:
            key.sem = self.free_sems_sw.pop() if q == "pool" else self.free_sems.pop()
        o.semid = key.sem
        self._track(o, r, w)
        return o

    def barrier(self):
        b = Op("sp", lambda e: e.nop())
        b.needed = True
        last = {}
        for o in self.ops:
            if o.is_dma:
                last[("d", o.semid)] = o
            else:
                last[o.eng] = o
        b.deps = list(last.values())
        if self.fence is not None:
            b.deps.append(self.fence)
        self.ops.append(b)
        self.fence = b
        for t in self.phase_tiles:
            t.writers = []
            t.readers = []

    def release_phase(self):
        for t in self.phase_tiles:
            if t.sem is not None:
                (self.free_sems_sw if t.sem < 24 else self.free_sems).append(t.sem)
                t.sem = None
        self.phase_tiles = []

    def emit(self):
        ops = self.ops
        self.ops = []
        for o in ops:
            for d in o.deps:
                if d.eng == "pe" and o.eng == "pe" and not d.is_dma and not o.is_dma:
                    continue
                d.needed = True
        for o in ops:
            if o.is_dma:
                self.dma_cnt[o.semid] += 16
                o.sigval = self.dma_cnt[o.semid]
            elif o.needed:
                self.eng_cnt[o.eng] += 1
                o.sigval = self.eng_cnt[o.eng]
                o.sig = True
            req = {}
            for d in o.deps:
                if d.is_dma:
                    k = ("d", d.semid)
                else:
                    if d.eng == "pe" and o.eng == "pe" and not o.is_dma:
                        continue
                    k = ("e", d.eng)
                if d.sigval > req.get(k, 0):
                    req[k] = d.sigval
            for k, v in req.items():
                wk = (o.eng, k)
                if self.waited.get(wk, 0) >= v:
                    continue
                self.waited[wk] = v
                o.waits.append((k, v))
        per = {e: [] for e in ENGS}
        for o in ops:
            per[o.eng].append(o)
        self.n_inst += len(ops)

        def run(e, lst):
            for o in lst:
                for (k, v) in o.waits:
                    sem = self.dma_sems[k[1]] if k[0] == "d" else self.eng_sem[k[1]]
                    e.wait_ge(sem, v)
                ins = o.fn(e)
                if o.is_dma:
                    ins.then_inc(self.dma_sems[o.semid], 16)
                elif o.sig:
                    ins.then_inc(self.eng_sem[o.eng], 1)

        with self.nc.Block() as block:
            @block.tensor
            def _(e):
                run(e, per["pe"])

            @block.vector
            def _(e):
                run(e, per["dve"])

            @block.scalar
            def _(e):
                run(e, per["act"])

            @block.gpsimd
            def _(e):
                run(e, per["pool"])

            @block.sync
            def _(e):
                run(e, per["sp"])


class Prog:
    def __init__(self, nc, sched, stack):
        self.nc = nc
        self.s = sched
        self.stack = stack
        self.dram = {}

    def sb(self, stack, name, shape, dt):
        return stack.enter_context(self.nc.sbuf_tensor(name, list(shape), dt))

    def ps(self, stack, name, shape, dt):
        return stack.enter_context(self.nc.psum_tensor(name, list(shape), dt))


def setup_consts(pg):
    nc, s = pg.nc, pg.s
    st = pg.stack
    pg.ident = pg.sb(st, "ident", [P, P], BF16)
    pg.ones_bf = pg.sb(st, "ones_bf", [P, P], BF16)
    pg.identf = pg.sb(st, "identf", [P, P], F32)
    t_id = s.tile("ident")
    s.op("pool", lambda e: e.memset(pg.identf[:], 1.0), w=[t_id])
    s.op("pool", lambda e: e.affine_select(pg.identf[:], pg.identf[:], [[-1, P]], ALU.is_equal, 0.0,
                                           base=0, channel_multiplier=1), r=[t_id], w=[t_id])
    s.op("pool", lambda e: e.tensor_copy(pg.ident[:], pg.identf[:]), r=[t_id], w=[t_id])
    s.op("pool", lambda e: e.memset(pg.ones_bf[:], 1.0), w=[t_id])
    pg.t_const = t_id
    pg.tri_le3 = pg.sb(st, "tri_le3", [P, 1, P], F32)
    pg.tri_ge3 = pg.sb(st, "tri_ge3", [P, 1, P], F32)
    s.op("pool", lambda e: e.memset(pg.tri_le3[:], 0.0), w=[t_id])
    s.op("pool", lambda e: e.affine_select(pg.tri_le3[:], pg.tri_le3[:], [[1, P]], ALU.is_ge, NEG,
                                           base=0, channel_multiplier=-1), r=[t_id], w=[t_id])
    s.op("pool", lambda e: e.memset(pg.tri_ge3[:], 0.0), w=[t_id])
    s.op("pool", lambda e: e.affine_select(pg.tri_ge3[:], pg.tri_ge3[:], [[-1, P]], ALU.is_ge, NEG,
                                           base=0, channel_multiplier=1), r=[t_id], w=[t_id])
    pg.tri_bf = pg.sb(st, "tri_bf", [P, 2, P], BF16)
    s.op("pool", lambda e: e.tensor_copy(pg.tri_bf[:, 0:1, :], pg.tri_le3[:]), r=[t_id], w=[t_id])
    s.op("pool", lambda e: e.tensor_copy(pg.tri_bf[:, 1:2, :], pg.tri_ge3[:]), r=[t_id], w=[t_id])
    pg.one_col = pg.sb(st, "one_col", [P, 1], F32)
    s.op("pool", lambda e: e.memset(pg.one_col[:], 1.0), w=[t_id])
    pg.neg_half = pg.sb(st, "neg_half", [P, 8], F32)
    s.op("pool", lambda e: e.memset(pg.neg_half[:], -0.5), w=[t_id])
    pg.eps_rms = pg.sb(st, "eps_rms", [P, 1], F32)
    s.op("pool", lambda e: e.memset(pg.eps_rms[:], RMS_EPS), w=[t_id])


def ln_tail(pg, xw, t_xw, gb, t_gb, xo_dram_rows, t_xo_dram, xTo, t_xTo, col0, bufs, i):
    s = pg.s
    st6 = bufs["st6"][i % 2]
    t_st = bufs["t_st"][i % 2]
    mv = bufs["mv"][i % 2]
    xbf = bufs["xbf"][i % 2]
    t_xbf = bufs["t_xbf"][i % 2]
    psT = bufs["psT"][i % 2]
    t_psT = bufs["t_psT"][i % 2]
    s.op("dve", lambda e: e.bn_stats(st6[:, 0:6], xw[:, 0:512]), r=[t_xw], w=[t_st])
    s.op("dve", lambda e: e.bn_stats(st6[:, 6:12], xw[:, 512:1024]), r=[t_xw], w=[t_st])
    s.op("dve", lambda e: e.bn_aggr(mv[:, 0:2], st6[:, 0:12]), r=[t_st], w=[t_st])
    s.op("act", lambda e: e.activation(mv[:, 2:3], mv[:, 1:2], AF.Sqrt, bias=pg.eps_ln[:, 0:1], scale=1.0),
         r=[t_st, pg.t_const], w=[t_st])
    s.op("dve", lambda e: e.reciprocal(mv[:, 3:4], mv[:, 2:3]), r=[t_st], w=[t_st])
    s.op("dve", lambda e: e.scalar_tensor_tensor(mv[:, 4:5], mv[:, 0:1], -1.0, mv[:, 3:4], ALU.mult, ALU.mult),
         r=[t_st], w=[t_st])
    s.op("act", lambda e: e.activation(xw[:], xw[:], AF.Identity, bias=mv[:, 4:5], scale=mv[:, 3:4]),
         r=[t_xw, t_st], w=[t_xw])
    g_bc, b_bc = gb
    s.op("pool", lambda e: e.tensor_tensor(xw[:], xw[:], g_bc[:], ALU.mult), r=[t_xw, t_gb], w=[t_xw])
    s.op("pool", lambda e: e.tensor_tensor(xw[:], xw[:], b_bc[:], ALU.add), r=[t_xw, t_gb], w=[t_xw])
    s.dma("sp", lambda e: e.dma_start(out=xo_dram_rows, in_=xw[:]), r=[t_xw], w=[t_xo_dram], key=t_xw)
    s.op("act", lambda e: e.copy(xbf[:], xw[:]), r=[t_xw], w=[t_xbf])

    def back():
        for j in range(8):
            s.op("pe", lambda e, j=j: e.transpose(psT[:, j, :], xbf[:, j * P:(j + 1) * P], pg.ident[:]),
                 r=[t_xbf, pg.t_const], w=[t_psT])
        s.op("dve", lambda e: e.tensor_copy(xTo[:, :, col0:col0 + P], psT[:]), r=[t_psT], w=[t_xTo])
    return back


def alloc_ln_bufs(pg, st, tag):
    s = pg.s
    b = {}
    b["st6"] = [pg.sb(st, "st6%s%d" % (tag, i), [P, 12], F32) for i in range(2)]
    b["mv"] = [pg.sb(st, "mv%s%d" % (tag, i), [P, 8], F32) for i in range(2)]
    b["t_st"] = s.tiles("t_st" + tag, 2)
    b["xbf"] = [pg.sb(st, "xbf%s%d" % (tag, i), [P, D], BF16) for i in range(2)]
    b["t_xbf"] = s.tiles("t_xbf" + tag, 2)
    b["psT"] = [pg.ps(st, "psT%s%d" % (tag, i), [P, 8, P], BF16) for i in range(1)] * 2
    b["t_psT"] = s.tiles("t_psT" + tag, 1) * 2
    return b


def load_gb(pg, st, g_ap, b_ap, tag):
    s = pg.s
    g_bc = pg.sb(st, "g_bc" + tag, [P, D], F32)
    b_bc = pg.sb(st, "b_bc" + tag, [P, D], F32)
    t_gb = s.tile("t_gb" + tag)
    s.dma("sp", lambda e: e.dma_start(out=g_bc[:], in_=g_ap.partition_broadcast(P)), w=[t_gb], key=t_gb)
    s.dma("sp", lambda e: e.dma_start(out=b_bc[:], in_=b_ap.partition_broadcast(P)), w=[t_gb], key=t_gb)
    return (g_bc, b_bc), t_gb


def phase_pre(pg, x_in, xT_dram):
    nc, s = pg.nc, pg.s
    with ExitStack() as st:
        xin = [pg.sb(st, "pre_x%d" % i, [P, D], F32) for i in range(3)]
        t_xin = s.tiles("pre_tx", 3)
        xbf = [pg.sb(st, "pre_xbf%d" % i, [P, D], BF16) for i in range(2)]
        t_xbf = s.tiles("pre_txbf", 2)
        psT = [pg.ps(st, "pre_psT%d" % i, [P, 8, P], BF16) for i in range(2)]
        t_psT = s.tiles("pre_tps", 2)
        xTo = [pg.sb(st, "pre_xTo%d" % i, [P, 8, TT], BF16) for i in range(2)]
        t_xTo = s.tiles("pre_txTo", 2)
        xT_v = xT_dram.rearrange("(c p) t -> p c t", p=P)
        nblk = S // P
        t_xT = pg.dram["xT"]

        def load(b):
            s.dma("sp", lambda e: e.dma_start(out=xin[b % 3][:], in_=x_in[b * P:(b + 1) * P, :]),
                  w=[t_xin[b % 3]], key=t_xin[b % 3])
        load(0)
        load(1)
        for b in range(nblk):
            if b + 2 < nblk:
                load(b + 2)
            tt, sub = divmod(b, 4)
            cur_xbf, cur_t = xbf[b % 2], t_xbf[b % 2]
            s.op("act", lambda e, b=b, o=cur_xbf: e.copy(o[:], xin[b % 3][:]), r=[t_xin[b % 3]], w=[cur_t])
            pt, tpt = psT[b % 2], t_psT[b % 2]
            for j in range(8):
                s.op("pe", lambda e, j=j, pt=pt, o=cur_xbf: e.transpose(pt[:, j, :], o[:, j * P:(j + 1) * P],
                                                                       pg.ident[:]),
                     r=[cur_t, pg.t_const], w=[tpt])
            xo, txo = xTo[tt % 2], t_xTo[tt % 2]
            s.op("dve", lambda e, pt=pt, xo=xo, sub=sub: e.tensor_copy(xo[:, :, sub * P:(sub + 1) * P], pt[:]),
                 r=[tpt], w=[txo])
            if sub == 3:
                s.dma("sp", lambda e, xo=xo, tt=tt: e.dma_start(out=xT_v[:, :, tt * TT:(tt + 1) * TT], in_=xo[:]),
                      r=[txo], w=[t_xT[tt]], key=txo)
        s.barrier()
        s.emit()
    s.release_phase()


def phase_ffn(pg, L, x_old_dram, xT_dram, w1, w2, g_ap, b_ap, x_new_dram, xT_new_dram):
    nc, s = pg.nc, pg.s
    tag = "f%d" % L
    with ExitStack() as st:
        w1_sb = pg.sb(st, "w1" + tag, [P, 8, DFF], BF16)
        w2_sb = pg.sb(st, "w2" + tag, [P, 32, D], BF16)
        t_w1 = s.tiles("t_w1", 8)
        t_w2 = s.tiles("t_w2", 8)
        w1_v = w1.rearrange("(c p) f -> p c f", p=P)
        w2_v = w2.rearrange("(c p) d -> p c d", p=P)
        t_w1a = s.tiles("t_w1a", 4)
        for c in range(4):
            s.dma("pool", lambda e, c=c: e.dma_start(out=w1_sb[:, :, c * P:(c + 1) * P],
                                                     in_=w1_v[:, :, c * P:(c + 1) * P]),
                  w=[t_w1a[c]], key=t_w1a[c])
        for c in range(1, 8):
            s.dma("pool", lambda e, c=c: e.dma_start(out=w1_sb[:, :, c * 512:(c + 1) * 512],
                                                     in_=w1_v[:, :, c * 512:(c + 1) * 512]),
                  w=[t_w1[c]], key=t_w1[c])
        for c in range(8):
            s.dma("pool", lambda e, c=c: e.dma_start(out=w2_sb[:, 4 * c:4 * c + 4, :], in_=w2_v[:, 4 * c:4 * c + 4, :]),
                  w=[t_w2[c]], key=t_w2[c])
        gb, t_gb = load_gb(pg, st, g_ap, b_ap, tag)
        xT = [pg.sb(st, "xT%s%d" % (tag, i), [P, 8, TT], BF16) for i in range(2)]
        t_xTb = s.tiles("t_xTb", 2)
        hT = pg.sb(st, "hT" + tag, [P, 32, TT], BF16)
        t_hT = s.tiles("t_hT", 32)
        rl = [pg.sb(st, "rl%s%d" % (tag, i), [P, TT], F32) for i in range(2)]
        t_rl = s.tiles("t_rl", 2)
        xw = [pg.sb(st, "xw%s%d" % (tag, i), [P, D], F32) for i in range(3)]
        t_xw = s.tiles("t_xw", 3)
        ps_h = [pg.ps(st, "psh%s%d" % (tag, i), [P, TT], F32) for i in range(3)]
        t_psh = s.tiles("t_psh", 3)
        ps_y = [pg.ps(st, "psy%s%d" % (tag, i), [P, D], F32) for i in range(2)]
        t_psy = s.tiles("t_psy", 2)
        lnb = alloc_ln_bufs(pg, st, tag)
        xT_v = xT_dram.rearrange("(c p) t -> p c t", p=P)
        xTn_v = xT_new_dram.rearrange("(c p) t -> p c t", p=P)
        t_xT = pg.dram["xT"]
        t_xo = pg.dram["xo"]
        t_xi = pg.dram["xi"]

        def load_xT(t):
            s.dma("sp", lambda e: e.dma_start(out=xT[t % 2][:], in_=xT_v[:, :, t * TT:(t + 1) * TT]),
                  r=[t_xT[t]], w=[t_xTb[t % 2]], key=t_xTb[t % 2])

        def load_xold(b):
            s.dma("sp", lambda e: e.dma_start(out=xw[b % 3][:], in_=x_old_dram[b * P:(b + 1) * P, :]),
                  r=[t_xi[b // 4]], w=[t_xw[b % 3]], key=t_xw[b % 3])

        load_xT(0)
        if NT > 1:
            load_xT(1)
        hcnt = 0
        pend = []

        def flush():
            while pend:
                pend.pop(0)()

        for t in range(NT):
            xTt, txT = xT[t % 2], t_xTb[t % 2]
            for c in range(32):
                ph, tph = ps_h[hcnt % 3], t_psh[hcnt % 3]
                r_, tr_ = rl[hcnt % 2], t_rl[hcnt % 2]
                hcnt += 1
                for kc in range(8):
                    s.op("pe", lambda e, c=c, kc=kc, ph=ph, xTt=xTt: e.matmul(
                        ph[:], w1_sb[:, kc, c * P:(c + 1) * P], xTt[:, kc, :], start=(kc == 0), stop=(kc == 7)),
                        r=[t_w1a[c] if c < 4 else t_w1[c // 4], txT], w=[tph])
                s.op("act", lambda e, ph=ph, r_=r_: e.activation(r_[:], ph[:], AF.Relu), r=[tph], w=[tr_])
                s.op("dve", lambda e, c=c, r_=r_: e.tensor_tensor(hT[:, c, :], r_[:], r_[:], ALU.mult),
                     r=[tr_], w=[t_hT[c]])
                if c == 5:
                    flush()
            for sub in range(4):
                b = t * 4 + sub
                load_xold(b)
                py, tpy = ps_y[b % 2], t_psy[b % 2]
                for c in range(32):
                    for half in range(2):
                        s.op("pe", lambda e, c=c, half=half, py=py, sub=sub: e.matmul(
                            py[:, half * 512:(half + 1) * 512], hT[:, c, sub * P:(sub + 1) * P],
                            w2_sb[:, c, half * 512:(half + 1) * 512], start=(c == 0), stop=(c == 31)),
                            r=[t_hT[c], t_w2[c // 4]], w=[tpy])
                xw_, txw_ = xw[b % 3], t_xw[b % 3]
                s.op("dve", lambda e, xw_=xw_, py=py: e.scalar_tensor_tensor(
                    xw_[:], xw_[:], float(ALPHA), py[:], ALU.mult, ALU.add), r=[txw_, tpy], w=[txw_])
                back = ln_tail(pg, xw_, txw_, gb, t_gb, x_new_dram[b * P:(b + 1) * P, :], t_xo[t],
                               xTt, txT, sub * P, lnb, b)
                flush()
                pend.append(back)
                if sub == 3:
                    def fin(t=t, xTt=xTt, txT=txT):
                        s.dma("sp", lambda e: e.dma_start(out=xTn_v[:, :, t * TT:(t + 1) * TT], in_=xTt[:]),
                              r=[txT], w=[t_xT[t]], key=txT)
                        if t + 2 < NT:
                            load_xT(t + 2)
                    pend.append(fin)
        flush()
        s.barrier()
        s.emit()
    s.release_phase()


NEG = -30000.0
DA_H = 8


def slopes(n):
    return [2.0 ** (-8.0 * (h + 1) / n) for h in range(n)]


def dram_tile(pg, name):
    if name not in pg.dram:
        pg.dram[name] = Tile(name)
    return pg.dram[name]


def phase_proj(pg, tag, xT_dram, w_ap, ncols, jobs):
    s = pg.s
    with ExitStack() as st:
        xT = pg.sb(st, "pjx" + tag, [P, 8, S], BF16)
        xT_v = xT_dram.rearrange("(c p) t -> p c t", p=P)
        NXT = S // 512
        t_x = s.tiles("pj_tx", NXT)
        for c in range(NXT):
            s.dma("sp", lambda e, c=c: e.dma_start(out=xT[:, :, c * 512:(c + 1) * 512],
                                                   in_=xT_v[:, :, c * 512:(c + 1) * 512]),
                  r=pg.dram["xT"], w=[t_x[c]], key=t_x[c])
        w_sb = pg.sb(st, "pjw" + tag, [P, 8, ncols], BF16)
        NWB = (ncols + 511) // 512
        t_w = s.tiles("pj_tw", NWB)
        w_v = w_ap.rearrange("(c p) n -> p c n", p=P)
        worder = []
        for job in jobs:
            for wb in range(job["c0"] // 512, (job["c0"] + job["n"] + 511) // 512):
                if wb not in worder:
                    worder.append(wb)
        for wb in worder:
            hi = min(ncols, (wb + 1) * 512)
            s.dma("pool", lambda e, wb=wb, hi=hi: e.dma_start(out=w_sb[:, :, wb * 512:hi], in_=w_v[:, :, wb * 512:hi]),
                  w=[t_w[wb]], key=t_w[wb])
        ps = [pg.ps(st, "pjps%s%d" % (tag, i), [P, 512], F32) for i in range(4)]
        t_ps = s.tiles("pj_tps", 4)
        stg = {}
        cnt = {"ps": 0, "ev": 0}

        def staging(kind, dt, n):
            key = (kind, dt, n)
            if key not in stg:
                nm = "pjs%s%d" % (tag, len(stg))
                bufs = [pg.sb(st, nm + "_%d" % i, [P, n], dt) for i in range(2)]
                stg[key] = [bufs, s.tiles(nm, 2), 0]
            ent = stg[key]
            i = ent[2] % 2
            ent[2] += 1
            return ent[0][i], ent[1][i]

        def evac(dst_ap, src_ap, scale, rt, wt):
            i = cnt["ev"]
            cnt["ev"] += 1
            if i % 2 == 0:
                if scale == 1.0:
                    s.op("act", lambda e: e.copy(dst_ap, src_ap), r=[rt], w=[wt])
                else:
                    s.op("act", lambda e: e.mul(dst_ap, src_ap, float(scale)), r=[rt], w=[wt])
            else:
                if scale == 1.0:
                    s.op("dve", lambda e: e.tensor_copy(dst_ap, src_ap), r=[rt], w=[wt])
                else:
                    s.op("dve", lambda e: e.tensor_scalar(dst_ap, src_ap, float(scale), None, ALU.mult),
                         r=[rt], w=[wt])

        for job in jobs:
            c0, n, dst, dt = job["c0"], job["n"], job["dst"], job["dt"]
            scale = job.get("scale", 1.0)
            t_dst = dram_tile(pg, job["name"])
            if job["kind"] == "fm":
                dil = job.get("dil", 1)
                for cb in range((n + P - 1) // P):
                    m = min(P, n - cb * P)
                    stage, t_stage = staging("fm", dt, S)
                    for t in range(S // 512):
                        bank, tb = ps[cnt["ps"] % 4], t_ps[cnt["ps"] % 4]
                        cnt["ps"] += 1
                        for kc in range(8):
                            lhs = w_sb[:, kc, c0 + cb * P:c0 + cb * P + m]
                            rhs = xT[:, kc, t * 512:(t + 1) * 512]
                            s.op("pe", lambda e, kc=kc, bank=bank, m=m, lhs=lhs, rhs=rhs: e.matmul(
                                bank[:m, :], lhs, rhs, start=(kc == 0), stop=(kc == 7)),
                                r=[t_w[(c0 + cb * P) // 512], t_x[t]], w=[tb])
                        if dil == 1:
                            evac(stage[:m, t * 512:(t + 1) * 512], bank[:m, :], scale, tb, t_stage)
                        else:
                            wd = 512 // dil
                            src_ap = bank[:m, :].rearrange("p (l r) -> p r l", r=dil)
                            dst_ap = stage[:m, :].rearrange("p (r l) -> p r l", r=dil)[:, :, t * wd:(t + 1) * wd]
                            evac(dst_ap, src_ap, scale, tb, t_stage)
                    s.dma("sp", lambda e, o_=dst[cb * P:cb * P + m, :], i_=stage[:m, :]: e.dma_start(out=o_, in_=i_),
                          r=[t_stage], w=[t_dst], key=t_stage)
            else:
                blocks = job.get("blocks") or [(b * P, 1) for b in range(S // P)]
                for bi, (start, step) in enumerate(blocks):
                    stage, t_stage = staging("tm", dt, n)
                    xdeps = [t_x[j] for j in range(start // 512, (start + (P - 1) * step) // 512 + 1)]
                    for cg in range((n + 511) // 512):
                        wcol = min(512, n - cg * 512)
                        bank, tb = ps[cnt["ps"] % 4], t_ps[cnt["ps"] % 4]
                        cnt["ps"] += 1
                        for kc in range(8):
                            if step == 1:
                                lhs = xT[:, kc, start:start + P]
                            else:
                                lhs = xT[:, kc, start:start + (P - 1) * step + 1:step]
                            rhs = w_sb[:, kc, c0 + cg * 512:c0 + cg * 512 + wcol]
                            s.op("pe", lambda e, kc=kc, bank=bank, lhs=lhs, rhs=rhs, wcol=wcol: e.matmul(
                                bank[:, :wcol], lhs, rhs, start=(kc == 0), stop=(kc == 7)),
                                r=[t_w[(c0 + cg * 512) // 512]] + xdeps, w=[tb])
                        evac(stage[:, cg * 512:cg * 512 + wcol], bank[:, :wcol], scale, tb, t_stage)
                    s.dma("sp", lambda e, o_=dst[bi * P:(bi + 1) * P, :], i_=stage[:]: e.dma_start(out=o_, in_=i_),
                          r=[t_stage], w=[t_dst], key=t_stage)
        s.barrier()
        s.emit()
    s.release_phase()


def phase_outproj(pg, L, o_dram, w_out, x_old_dram, g_ap, b_ap, x_new_dram, xT_new_dram, nd_drams=None):
    s = pg.s
    tag = "o%d" % L
    NBX = 8
    with ExitStack() as st:
        w_sb = pg.sb(st, "wo" + tag, [P, 8, D], BF16)
        t_w = s.tile("t_wo")
        w_v = w_out.rearrange("(c p) n -> p c n", p=P)
        for c in range(2):
            s.dma("pool", lambda e, c=c: e.dma_start(out=w_sb[:, 4 * c:4 * c + 4, :], in_=w_v[:, 4 * c:4 * c + 4, :]),
                  w=[t_w], key=t_w)
        (g_bc, b_bc), t_gb = load_gb(pg, st, g_ap, b_ap, tag)
        NOB = 4
        ob = [pg.sb(st, "ob%s%d" % (tag, i), [P, D], BF16) for i in range(NOB)]
        t_ob = s.tiles("t_ob", NOB)
        if nd_drams is not None:
            ndt = [[pg.sb(st, "ndt%s%d%d" % (tag, i, g), [P, 8, 129], F32) for g in range(3)] for i in range(NOB)]
            t_ndt = [[s.tile("t_ndt") for g in range(3)] for i in range(NOB)]
            rec = [pg.sb(st, "rec%s%d" % (tag, i), [P, 8], F32) for i in range(NOB)]
            t_nd = dram_tile(pg, "nd")
        oT = [pg.sb(st, "oT%s%d" % (tag, i), [P, 8, P], BF16) for i in range(2)]
        t_oT = s.tiles("t_oT", 2)
        psT = [pg.ps(st, "psTo%s%d" % (tag, i), [P, 8, P], BF16) for i in range(2)]
        t_psT = s.tiles("t_psTo", 2)
        ps_y = [pg.ps(st, "psy%s%d" % (tag, i), [P, D], F32) for i in range(2)]
        t_psy = s.tiles("t_psy", 2)
        xw = [pg.sb(st, "xw%s%d" % (tag, i), [P, D], F32) for i in range(NBX)]
        t_xw = s.tiles("t_xw", NBX)
        st6 = [pg.sb(st, "st6%s%d" % (tag, i), [P, 12], F32) for i in range(NBX)]
        mv = [pg.sb(st, "mv%s%d" % (tag, i), [P, 8], F32) for i in range(NBX)]
        t_st = s.tiles("t_st", NBX)
        xbf = [pg.sb(st, "xbf%s%d" % (tag, i), [P, D], BF16) for i in range(3)]
        t_xbf = s.tiles("t_xbf", 3)
        psX = pg.ps(st, "psX" + tag, [P, 8, P], BF16)
        t_psX = s.tile("t_psX")
        xTo = [pg.sb(st, "xTo%s%d" % (tag, i), [P, 8, TT], BF16) for i in range(3)]
        t_xTo = s.tiles("t_xTo", 3)
        xTn_v = xT_new_dram.rearrange("(c p) t -> p c t", p=P)
        t_o = dram_tile(pg, "o_tm")
        t_xT = pg.dram["xT"]
        nblk = S // P

        def load(b):
            i = b % NOB
            if nd_drams is None:
                DMA(s, "sp", ob[i][:], o_dram[b * P:(b + 1) * P, :], r=[t_o], w=[t_ob[i]], key=t_ob[i])
            else:
                for g in range(3):
                    DMA(s, "sp", ndt[i][g][:], nd_drams[g][b * P:(b + 1) * P, :].rearrange("p (h c) -> p h c", h=8),
                        r=[t_nd], w=[t_ndt[i][g]], key=t_ndt[i][g])
            DMA(s, "sp", xw[b % NBX][:], x_old_dram[b * P:(b + 1) * P, :], r=[pg.dram["xi"][b // 4]],
                w=[t_xw[b % NBX]], key=t_xw[b % NBX])

        def stage_pre(b):
            if nd_drams is None:
                return
            i = b % NOB
            n0, n1, n2 = ndt[i]
            t0, t1, t2 = t_ndt[i]
            OP(s, "pool", "tensor_tensor", n0[:], n0[:], n1[:], ALU.add, r=[t0, t1], w=[t0])
            OP(s, "pool", "tensor_tensor", n0[:], n0[:], n2[:], ALU.add, r=[t0, t2], w=[t0])
            OP(s, "dve", "reciprocal", rec[i][:], n0[:, :, 128], r=[t0], w=[t0])
            OP(s, "dve", "tensor_tensor", ob[i][:].rearrange("p (h c) -> p h c", h=8), n0[:, :, 0:128],
               rec[i][:].unsqueeze(2).to_broadcast([P, 8, P]), ALU.mult, r=[t0], w=[t_ob[i]])

        def stage_a(b):
            pt, tpt = psT[b % 2], t_psT[b % 2]
            o_, to_ = ob[b % NOB], t_ob[b % NOB]
            for j in range(8):
                OP(s, "pe", "transpose", pt[:, j, :], o_[:, j * P:(j + 1) * P], pg.ident[:], r=[to_, pg.t_const],
                   w=[tpt])
            oT_, toT_ = oT[b % 2], t_oT[b % 2]
            OP(s, "act", "copy", oT_[:], pt[:], r=[tpt], w=[toT_])
            py, tpy = ps_y[b % 2], t_psy[b % 2]
            for kc in range(8):
                for half in range(2):
                    OP(s, "pe", "matmul", py[:, half * 512:(half + 1) * 512], oT_[:, kc, :],
                       w_sb[:, kc, half * 512:(half + 1) * 512], start=(kc == 0), stop=(kc == 7),
                       r=[toT_, t_w], w=[tpy])

        def stages(b):
            t, sub = divmod(b, 4)
            py, tpy = ps_y[b % 2], t_psy[b % 2]
            x_, tx_ = xw[b % NBX], t_xw[b % NBX]
            s6_, mv_, ts_ = st6[b % NBX], mv[b % NBX], t_st[b % NBX]
            xb_, txb_ = xbf[b % 3], t_xbf[b % 3]
            xo_, txo_ = xTo[t % 3], t_xTo[t % 3]

            def s1():
                OP(s, "dve", "scalar_tensor_tensor", x_[:], x_[:], float(ALPHA), py[:], ALU.mult, ALU.add,
                   r=[tx_, tpy], w=[tx_])
                OP(s, "dve", "bn_stats", s6_[:, 0:6], x_[:, 0:512], r=[tx_], w=[ts_])
                OP(s, "dve", "bn_stats", s6_[:, 6:12], x_[:, 512:1024], r=[tx_], w=[ts_])
                OP(s, "dve", "bn_aggr", mv_[:, 0:2], s6_[:, 0:12], r=[ts_], w=[ts_])

            def s2():
                OP(s, "act", "activation", mv_[:, 2:3], mv_[:, 1:2], AF.Sqrt, bias=pg.eps_ln[:, 0:1], scale=1.0,
                   r=[ts_, pg.t_const], w=[ts_])
                OP(s, "dve", "reciprocal", mv_[:, 3:4], mv_[:, 2:3], r=[ts_], w=[ts_])
                OP(s, "dve", "scalar_tensor_tensor", mv_[:, 4:5], mv_[:, 0:1], -1.0, mv_[:, 3:4], ALU.mult, ALU.mult,
                   r=[ts_], w=[ts_])

            def s3():
                OP(s, "act", "activation", x_[:], x_[:], AF.Identity, bias=mv_[:, 4:5], scale=mv_[:, 3:4],
                   r=[tx_, ts_], w=[tx_])

            def s4():
                OP(s, "dve" if nd_drams is not None else "pool", "tensor_tensor", x_[:], x_[:], g_bc[:], ALU.mult,
                   r=[tx_, t_gb], w=[tx_])
                OP(s, "dve", "tensor_tensor", x_[:], x_[:], b_bc[:], ALU.add, r=[tx_, t_gb], w=[tx_])

            def s5():
                DMA(s, "sp", x_new_dram[b * P:(b + 1) * P, :], x_[:], r=[tx_], w=[pg.dram["xo"][t]], key=tx_)
                OP(s, "act", "copy", xb_[:], x_[:], r=[tx_], w=[txb_])

            def s6():
                for j in range(8):
                    OP(s, "pe", "transpose", psX[:, j, :], xb_[:, j * P:(j + 1) * P], pg.ident[:],
                       r=[txb_, pg.t_const], w=[t_psX])
                OP(s, "act", "copy", xo_[:, :, sub * P:(sub + 1) * P], psX[:], r=[t_psX], w=[txo_])
                if sub == 3:
                    DMA(s, "sp", xTn_v[:, :, t * TT:(t + 1) * TT], xo_[:], r=[txo_], w=[t_xT[t]], key=txo_)
            return [s1, s2, s3, s4, s5, s6]

        for b in range(min(3, nblk)):
            load(b)
        for b in range(min(2, nblk)):
            stage_pre(b)
        stage_a(0)
        NST = 6
        all_st = {}
        for i in range(nblk + NST - 1):
            if i + 3 < nblk:
                load(i + 3)
            if i + 2 < nblk:
                stage_pre(i + 2)
            if i + 1 < nblk:
                stage_a(i + 1)
            if i < nblk:
                all_st[i] = stages(i)
            for k in range(NST):
                b = i - k
                if 0 <= b < nblk:
                    all_st[b][k]()
        s.barrier()
        s.emit()
    s.release_phase()


def build_aux(pg, specs):
    s = pg.s
    with ExitStack() as st:
        ia = pg.sb(st, "aux_ia", [1, S], I32)
        bB = pg.sb(st, "aux_bB", [1, S], BF16)
        bO = pg.sb(st, "aux_bO", [1, S], BF16)
        t_b = s.tile("t_auxb")
        OP(s, "pool", "iota", ia[:], [[0, S // P], [1, P]], base=0, channel_multiplier=0, w=[t_b])
        OP(s, "pool", "tensor_copy", bB[:], ia[:], r=[t_b], w=[t_b])
        OP(s, "pool", "memset", bO[:], 1.0, w=[t_b])
        ps = [pg.ps(st, "aux_ps%d" % i, [64, 512], F32) for i in range(2)]
        t_ps = s.tiles("t_auxps", 2)
        t_aq = dram_tile(pg, "auxda")
        cnt = 0
        for si, (aux_dram, mults, Lseq) in enumerate(specs):
            ib = pg.sb(st, "aux_ib%d" % si, [1, S], I32)
            bA = pg.sb(st, "aux_bA%d" % si, [1, S], BF16)
            mm = pg.sb(st, "aux_m%d" % si, [1, 3, 64], BF16)
            outb = pg.sb(st, "aux_o%d" % si, [64, S], BF16)
            t_s = s.tile("t_auxs%d" % si)
            t_o = s.tile("t_auxo%d" % si)
            nrep = S // Lseq
            OP(s, "pool", "iota", ib[:], [[0, nrep], [128, Lseq // P], [0, P]], base=0, channel_multiplier=0, w=[t_s])
            OP(s, "pool", "tensor_copy", bA[:], ib[:], r=[t_s], w=[t_s])
            OP(s, "pool", "memset", mm[:], 0.0, w=[t_s])
            mo = mm[:, 2, :].rearrange("p (h c) -> p h c", c=8)
            OP(s, "pool", "memset", mo[:, :, 2:6], 1.0, w=[t_s])
            for h in range(8):
                m = float(mults[h])
                OP(s, "pool", "memset", mm[:, 0, h * 8 + 0:h * 8 + 1], -m, w=[t_s])
                OP(s, "pool", "memset", mm[:, 0, h * 8 + 6:h * 8 + 7], m, w=[t_s])
                OP(s, "pool", "memset", mm[:, 1, h * 8 + 1:h * 8 + 2], -m, w=[t_s])
                OP(s, "pool", "memset", mm[:, 1, h * 8 + 7:h * 8 + 8], m, w=[t_s])
            for j in range(S // 512):
                p_, tp_ = ps[cnt % 2], t_ps[cnt % 2]
                cnt += 1
                cs = slice(j * 512, (j + 1) * 512)
                OP(s, "pe", "matmul", p_[:], mm[:, 0, :], bA[:, cs], start=True, stop=False, r=[t_s], w=[tp_])
                OP(s, "pe", "matmul", p_[:], mm[:, 1, :], bB[:, cs], start=False, stop=False, r=[t_s, t_b], w=[tp_])
                OP(s, "pe", "matmul", p_[:], mm[:, 2, :], bO[:, cs], start=False, stop=True, r=[t_s, t_b], w=[tp_])
                OP(s, "dve" if j % 2 else "act", "tensor_copy" if j % 2 else "copy", outb[:, cs], p_[:],
                   r=[tp_], w=[t_o])
            DMA(s, "sp", aux_dram.rearrange("h q r s -> (h q r) s"), outb[:], r=[t_o], w=[t_aq], key=t_o)
        s.barrier()
        s.emit()
    s.release_phase()


def phase_diffattn(pg, L, qT_dram, kT_dram, v_dram, aux, lam_aps, subln_g, lambda_init, o_dram):
    s = pg.s
    tag = "a%d" % L
    H = DA_H
    with ExitStack() as st:
        lamt = pg.sb(st, "lamt" + tag, [1, 4, 64], F32)
        lams = pg.sb(st, "lams" + tag, [1, 8], F32)
        junk = pg.sb(st, "lamj" + tag, [1, 64], F32)
        neglam = pg.sb(st, "neglam" + tag, [P, 1], F32)
        t_lam = s.tile("t_lam")
        t_nl = s.tile("t_nl")
        for i, ap in enumerate(lam_aps):
            s.dma("sp", lambda e, i=i, ap=ap: e.dma_start(out=lamt[:, i, :], in_=ap.rearrange("(o n) -> o n", o=1)),
                  w=[t_lam], key=t_lam)
        s.op("dve", lambda e: e.scalar_tensor_tensor(junk[:], lamt[:, 0, :], 1.0, lamt[:, 1, :], ALU.mult, ALU.mult,
                                                     accum_out=lams[:, 0:1]), r=[t_lam], w=[t_nl])
        s.op("dve", lambda e: e.scalar_tensor_tensor(junk[:], lamt[:, 2, :], 1.0, lamt[:, 3, :], ALU.mult, ALU.mult,
                                                     accum_out=lams[:, 1:2]), r=[t_lam], w=[t_nl])
        s.op("act", lambda e: e.activation(lams[:, 2:4], lams[:, 0:2], AF.Exp), r=[t_nl], w=[t_nl])
        s.op("dve", lambda e: e.scalar_tensor_tensor(lams[:, 4:5], lams[:, 3:4], float(-lambda_init), lams[:, 2:3],
                                                     ALU.add, ALU.subtract), r=[t_nl], w=[t_nl])
        t_lamd = dram_tile(pg, "lam_d")
        lam_d = pg.lam_dram
        s.dma("sp", lambda e: e.dma_start(out=lam_d[L:L + 1, :], in_=lams[:, 4:5]), r=[t_nl], w=[t_lamd], key=t_nl)
        s.dma("sp", lambda e: e.dma_start(out=neglam[:], in_=lam_d[L, :].partition_broadcast(P)),
              r=[t_lamd], w=[t_nl], key=t_nl)
        g_bc = pg.sb(st, "sg" + tag, [P, P], F32)
        t_g = s.tile("t_sg")
        s.dma("sp", lambda e: e.dma_start(out=g_bc[:], in_=subln_g.partition_broadcast(P)), w=[t_g], key=t_g)
        s.op("pool", lambda e: e.tensor_scalar(g_bc[:], g_bc[:], float(1.0 - lambda_init), None, ALU.mult),
             r=[t_g], w=[t_g])
        qa = [[pg.sb(st, "qa%s%d%d" % (tag, i, m), [P, S], BF16) for m in range(2)] for i in range(2)]
        ka = [[pg.sb(st, "ka%s%d%d" % (tag, i, m), [P, S], BF16) for m in range(2)] for i in range(2)]
        vaug = [pg.sb(st, "va%s%d" % (tag, i), [P, S // P, 129], BF16) for i in range(2)]
        t_qkv = s.tiles("t_qkv", 2)
        for i in range(2):
            for m in range(2):
                s.op("pool", lambda e, i=i, m=m: e.memset(qa[i][m][64:128, :], 0.0), w=[t_qkv[i]])
                s.op("pool", lambda e, i=i, m=m: e.memset(ka[i][m][64:128, :], 0.0), w=[t_qkv[i]])
            s.op("pool", lambda e, i=i: e.memset(vaug[i][:, :, 128:129], 1.0), w=[t_qkv[i]])
        t_q = dram_tile(pg, "qT")
        t_aux = dram_tile(pg, "auxda")
        v_v = v_dram.rearrange("(n p) d -> p n d", p=P)

        def load_head(h):
            i = h % 2
            for m in range(2):
                r0 = h * 128 + m * 64
                s.dma("sp", lambda e, m=m, r0=r0: e.dma_start(out=qa[i][m][0:64, :], in_=qT_dram[r0:r0 + 64, :]),
                      r=[t_q], w=[t_qkv[i]], key=t_qkv[i])
                s.dma("sp", lambda e, m=m, r0=r0: e.dma_start(out=ka[i][m][0:64, :], in_=kT_dram[r0:r0 + 64, :]),
                      r=[t_q], w=[t_qkv[i]], key=t_qkv[i])
                s.dma("sp", lambda e, m=m: e.dma_start(out=qa[i][m][64:68, :], in_=aux[h, 0, :, :]),
                      r=[t_aux], w=[t_qkv[i]], key=t_qkv[i])
                s.dma("sp", lambda e, m=m: e.dma_start(out=ka[i][m][64:68, :], in_=aux[h, 1, :, :]),
                      r=[t_aux], w=[t_qkv[i]], key=t_qkv[i])
            s.dma("sp", lambda e: e.dma_start(out=vaug[i][:, :, 0:128], in_=v_v[:, :, h * 128:(h + 1) * 128]),
                  r=[t_q], w=[t_qkv[i]], key=t_qkv[i])

        oacc = [pg.sb(st, "oacc%s%d" % (tag, i), [P, 2, 4, 129], F32) for i in range(2)]
        t_oacc = s.tiles("t_oacc", 2)
        sm = [pg.sb(st, "sm%s%d" % (tag, i), [P, 32], F32) for i in range(2)]
        t_sm = s.tiles("t_sm", 2)
        otmp = [pg.sb(st, "otmp%s%d" % (tag, i), [P, 2, 4, P], F32) for i in range(2)]
        t_otmp = s.tiles("t_otmp", 2)
        ost = [pg.sb(st, "ost%s%d" % (tag, i), [P, 4, P], BF16) for i in range(2)]
        t_ost = s.tiles("t_ost", 2)
        t_o = dram_tile(pg, "o_tm")
        o_v = o_dram.rearrange("(n p) d -> p n d", p=P)
        NTL = S // 512
        pending = []
        AX = mybir.AxisListType

        def post(h, t, ob):
            oa, sm_, ot = oacc[ob], sm[ob], otmp[ob]
            toa, tsm, tot = t_oacc[ob], t_sm[ob], t_otmp[ob]
            bc = lambda ap: ap.unsqueeze(2).to_broadcast([P, 4, P])
            OP(s, "dve", "reciprocal", sm_[:, 0:8].rearrange("p (a b) -> p a b", a=2), oa[:, :, :, 128],
               r=[toa], w=[tsm])
            OP(s, "dve", "tensor_scalar", sm_[:, 8:12], sm_[:, 4:8], neglam[:, 0:1], None, ALU.mult,
               r=[tsm, t_nl], w=[tsm])
            OP(s, "dve", "tensor_tensor", ot[:, 0], oa[:, 0, :, 0:128], bc(sm_[:, 0:4]), ALU.mult,
               r=[toa, tsm], w=[tot])
            OP(s, "dve", "tensor_tensor", ot[:, 1], oa[:, 1, :, 0:128], bc(sm_[:, 8:12]), ALU.mult,
               r=[toa, tsm], w=[tot])
            OP(s, "dve", "tensor_tensor", ot[:, 0], ot[:, 0], ot[:, 1], ALU.add, r=[tot], w=[tot])
            OP(s, "dve", "tensor_tensor", ot[:, 1], ot[:, 0], ot[:, 0], ALU.mult, r=[tot], w=[tot])
            OP(s, "dve", "tensor_reduce", sm_[:, 12:16], ot[:, 1], AX.X, ALU.add, r=[tot], w=[tsm])
            OP(s, "dve", "tensor_scalar", sm_[:, 16:20], sm_[:, 12:16], 1.0 / 128.0, float(RMS_EPS), ALU.mult,
               ALU.add, r=[tsm], w=[tsm])
            OP(s, "pool", "tensor_tensor", sm_[:, 20:24], sm_[:, 16:20], pg.neg_half[:, 0:4], ALU.pow,
               r=[tsm, pg.t_const], w=[tsm])
            OP(s, "dve", "tensor_tensor", ot[:, 0], ot[:, 0], bc(sm_[:, 20:24]), ALU.mult, r=[tot, tsm], w=[tot])
            OP(s, "dve", "tensor_tensor", ost[ob][:], ot[:, 0], g_bc[:].unsqueeze(1).to_broadcast([P, 4, P]),
               ALU.mult, r=[tot, t_g], w=[t_ost[ob]])
            DMA(s, "sp", o_v[:, t * 4:(t + 1) * 4, h * 128:(h + 1) * 128], ost[ob][:], r=[t_ost[ob]], w=[t_o],
                key=t_ost[ob])

        def make_end(h, t, m):
            ob = (h * NTL + t) % 2

            def end(acc, t_acc):
                for qb in range(4):
                    OP(s, "dve", "tensor_copy", oacc[ob][:, m, qb, :], acc[qb][:, 0:129], r=[t_acc[qb]],
                       w=[t_oacc[ob]])
                if m == 0:
                    while pending:
                        pending.pop(0)()
                else:
                    pending.append(lambda: post(h, t, ob))
            return end

        units = []
        head_first_unit = {}
        sl_h = slopes(H)
        for h in range(H):
            i = h % 2
            head_first_unit[h] = len(units)
            wkeep = 0
            while sl_h[h] * (128 * (wkeep + 1) - 127) < 88.0:
                wkeep += 1
            for t in range(NTL):
                for m in range(2):
                    q_, k_, v_ = qa[i][m], ka[i][m], vaug[i]
                    reads = [t_qkv[i]]
                    kb_first = max(0, 4 * t - wkeep)
                    nd = list(range(kb_first, 4 * t))
                    groups = []
                    if len(nd) % 2 == 1:
                        groups.append(nd[:1])
                        nd = nd[1:]
                    for x0 in range(0, len(nd), 2):
                        groups.append(nd[x0:x0 + 2])
                    first_kb = kb_first if 4 * t > kb_first else 4 * t
                    for grp in groups:
                        u = dict(reads=reads, mm=[], exp=[(0, len(grp), 0, 512)], pv=[])
                        for bi, kb in enumerate(grp):
                            u["mm"].append((bi, 0, 512, [(k_[:, kb * P:(kb + 1) * P], q_[:, t * 512:(t + 1) * 512])]))
                            for qb in range(4):
                                u["pv"].append((qb, bi, qb * P, v_[:, kb, :], kb == first_kb, False))
                        units.append(u)
                    tri2 = pg.tri_le3[:, 0, :].unsqueeze(1).to_broadcast([P, 2, P])
                    for j0 in (0, 2):
                        w0 = 512 - P * j0
                        u = dict(reads=reads, mm=[], exp=[(0, 2, 0, w0)], pv=[])
                        for bi in range(2):
                            j = j0 + bi
                            kb = 4 * t + j
                            u["mm"].append((bi, 0, 512 - P * j, [(k_[:, kb * P:(kb + 1) * P],
                                                                  q_[:, t * 512 + P * j:(t + 1) * 512]),
                                                                 (pg.ident[:], pg.tri_bf[:, 0, :], 0, P)]))
                            for qb in range(j, 4):
                                u["pv"].append((qb, bi, (qb - j) * P, v_[:, kb, :], kb == first_kb, qb == j))
                        if j0 == 2:
                            u["end"] = make_end(h, t, m)
                        units.append(u)
        load_head(0)
        bounds = [head_first_unit[h] for h in range(H)] + [len(units)]
        for h in range(H):
            if h + 1 < H:
                load_head(h + 1)
            attn_core_run(pg, st, tag, units[bounds[h]:bounds[h + 1]], h == 0)
        while pending:
            pending.pop(0)()
        s.barrier()
        s.emit()
    s.release_phase()


DIL_GROUPS = ((128, 1), (512, 4), (2048, 16))


def phase_dilattn(pg, L, g, dil, qT_dram, kT_dram, v_dram, nd_dram):
    s = pg.s
    tag = "d%d%d" % (L, g)
    H = 8
    Lg = S // dil
    nb = Lg // P
    NB = S // P
    sl8 = slopes(8)
    with ExitStack() as st:
        qa = [pg.sb(st, "qa%s%d" % (tag, i), [P, S], BF16) for i in range(2)]
        ka = [pg.sb(st, "ka%s%d" % (tag, i), [P, S], BF16) for i in range(2)]
        vaug = [pg.sb(st, "va%s%d" % (tag, i), [P, NB, 129], BF16) for i in range(2)]
        t_qkv = s.tiles("t_qkv", 2)
        for i in range(2):
            s.op("pool", lambda e, i=i: e.memset(vaug[i][:, :, 128:129], 1.0), w=[t_qkv[i]])
        idist = pg.sb(st, "idist" + tag, [P, 256], I32)
        dist = pg.sb(st, "dist" + tag, [P, 256], F32)
        mbase = pg.sb(st, "mbase" + tag, [P, 2, P], F32)
        t_dm = s.tile("t_dm")
        OP(s, "pool", "iota", idist[:], [[1, 256]], base=0, channel_multiplier=-1, w=[t_dm])
        OP(s, "pool", "tensor_copy", dist[:], idist[:], r=[t_dm], w=[t_dm])
        OP(s, "pool", "tensor_copy", mbase[:, 0:1, :], pg.tri_le3[:], r=[pg.t_const], w=[t_dm])
        OP(s, "pool", "tensor_copy", mbase[:, 1:2, :], pg.tri_ge3[:], r=[pg.t_const], w=[t_dm])
        tbl = [pg.sb(st, "tbl%s%d" % (tag, i), [P, 2, 512], F32) for i in range(2)]
        t_tbl = s.tiles("t_tbl", 2)
        t_q = dram_tile(pg, "qT")
        v_v = v_dram.rearrange("(n p) d -> p n d", p=P)

        def load_head(h):
            i = h % 2
            DMA(s, "sp", qa[i][:], qT_dram[h * P:(h + 1) * P, :], r=[t_q], w=[t_qkv[i]], key=t_qkv[i])
            DMA(s, "sp", ka[i][:], kT_dram[h * P:(h + 1) * P, :], r=[t_q], w=[t_qkv[i]], key=t_qkv[i])
            DMA(s, "sp", vaug[i][:, :, 0:128], v_v[:, :, h * P:(h + 1) * P], r=[t_q], w=[t_qkv[i]], key=t_qkv[i])
            m = -float(sl8[h] * dil)
            OP(s, "dve", "scalar_tensor_tensor", tbl[i][:].rearrange("p a (b c) -> p (a b) c", c=256),
               dist[:].unsqueeze(1).to_broadcast([P, 4, 256]), m,
               mbase[:].rearrange("p a b -> p (a b)").unsqueeze(1).to_broadcast([P, 4, 256]), ALU.mult, ALU.add,
               r=[t_dm], w=[t_tbl[i]])

        ndh = [pg.sb(st, "ndh%s%d" % (tag, i), [P, NB, 129], F32) for i in range(2)]
        t_ndh = s.tiles("t_ndh", 2)
        t_nd = dram_tile(pg, "nd")
        nd_v = nd_dram.rearrange("(n p r) c -> p r n c", p=P, r=dil)

        def make_evac(h, beta):
            a = beta % 4

            def evac(acc, t_acc):
                k = h % 2
                OP(s, "dve", "tensor_copy", ndh[k][:, beta, :], acc[a][:, 0:129], r=[t_acc[a]], w=[t_ndh[k]])
                if beta == NB - 1:
                    src_ap = ndh[k][:].rearrange("p (r n) c -> p r n c", r=dil)
                    dst_ap = nd_v[:, :, :, h * 129:(h + 1) * 129]
                    DMA(s, "sp", dst_ap, src_ap, r=[t_ndh[k]], w=[t_nd], key=t_ndh[k])
            return evac

        load_head(0)
        for h in range(H):
            if h + 1 < H:
                load_head(h + 1)
            i = h % 2
            q_, k_, v_ = qa[i], ka[i], vaug[i]
            units = []
            for b0 in range(0, NB, 4):
                u = dict(reads=[t_qkv[i]], mm=[], mask=[(0, 2, 0, 512, tbl[i][:], t_tbl[i])],
                         exp=[(0, 2, 0, 512)], pv=[])
                for sl in range(4):
                    beta = b0 + sl
                    r_, n_ = divmod(beta, nb)
                    has_next = (n_ + 1 < nb)
                    width = 256 if has_next else 128
                    bank, off = sl // 2, (sl % 2) * 256
                    u["mm"].append((bank, off, width, [
                        (k_[:, beta * P:(beta + 1) * P], q_[:, beta * P:beta * P + width])]))
                    u["pv"].append((beta % 4, bank, off, v_[:, beta, :], n_ == 0, True, make_evac(h, beta)))
                    if has_next:
                        u["pv"].append(((beta + 1) % 4, bank, off + P, v_[:, beta, :], True, False, None))
                units.append(u)
            attn_core_run(pg, st, tag, units, h == 0)
        s.barrier()
        s.emit()
    s.release_phase()


def phase_dilcombine(pg, L, nd_drams, o_dram):
    s = pg.s
    tag = "c%d" % L
    with ExitStack() as st:
        nd = [[pg.sb(st, "nd%s%d%d" % (tag, i, g), [P, 8, 129], F32) for g in range(3)] for i in range(2)]
        t_ndb = [[s.tile("t_ndb") for g in range(3)] for i in range(2)]
        rec = [pg.sb(st, "rec%s%d" % (tag, i), [P, 8], F32) for i in range(2)]
        ob = [pg.sb(st, "ob%s%d" % (tag, i), [P, D], BF16) for i in range(2)]
        t_ob = s.tiles("t_ob", 2)
        t_nd = dram_tile(pg, "nd")
        t_o = dram_tile(pg, "o_tm")
        nblk = S // P

        def load(b):
            for g in range(3):
                s.dma("sp", lambda e, g=g: e.dma_start(
                    out=nd[b % 2][g][:], in_=nd_drams[g][b * P:(b + 1) * P, :].rearrange("p (h c) -> p h c", h=8)),
                    r=[t_nd], w=[t_ndb[b % 2][g]], key=t_ndb[b % 2][g])
        load(0)
        for b in range(nblk):
            if b + 1 < nblk:
                load(b + 1)
            n0, n1, n2 = nd[b % 2]
            t0, t1, t2 = t_ndb[b % 2]
            s.op("pool", lambda e, n0=n0, n1=n1: e.tensor_tensor(n0[:], n0[:], n1[:], ALU.add), r=[t0, t1], w=[t0])
            s.op("pool", lambda e, n0=n0, n2=n2: e.tensor_tensor(n0[:], n0[:], n2[:], ALU.add), r=[t0, t2], w=[t0])
            rc = rec[b % 2]
            s.op("dve", lambda e, n0=n0, rc=rc: e.reciprocal(rc[:], n0[:, :, 128]), r=[t0], w=[t0])
            for h in range(8):
                eng = "dve" if h % 2 == 0 else "act"
                if eng == "dve":
                    s.op("dve", lambda e, h=h, n0=n0, rc=rc, b=b: e.tensor_scalar(
                        ob[b % 2][:, h * P:(h + 1) * P], n0[:, h, 0:128], rc[:, h:h + 1], None, ALU.mult),
                        r=[t0], w=[t_ob[b % 2]])
                else:
                    s.op("act", lambda e, h=h, n0=n0, rc=rc, b=b: e.activation(
                        ob[b % 2][:, h * P:(h + 1) * P], n0[:, h, 0:128], AF.Copy, scale=rc[:, h:h + 1]),
                        r=[t0], w=[t_ob[b % 2]])
            s.dma("sp", lambda e, b=b: e.dma_start(out=o_dram[b * P:(b + 1) * P, :], in_=ob[b % 2][:]),
                  r=[t_ob[b % 2]], w=[t_o], key=t_ob[b % 2])
        s.barrier()
        s.emit()
    s.release_phase()


def OP(s, eng, name, *args, r=(), w=(), **kw):
    return s.op(eng, lambda e: getattr(e, name)(*args, **kw), r=list(r), w=list(w))


def DMA(s, q, out, in_, r=(), w=(), key=None):
    return s.dma(q, lambda e: e.dma_start(out=out, in_=in_), r=list(r), w=list(w), key=key)


def phase_gla(pg, L, gqT, gkT, gk_tm, v_dram, gr_tm, glT_dram, w_gate2, b_gate, gnorm_g, o_dram):
    s = pg.s
    tag = "g%d" % L
    HG, DK, DV = 4, 128, 256
    NCH = S // P
    I16 = 1.0 / 16.0
    with ExitStack() as st:
        sb = lambda name, shape, dt: pg.sb(st, name + tag, shape, dt)
        tri_incl = sb("tri_incl", [P, 4, P], F32)
        sgt = sb("sgt", [P, P], F32)
        ones_row = sb("ones_row", [1, P], F32)
        bg = sb("bg", [1, 512], F32)
        wg2 = sb("wg2", [16, 512], F32)
        glT = sb("glT", [16, S], F32)
        gn_bc = sb("gn_bc", [P, DV], F32)
        t_c = s.tile("t_glac")
        OP(s, "pool", "memset", tri_incl[:], 1.0, w=[t_c])
        OP(s, "pool", "affine_select", tri_incl[:], tri_incl[:], [[0, 4], [1, P]], ALU.is_ge, 0.0,
           base=0, channel_multiplier=-1, r=[t_c], w=[t_c])
        OP(s, "pool", "memset", sgt[:], 1.0, w=[t_c])
        OP(s, "pool", "affine_select", sgt[:], sgt[:], [[-1, P]], ALU.is_gt, 0.0,
           base=0, channel_multiplier=1, r=[t_c], w=[t_c])
        OP(s, "pool", "memset", ones_row[:], 1.0, w=[t_c])
        t_ld = s.tile("t_glald")
        DMA(s, "sp", bg[:], b_gate.rearrange("(o n) -> o n", o=1), w=[t_ld], key=t_ld)
        DMA(s, "sp", wg2[:], w_gate2, w=[t_ld], key=t_ld)
        DMA(s, "sp", glT[:], glT_dram, r=[dram_tile(pg, "qT")], w=[t_ld], key=t_ld)
        DMA(s, "sp", gn_bc[:], gnorm_g.partition_broadcast(P), w=[t_ld], key=t_ld)
        def dbl(name, shape, dt):
            return [sb("%s%d" % (name, i), shape, dt) for i in range(2)], s.tiles("t_" + name, 2)
        qT, t_qT = dbl("qT", [P, 4, P], F32)
        kT, t_kT = dbl("kT", [P, 4, P], F32)
        ktm, t_ktm = dbl("ktm", [P, 512], F32)
        vv, t_vv = dbl("vv", [P, D], BF16)
        rr, t_rr = dbl("rr", [P, D], F32)
        e1, t_e1 = dbl("e1", [P, 512], F32)
        eq, t_eq = dbl("eq", [P, 4, P], F32)
        ek, t_ek = dbl("ek", [P, 4, P], F32)
        es, t_es = dbl("es", [P, 512], F32)
        qd, t_qd = dbl("qd", [P, 4, P], BF16)
        ki, t_ki = dbl("ki", [P, 4, P], BF16)
        kst, t_kst = dbl("kst", [P, 512], BF16)
        sT, t_sT = dbl("sT", [P, 4, P], BF16)
        sg, t_sg = dbl("sg", [P, D], F32)
        ot, t_ot = dbl("ot", [P, D], F32)
        ost, t_ost = dbl("ost", [P, D], BF16)
        sm, t_sm = dbl("sm", [P, 16], F32)
        junk, t_junk = dbl("junk", [P, DV], F32)
        state = sb("state", [P, 4, DV], F32)
        state_bf = sb("state_bf", [P, 4, DV], BF16)
        t_state = s.tiles("t_state", 4)
        t_sbf = s.tiles("t_sbf", 4)
        ps_z = pg.ps(st, "psz" + tag, [P, 512], F32)
        ps_c = pg.ps(st, "psc" + tag, [P, 4, P], F32)
        ps_r = ps_z
        ps_s = ps_c
        ps_o2 = [pg.ps(st, "pso%s%d" % (tag, i), [P, 4, DV], F32) for i in range(2)]
        ps_kv = pg.ps(st, "pskv" + tag, [P, 4, DV], F32)
        t_psz, t_psc = s.tile("t_psz"), s.tile("t_psc")
        t_psr, t_pss = t_psz, t_psc
        t_pso2 = [s.tiles("t_pso", 2) for i in range(2)]
        t_pskv = [t for t in s.tiles("t_pskv", 2) for _ in range(2)]
        t_q = dram_tile(pg, "qT")
        t_o = dram_tile(pg, "o_tm")
        gq_v = gqT.rearrange("(h p) t -> p h t", p=P)
        gk_v = gkT.rearrange("(h p) t -> p h t", p=P)

        def load(c):
            i = c % 2
            cs = slice(c * P, (c + 1) * P)
            DMA(s, "sp", qT[i][:], gq_v[:, :, cs], r=[t_q], w=[t_qT[i]], key=t_qT[i])
            DMA(s, "sp", kT[i][:], gk_v[:, :, cs], r=[t_q], w=[t_kT[i]], key=t_kT[i])
            DMA(s, "sp", ktm[i][:], gk_tm[cs, :], r=[t_q], w=[t_ktm[i]], key=t_ktm[i])
            DMA(s, "sp", vv[i][:], v_dram[cs, :], r=[t_q], w=[t_vv[i]], key=t_vv[i])
            DMA(s, "sp", rr[i][:], gr_tm[cs, :], r=[t_q], w=[t_rr[i]], key=t_rr[i])

        def prep(c):
            i = c % 2
            cs = slice(c * P, (c + 1) * P)
            OP(s, "pe", "matmul", ps_z[:], glT[:, cs], wg2[:], start=True, stop=False, r=[t_ld], w=[t_psz])
            OP(s, "pe", "matmul", ps_z[:], ones_row[:], bg[:], start=False, stop=True, r=[t_ld, t_c], w=[t_psz])
            OP(s, "act", "activation", e1[i][:], ps_z[:], AF.Exp, scale=-1.0, r=[t_psz], w=[t_e1[i]])
            OP(s, "act", "activation", e1[i][:], e1[i][:], AF.Ln, bias=pg.one_col[:, 0:1], scale=1.0,
               r=[t_e1[i], pg.t_const], w=[t_e1[i]])
            for h in range(HG):
                OP(s, "pe", "matmul", ps_c[:, h, :], e1[i][:, h * P:(h + 1) * P], tri_incl[:, 0, :],
                   start=True, stop=True, r=[t_e1[i], t_c], w=[t_psc])
            OP(s, "pe", "matmul", ps_r[:], sgt[:], e1[i][:], start=True, stop=True, r=[t_e1[i], t_c], w=[t_psr])
            OP(s, "act", "activation", eq[i][:], ps_c[:], AF.Exp, scale=-I16, r=[t_psc], w=[t_eq[i]])
            OP(s, "act", "activation", ek[i][:], ps_c[:], AF.Exp, scale=I16, r=[t_psc], w=[t_ek[i]])
            OP(s, "act", "activation", es[i][:], ps_r[:], AF.Exp, scale=-I16, r=[t_psr], w=[t_es[i]])
            OP(s, "dve", "tensor_tensor", qd[i][:], qT[i][:], eq[i][:], ALU.mult, r=[t_qT[i], t_eq[i]], w=[t_qd[i]])
            OP(s, "pool", "tensor_tensor", ki[i][:], kT[i][:], ek[i][:], ALU.mult, r=[t_kT[i], t_ek[i]], w=[t_ki[i]])
            OP(s, "pool", "tensor_tensor", kst[i][:], ktm[i][:], es[i][:], ALU.mult, r=[t_ktm[i], t_es[i]],
               w=[t_kst[i]])
            for h in range(HG):
                OP(s, "pe", "matmul", ps_s[:, h, :], ki[i][:, h, :], qd[i][:, h, :], start=True, stop=True,
                   r=[t_ki[i], t_qd[i]], w=[t_pss])
            OP(s, "dve", "tensor_tensor", sT[i][:], ps_s[:], tri_incl[:], ALU.mult, r=[t_pss, t_c], w=[t_sT[i]])
            OP(s, "act", "activation", sg[i][:], rr[i][:], AF.Exp, scale=-1.0, r=[t_rr[i]], w=[t_sg[i]])
            OP(s, "act", "activation", sg[i][:], sg[i][:], AF.Ln, bias=pg.one_col[:, 0:1], scale=1.0,
               r=[t_sg[i], pg.t_const], w=[t_sg[i]])
            OP(s, "act", "activation", sg[i][:], sg[i][:], AF.Exp, scale=-1.0, r=[t_sg[i]], w=[t_sg[i]])
            OP(s, "pool", "tensor_tensor", sg[i][:], sg[i][:], rr[i][:], ALU.mult, r=[t_sg[i], t_rr[i]], w=[t_sg[i]])

        def recur(c):
            i = c % 2
            ps_o = ps_o2[c % 2]
            t_pso = t_pso2[c % 2]
            for h in range(HG):
                tpo = t_pso[h // 2]
                vs = vv[i][:, h * DV:(h + 1) * DV]
                if c > 0:
                    OP(s, "pe", "matmul", ps_o[:, h, :], qd[i][:, h, :], state_bf[:, h, :], start=True, stop=False,
                       r=[t_qd[i], t_sbf[h]], w=[tpo])
                OP(s, "pe", "matmul", ps_o[:, h, :], sT[i][:, h, :], vs, start=(c == 0), stop=True,
                   r=[t_sT[i], t_vv[i]], w=[tpo])
            for h in range(HG):
                vs = vv[i][:, h * DV:(h + 1) * DV]
                OP(s, "pe", "matmul", ps_kv[:, h, :], kst[i][:, h * P:(h + 1) * P], vs, start=True, stop=True,
                   r=[t_kst[i], t_vv[i]], w=[t_pskv[h]])
                if c == 0:
                    OP(s, "dve", "tensor_copy", state[:, h, :], ps_kv[:, h, :], r=[t_pskv[h]], w=[t_state[h]])
                else:
                    OP(s, "dve", "scalar_tensor_tensor", state[:, h, :], state[:, h, :], eq[i][:, h, P - 1:P],
                       ps_kv[:, h, :], ALU.mult, ALU.add, r=[t_pskv[h], t_eq[i], t_state[h]], w=[t_state[h]])
                if c + 1 < NCH:
                    OP(s, "act", "copy", state_bf[:, h, :], state[:, h, :], r=[t_state[h]], w=[t_sbf[h]])
            for h in range(HG):
                OP(s, "act", "activation", junk[i][:], ps_o[:, h, :], AF.Square, accum_out=sm[i][:, h:h + 1],
                   r=[t_pso[h // 2]], w=[t_junk[i], t_sm[i]])
            OP(s, "dve", "tensor_scalar", sm[i][:, 4:8], sm[i][:, 0:4], 1.0 / DV, float(RMS_EPS), ALU.mult, ALU.add,
               r=[t_sm[i]], w=[t_sm[i]])
            OP(s, "pool", "tensor_tensor", sm[i][:, 8:12], sm[i][:, 4:8], pg.neg_half[:, 0:4], ALU.pow,
               r=[t_sm[i], pg.t_const], w=[t_sm[i]])
            for h in range(HG):
                OP(s, "dve", "scalar_tensor_tensor", ot[i][:, h * DV:(h + 1) * DV], ps_o[:, h, :],
                   sm[i][:, 8 + h:9 + h], gn_bc[:], ALU.mult, ALU.mult, r=[t_pso[h // 2], t_sm[i], t_ld],
                   w=[t_ot[i]])
            OP(s, "pool", "tensor_tensor", ost[i][:], ot[i][:], sg[i][:], ALU.mult, r=[t_ot[i], t_sg[i]],
               w=[t_ost[i]])
            DMA(s, "sp", o_dram[c * P:(c + 1) * P, :], ost[i][:], r=[t_ost[i]], w=[t_o], key=t_ost[i])

        load(0)
        prep(0)
        for c in range(NCH):
            if c + 1 < NCH:
                load(c + 1)
                prep(c + 1)
            recur(c)
        s.barrier()
        s.emit()
    s.release_phase()


_CORE_STATE = {}


def attn_core_run(pg, st, tag, units, first):
    s = pg.s
    if first:
        cs = {}
        cs["ps_s"] = [pg.ps(st, "pss%s%d" % (tag, i), [P, 2, 512], F32) for i in range(2)]
        cs["t_pss"] = s.tiles("t_pss", 2)
        cs["pT"] = [pg.sb(st, "pT%s%d" % (tag, i), [P, 2, 512], BF16) for i in range(3)]
        cs["t_pT"] = s.tiles("t_pT", 3)
        cs["acc"] = [pg.ps(st, "acc%s%d" % (tag, i), [P, 512], F32) for i in range(4)]
        cs["t_acc"] = s.tiles("t_acc", 4)
        cs["cnt"] = 0
        for i in range(2):
            s.op("dve", lambda e, i=i: e.memset(cs["ps_s"][i][:], 0.0), w=[cs["t_pss"][i]])
        _CORE_STATE[tag] = cs
    cs = _CORE_STATE[tag]
    ps_s, t_pss, pT, t_pT, acc, t_acc = cs["ps_s"], cs["t_pss"], cs["pT"], cs["t_pT"], cs["acc"], cs["t_acc"]

    def s_stage(u, i):
        ps, tps = ps_s[i % 2], t_pss[i % 2]
        p_, tp_ = pT[i % 3], t_pT[i % 3]
        for (bank, off, width, pairs) in u["mm"]:
            for pi, pr in enumerate(pairs):
                lhsT, rhs = pr[0], pr[1]
                o0, wd = (off + pr[2], pr[3]) if len(pr) > 2 else (off, width)
                s.op("pe", lambda e, bank=bank, o0=o0, wd=wd, lhsT=lhsT, rhs=rhs, ps=ps, pi=pi,
                     np_=len(pairs): e.matmul(ps[:, bank, o0:o0 + wd], lhsT, rhs, start=(pi == 0),
                                              stop=(pi == np_ - 1)), r=u["reads"] + [pg.t_const], w=[tps])
        for (bank0, nb, off, width, tab, ttab) in u.get("mask", []):
            s.op("dve", lambda e, bank0=bank0, nb=nb, off=off, width=width, tab=tab, ps=ps: e.tensor_tensor(
                ps[:, bank0:bank0 + nb, off:off + width], ps[:, bank0:bank0 + nb, off:off + width], tab, ALU.add),
                r=[tps, ttab], w=[tps])
        for (bank0, nb, off, width) in u["exp"]:
            s.op("act", lambda e, bank0=bank0, nb=nb, off=off, width=width, ps=ps, p_=p_: e.activation(
                p_[:, bank0:bank0 + nb, off:off + width], ps[:, bank0:bank0 + nb, off:off + width], AF.Exp),
                r=[tps], w=[tp_])

    def pv_stage(u, i):
        p_, tp_ = pT[i % 3], t_pT[i % 3]
        for ent in u["pv"]:
            (a, bank, off, vap, start, stop) = ent[:6]
            s.op("pe", lambda e, a=a, bank=bank, off=off, vap=vap, start=start, stop=stop, p_=p_: e.matmul(
                acc[a][:, 0:129], p_[:, bank, off:off + P], vap, start=start, stop=stop),
                r=[tp_] + u["reads"], w=[t_acc[a]])
            if len(ent) > 6 and ent[6] is not None:
                ent[6](acc, t_acc)
        if u.get("end") is not None:
            u["end"](acc, t_acc)

    n = len(units)
    base = cs["cnt"]
    for i in range(min(2, n)):
        s_stage(units[i], base + i)
    for i in range(n):
        if i + 2 < n:
            s_stage(units[i + 2], base + i + 2)
        pv_stage(units[i], base + i)
    cs["cnt"] = base + n


import math


def diff_lambda_init(layer_idx):
    return 0.8 - 0.6 * math.exp(-0.3 * layer_idx)


DIFF_IN = ["w_in", "lam_q1", "lam_k1", "lam_q2", "lam_k2", "subln_g", "w_out"]
DIFF_SHAPES = {"w_in": [D, 3072], "lam_q1": [64], "lam_k1": [64], "lam_q2": [64], "lam_k2": [64],
               "subln_g": [128], "w_out": [D, D]}
DIL_SHAPES = {"w_in": [D, 9216], "w_out": [D, D]}
GLA_SHAPES = {"w_in": [D, 3088], "w_gate2": [16, 512], "b_gate": [512], "gnorm_g": [256], "w_out": [D, D]}
FFN_SHAPES = {"ln1_g": [D], "ln1_b": [D], "w_ff1": [D, DFF], "w_ff2": [DFF, D], "ln2_g": [D], "ln2_b": [D]}


def layer_shapes(L):
    kind = L % 3
    d = dict([DIFF_SHAPES, DIL_SHAPES, GLA_SHAPES][kind])
    d.update(FFN_SHAPES)
    return d


def build_program(mode="full"):
    nc = bass.Bass("TRN2", target_bir_lowering=False)
    ins = {}

    def din(name, shape):
        ins[name] = nc.dram_tensor(name, list(shape), F32, kind="ExternalInput").ap()
        return ins[name]

    if mode == "full":
        layers = list(range(DEPTH))
    else:
        layers = [int(mode[-1])]
    x = din("x", [S, D])
    for L in layers:
        for k, shp in layer_shapes(L).items():
            din("l%d_%s" % (L, k), shp)
    out = nc.dram_tensor("out", [S, D], F32, kind="ExternalOutput").ap()

    def scratch(name, shape, dt):
        return nc.dram_tensor(name, list(shape), dt, kind="Internal").ap()

    xTa = scratch("xTa", [D, S], BF16)
    xa = scratch("xa", [S, D], F32)
    xb = scratch("xb", [S, D], F32)
    qT = scratch("qT", [D, S], BF16)
    kT = scratch("kT", [D, S], BF16)
    v_tm = scratch("v_tm", [S, D], BF16)
    o_tm = scratch("o_tm", [S, D], BF16)
    aux_da = scratch("aux_da", [8, 2, 4, S], BF16)

    with ExitStack() as stack:
        sched = Sched(nc, stack)
        pg = Prog(nc, sched, stack)
        pg.lam_dram = scratch("lam_d", [DEPTH, 1], F32)
        pg.eps_ln = pg.sb(stack, "eps_ln", [P, 1], F32)
        setup_consts(pg)
        sched.op("pool", lambda e: e.memset(pg.eps_ln[:], LN_EPS), w=[pg.t_const])
        pg.dram["xT"] = [Tile("xT%d" % i) for i in range(NT)]
        pg.dram["xo"] = [Tile("xo%d" % i) for i in range(NT)]
        pg.dram["xi"] = [Tile("xi%d" % i) for i in range(NT)]
        phase_pre(pg, x, xTa)
        if mode.startswith("ffn"):
            L = layers[0]
            p = "l%d_" % L
            phase_ffn(pg, L, x, xTa, ins[p + "w_ff1"], ins[p + "w_ff2"], ins[p + "ln2_g"], ins[p + "ln2_b"],
                      out, xTa)
            return nc, list(ins.keys())
        specs = []
        if any(L % 3 == 0 for L in layers):
            specs.append((aux_da, slopes(DA_H), S))
        if specs:
            build_aux(pg, specs)
        x_cur = x
        for li, L in enumerate(layers):
            p = "l%d_" % L
            kind = L % 3
            last = (li == len(layers) - 1)
            if kind == 0:
                jobs = [dict(kind="fm", name="qT", c0=0, n=1024, dst=qT, dt=BF16, scale=0.125),
                        dict(kind="fm", name="qT", c0=1024, n=1024, dst=kT, dt=BF16),
                        dict(kind="tm", name="qT", c0=2048, n=1024, dst=v_tm, dt=BF16)]
                phase_proj(pg, "p%d" % L, xTa, ins[p + "w_in"], 3072, jobs)
                phase_diffattn(pg, L, qT, kT, v_tm, aux_da,
                               [ins[p + "lam_q1"], ins[p + "lam_k1"], ins[p + "lam_q2"], ins[p + "lam_k2"]],
                               ins[p + "subln_g"], diff_lambda_init(L), o_tm)
            elif kind == 1:
                nds = []
                for g, (window, dil) in enumerate(DIL_GROUPS):
                    nd_g = scratch("nd%d" % g, [S, 8 * 129], F32)
                    nds.append(nd_g)
                    Lg = S // dil
                    blocks = [(r + dil * P * n, dil) for r in range(dil) for n in range(Lg // P)]
                    jobs = [dict(kind="fm", name="qT", c0=0, n=1024, dst=qT, dt=BF16, scale=128.0 ** -0.5, dil=dil),
                            dict(kind="fm", name="qT", c0=1024, n=1024, dst=kT, dt=BF16, dil=dil),
                            dict(kind="tm", name="qT", c0=2048, n=1024, dst=v_tm, dt=BF16, blocks=blocks)]
                    phase_proj(pg, "p%d%d" % (L, g), xTa, ins[p + "w_in"][:, g * 3072:(g + 1) * 3072], 3072, jobs)
                    phase_dilattn(pg, L, g, dil, qT, kT, v_tm, nd_g)
            else:
                gqT = scratch("gqT", [512, S], F32)
                gkT = scratch("gkT", [512, S], F32)
                gk_tm = scratch("gk_tm", [S, 512], F32)
                gr_tm = scratch("gr_tm", [S, D], F32)
                glT = scratch("glT", [16, S], F32)
                jobs = [dict(kind="fm", name="qT", c0=0, n=512, dst=gqT, dt=F32, scale=128.0 ** -0.5),
                        dict(kind="fm", name="qT", c0=512, n=512, dst=gkT, dt=F32),
                        dict(kind="fm", name="qT", c0=3072, n=16, dst=glT, dt=F32),
                        dict(kind="tm", name="qT", c0=512, n=512, dst=gk_tm, dt=F32),
                        dict(kind="tm", name="qT", c0=1024, n=1024, dst=v_tm, dt=BF16),
                        dict(kind="tm", name="qT", c0=2048, n=1024, dst=gr_tm, dt=F32)]
                phase_proj(pg, "p%d" % L, xTa, ins[p + "w_in"], 3088, jobs)
                phase_gla(pg, L, gqT, gkT, gk_tm, v_tm, gr_tm, glT, ins[p + "w_gate2"], ins[p + "b_gate"],
                          ins[p + "gnorm_g"], o_tm)
            mix_only = mode.startswith("mix")
            phase_outproj(pg, L, o_tm, ins[p + "w_out"], x_cur, ins[p + "ln1_g"], ins[p + "ln1_b"],
                          out if mix_only else xa, xTa, nd_drams=(nds if kind == 1 else None))
            if mix_only:
                break
            phase_ffn(pg, L, xa, xTa, ins[p + "w_ff1"], ins[p + "w_ff2"], ins[p + "ln2_g"], ins[p + "ln2_b"],
                      out if last else xb, xTa)
            x_cur = xb
    return nc, list(ins.keys())


_CACHE = {}
LAST_EXEC_NS = None


def kernel(**inputs):
    mode = inputs.pop("_mode", "full")
    if mode not in _CACHE:
        _CACHE[mode] = build_program(mode)
    nc, names = _CACHE[mode]
    x = np.ascontiguousarray(inputs["x"], dtype=np.float32)
    in_maps = []
    for c in range(NCORES):
        m = {}
        for n in names:
            if n == "x":
                m[n] = np.ascontiguousarray(x[c])
            else:
                m[n] = np.ascontiguousarray(inputs[n], dtype=np.float32)
        in_maps.append(m)
    import os
    tr = bool(os.environ.get("KTRACE"))
    res = run_bass_kernel_spmd(nc, in_maps, core_ids=list(range(NCORES)), trace=tr)
    global LAST_EXEC_NS
    LAST_EXEC_NS = res.exec_time_ns
    return np.stack([r["out"] for r in res.results], axis=0)
```

```python
import numpy as np
from contextlib import ExitStack

import concourse.bass as bass
import concourse.mybir as mybir
from concourse.bass_utils import run_bass_kernel_spmd

F32 = mybir.dt.float32
BF16 = mybir.dt.bfloat16
I32 = mybir.dt.int32
AF = mybir.ActivationFunctionType
ALU = mybir.AluOpType

D = 1024
S = 4096
DFF = 4096
DEPTH = 4
ALPHA = (2 * DEPTH) ** 0.25
LN_EPS = 1e-5
RMS_EPS = 1e-6
NCORES = 8
TT = 512
NT = S // TT
P = 128

ENGS = ("pe", "dve", "act", "pool", "sp")


class Tile:
    __slots__ = ("name", "writers", "readers", "sem")

    def __init__(self, name):
        self.name = name
        self.writers = []
        self.readers = []
        self.sem = None


class Op:
    __slots__ = ("eng", "fn", "deps", "is_dma", "semid", "sig", "sigval", "waits", "needed")

    def __init__(self, eng, fn, is_dma=False):
        self.eng = eng
        self.fn = fn
        self.deps = []
        self.is_dma = is_dma
        self.semid = None
        self.sig = False
        self.sigval = 0
        self.waits = []
        self.needed = False


class Sched:
    N_DMA_SEMS = 80

    def __init__(self, nc, stack):
        self.nc = nc
        self.eng_sem = {e: stack.enter_context(nc.semaphore("s_" + e)) for e in ENGS}
        self.dma_sems = [stack.enter_context(nc.semaphore("d%d" % i)) for i in range(self.N_DMA_SEMS)]
        self.eng_cnt = {e: 0 for e in ENGS}
        self.dma_cnt = [0] * self.N_DMA_SEMS
        self.waited = {}
        self.ops = []
        self.fence = None
        self.free_sems = list(range(24, self.N_DMA_SEMS))
        self.free_sems_sw = list(range(24))
        self.phase_tiles = []
        self.n_inst = 0

    def tile(self, name):
        t = Tile(name)
        self.phase_tiles.append(t)
        return t

    def tiles(self, name, n):
        return [self.tile("%s%d" % (name, i)) for i in range(n)]

    def _track(self, op, r, w):
        deps = op.deps
        if self.fence is not None:
            deps.append(self.fence)
        for t in r:
            deps.extend(t.writers)
            t.readers.append(op)
        for t in w:
            if t.readers:
                deps.extend(x for x in t.readers if x is not op)
                deps.extend(t.writers)
                t.writers = [op]
                t.readers = []
            else:
                if op.is_dma and t.writers and all(x.is_dma for x in t.writers):
                    t.writers.append(op)
                else:
                    deps.extend(t.writers)
                    t.writers = [op]
        self.ops.append(op)

    def op(self, eng, fn, r=(), w=()):
        o = Op(eng, fn)
        self._track(o, r, w)
        return o

    def dma(self, q, fn, r=(), w=(), key=None):
        o = Op(q, fn, is_dma=True)
        assert key is not None
        if key.sem is None:
            key.sem = self.free_sems_sw.pop() if q == "pool" else self.free_sems.pop()
        o.semid = key.sem
        self._track(o, r, w)
        return o

    def barrier(self):
        b = Op("sp", lambda e: e.nop())
        b.needed = True
        last = {}
        for o in self.ops:
            if o.is_dma:
                last[("d", o.semid)] = o
            else:
                last[o.eng] = o
        b.deps = list(last.values())
        if self.fence is not None:
            b.deps.append(self.fence)
        self.ops.append(b)
        self.fence = b
        for t in self.phase_tiles:
            t.writers = []
            t.readers = []

    def release_phase(self):
        for t in self.phase_tiles:
            if t.sem is not None:
                (self.free_sems_sw if t.sem < 24 else self.free_sems).append(t.sem)
                t.sem = None
        self.phase_tiles = []

    def emit(self):
        ops = self.ops
        self.ops = []
        for o in ops:
            for d in o.deps:
                if d.eng == "pe" and o.eng == "pe" and not d.is_dma and not o.is_dma:
                    continue
                d.needed = True
        for o in ops:
            if o.is_dma:
                self.dma_cnt[o.semid] += 16
                o.sigval = self.dma_cnt[o.semid]
            elif o.needed:
                self.eng_cnt[o.eng] += 1
                o.sigval = self.eng_cnt[o.eng]
                o.sig = True
            req = {}
            for d in o.deps:
                if d.is_dma:
                    k = ("d", d.semid)
                else:
                    if d.eng == "pe" and o.eng == "pe" and not o.is_dma:
                        continue
                    k = ("e", d.eng)
                if d.sigval > req.get(k, 0):
                    req[k] = d.sigval
            for k, v in req.items():
                wk = (o.eng, k)
                if self.waited.get(wk, 0) >= v:
                    continue
                self.waited[wk] = v
                o.waits.append((k, v))
        per = {e: [] for e in ENGS}
        for o in ops:
            per[o.eng].append(o)
        self.n_inst += len(ops)

        def run(e, lst):
            for o in lst:
                for (k, v) in o.waits:
                    sem = self.dma_sems[k[1]] if k[0] == "d" else self.eng_sem[k[1]]
                    e.wait_ge(sem, v)
                ins = o.fn(e)
                if o.is_dma:
                    ins.then_inc(self.dma_sems[o.semid], 16)
                elif o.sig:
                    ins.then_inc(self.eng_sem[o.eng], 1)

        with self.nc.Block() as block:
            @block.tensor
            def _(e):
                run(e, per["pe"])

            @block.vector
            def _(e):
                run(e, per["dve"])

            @block.scalar
            def _(e):
                run(e, per["act"])

            @block.gpsimd
            def _(e):
                run(e, per["pool"])

            @block.sync
            def _(e):
                run(e, per["sp"])


class Prog:
    def __init__(self, nc, sched, stack):
        self.nc = nc
        self.s = sched
        self.stack = stack
        self.dram = {}

    def sb(self, stack, name, shape, dt):
        return stack.enter_context(self.nc.sbuf_tensor(name, list(shape), dt))

    def ps(self, stack, name, shape, dt):
        return stack.enter_context(self.nc.psum_tensor(name, list(shape), dt))


def setup_consts(pg):
    nc, s = pg.nc, pg.s
    st = pg.stack
    pg.ident = pg.sb(st, "ident", [P, P], BF16)
    pg.ones_bf = pg.sb(st, "ones_bf", [P, P], BF16)
    pg.identf = pg.sb(st, "identf", [P, P], F32)
    t_id = s.tile("ident")
    s.op("pool", lambda e: e.memset(pg.identf[:], 1.0), w=[t_id])
    s.op("pool", lambda e: e.affine_select(pg.identf[:], pg.identf[:], [[-1, P]], ALU.is_equal, 0.0,
                                           base=0, channel_multiplier=1), r=[t_id], w=[t_id])
    s.op("pool", lambda e: e.tensor_copy(pg.ident[:], pg.identf[:]), r=[t_id], w=[t_id])
    s.op("pool", lambda e: e.memset(pg.ones_bf[:], 1.0), w=[t_id])
    pg.t_const = t_id
    pg.tri_le3 = pg.sb(st, "tri_le3", [P, 1, P], F32)
    pg.tri_ge3 = pg.sb(st, "tri_ge3", [P, 1, P], F32)
    s.op("pool", lambda e: e.memset(pg.tri_le3[:], 0.0), w=[t_id])
    s.op("pool", lambda e: e.affine_select(pg.tri_le3[:], pg.tri_le3[:], [[1, P]], ALU.is_ge, NEG,
                                           base=0, channel_multiplier=-1), r=[t_id], w=[t_id])
    s.op("pool", lambda e: e.memset(pg.tri_ge3[:], 0.0), w=[t_id])
    s.op("pool", lambda e: e.affine_select(pg.tri_ge3[:], pg.tri_ge3[:], [[-1, P]], ALU.is_ge, NEG,
                                           base=0, channel_multiplier=1), r=[t_id], w=[t_id])
    pg.tri_bf = pg.sb(st, "tri_bf", [P, 2, P], BF16)
    s.op("pool", lambda e: e.tensor_copy(pg.tri_bf[:, 0:1, :], pg.tri_le3[:]), r=[t_id], w=[t_id])
    s.op("pool", lambda e: e.tensor_copy(pg.tri_bf[:, 1:2, :], pg.tri_ge3[:]), r=[t_id], w=[t_id])
    pg.one_col = pg.sb(st, "one_col", [P, 1], F32)
    s.op("pool", lambda e: e.memset(pg.one_col[:], 1.0), w=[t_id])
    pg.neg_half = pg.sb(st, "neg_half", [P, 8], F32)
    s.op("pool", lambda e: e.memset(pg.neg_half[:], -0.5), w=[t_id])
    pg.eps_rms = pg.sb(st, "eps_rms", [P, 1], F32)
    s.op("pool", lambda e: e.memset(pg.eps_rms[:], RMS_EPS), w=[t_id])


def ln_tail(pg, xw, t_xw, gb, t_gb, xo_dram_rows, t_xo_dram, xTo, t_xTo, col0, bufs, i, staged=False):
    s = pg.s
    st6 = bufs["st6"][i % 2]
    t_st = bufs["t_st"][i % 2]
    mv = bufs["mv"][i % 2]
    xbf = bufs["xbf"][i % 2]
    t_xbf = bufs["t_xbf"][i % 2]
    psT = bufs["psT"][i % 2]
    t_psT = bufs["t_psT"][i % 2]
    g_bc, b_bc = gb
    s.op("dve", lambda e: e.bn_stats(st6[:, 0:6], xw[:, 0:512]), r=[t_xw], w=[t_st])
    s.op("dve", lambda e: e.bn_stats(st6[:, 6:12], xw[:, 512:1024]), r=[t_xw], w=[t_st])
    s.op("dve", lambda e: e.bn_aggr(mv[:, 0:2], st6[:, 0:12]), r=[t_st], w=[t_st])

    def sB():
        s.op("act", lambda e: e.activation(mv[:, 2:3], mv[:, 1:2], AF.Sqrt, bias=pg.eps_ln[:, 0:1], scale=1.0),
             r=[t_st, pg.t_const], w=[t_st])
        s.op("dve", lambda e: e.reciprocal(mv[:, 3:4], mv[:, 2:3]), r=[t_st], w=[t_st])
        s.op("dve", lambda e: e.scalar_tensor_tensor(mv[:, 4:5], mv[:, 0:1], -1.0, mv[:, 3:4], ALU.mult, ALU.mult),
             r=[t_st], w=[t_st])

    def sC():
        s.op("act", lambda e: e.activation(xw[:], xw[:], AF.Identity, bias=mv[:, 4:5], scale=mv[:, 3:4]),
             r=[t_xw, t_st], w=[t_xw])

    def sD():
        s.op("pool", lambda e: e.tensor_tensor(xw[:], xw[:], g_bc[:], ALU.mult), r=[t_xw, t_gb], w=[t_xw])
        s.op("pool", lambda e: e.tensor_tensor(xw[:], xw[:], b_bc[:], ALU.add), r=[t_xw, t_gb], w=[t_xw])

    def sE():
        s.dma("sp", lambda e: e.dma_start(out=xo_dram_rows, in_=xw[:]), r=[t_xw], w=[t_xo_dram], key=t_xw)
        s.op("act", lambda e: e.copy(xbf[:], xw[:]), r=[t_xw], w=[t_xbf])

    def sF():
        for j in range(8):
            s.op("pe", lambda e, j=j: e.transpose(psT[:, j, :], xbf[:, j * P:(j + 1) * P], pg.ident[:]),
                 r=[t_xbf, pg.t_const], w=[t_psT])
        s.op("dve", lambda e: e.tensor_copy(xTo[:, :, col0:col0 + P], psT[:]), r=[t_psT], w=[t_xTo])

    if staged:
        return [sB, sC, sD, sE, sF]
    sB(); sC(); sD(); sE()
    return [sF]


def alloc_ln_bufs(pg, st, tag):
    s = pg.s
    b = {}
    b["st6"] = [pg.sb(st, "st6%s%d" % (tag, i), [P, 12], F32) for i in range(2)]
    b["mv"] = [pg.sb(st, "mv%s%d" % (tag, i), [P, 8], F32) for i in range(2)]
    b["t_st"] = s.tiles("t_st" + tag, 2)
    b["xbf"] = [pg.sb(st, "xbf%s%d" % (tag, i), [P, D], BF16) for i in range(2)]
    b["t_xbf"] = s.tiles("t_xbf" + tag, 2)
    b["psT"] = [pg.ps(st, "psT%s%d" % (tag, i), [P, 8, P], BF16) for i in range(1)] * 2
    b["t_psT"] = s.tiles("t_psT" + tag, 1) * 2
    return b


def load_gb(pg, st, g_ap, b_ap, tag):
    s = pg.s
    g_bc = pg.sb(st, "g_bc" + tag, [P, D], F32)
    b_bc = pg.sb(st, "b_bc" + tag, [P, D], F32)
    t_gb = s.tile("t_gb" + tag)
    s.dma("sp", lambda e: e.dma_start(out=g_bc[:], in_=g_ap.partition_broadcast(P)), w=[t_gb], key=t_gb)
    s.dma("sp", lambda e: e.dma_start(out=b_bc[:], in_=b_ap.partition_broadcast(P)), w=[t_gb], key=t_gb)
    return (g_bc, b_bc), t_gb


def phase_pre(pg, x_in, xT_dram):
    nc, s = pg.nc, pg.s
    with ExitStack() as st:
        xin = [pg.sb(st, "pre_x%d" % i, [P, D], F32) for i in range(3)]
        t_xin = s.tiles("pre_tx", 3)
        xbf = [pg.sb(st, "pre_xbf%d" % i, [P, D], BF16) for i in range(2)]
        t_xbf = s.tiles("pre_txbf", 2)
        psT = [pg.ps(st, "pre_psT%d" % i, [P, 8, P], BF16) for i in range(2)]
        t_psT = s.tiles("pre_tps", 2)
        xTo = [pg.sb(st, "pre_xTo%d" % i, [P, 8, TT], BF16) for i in range(2)]
        t_xTo = s.tiles("pre_txTo", 2)
        xT_v = xT_dram.rearrange("(c p) t -> p c t", p=P)
        nblk = S // P
        t_xT = pg.dram["xT"]

        def load(b):
            s.dma("sp", lambda e: e.dma_start(out=xin[b % 3][:], in_=x_in[b * P:(b + 1) * P, :]),
                  w=[t_xin[b % 3]], key=t_xin[b % 3])
        load(0)
        load(1)
        for b in range(nblk):
            if b + 2 < nblk:
                load(b + 2)
            tt, sub = divmod(b, 4)
            cur_xbf, cur_t = xbf[b % 2], t_xbf[b % 2]
            s.op("act", lambda e, b=b, o=cur_xbf: e.copy(o[:], xin[b % 3][:]), r=[t_xin[b % 3]], w=[cur_t])
            pt, tpt = psT[b % 2], t_psT[b % 2]
            for j in range(8):
                s.op("pe", lambda e, j=j, pt=pt, o=cur_xbf: e.transpose(pt[:, j, :], o[:, j * P:(j + 1) * P],
                                                                       pg.ident[:]),
                     r=[cur_t, pg.t_const], w=[tpt])
            xo, txo = xTo[tt % 2], t_xTo[tt % 2]
            s.op("dve", lambda e, pt=pt, xo=xo, sub=sub: e.tensor_copy(xo[:, :, sub * P:(sub + 1) * P], pt[:]),
                 r=[tpt], w=[txo])
            if sub == 3:
                s.dma("sp", lambda e, xo=xo, tt=tt: e.dma_start(out=xT_v[:, :, tt * TT:(tt + 1) * TT], in_=xo[:]),
                      r=[txo], w=[t_xT[tt]], key=txo)
        s.barrier()
        s.emit()
    s.release_phase()


def phase_ffn(pg, L, x_old_dram, xT_dram, w1, w2, g_ap, b_ap, x_new_dram, xT_new_dram):
    nc, s = pg.nc, pg.s
    tag = "f%d" % L
    with ExitStack() as st:
        w1_sb = pg.sb(st, "w1" + tag, [P, 8, DFF], BF16)
        w2_sb = pg.sb(st, "w2" + tag, [P, 32, D], BF16)
        t_w1 = s.tiles("t_w1", 8)
        t_w2 = s.tiles("t_w2", 8)
        w1_v = w1.rearrange("(c p) f -> p c f", p=P)
        w2_v = w2.rearrange("(c p) d -> p c d", p=P)
        t_w1a = s.tiles("t_w1a", 4)
        for c in range(4):
            s.dma("pool", lambda e, c=c: e.dma_start(out=w1_sb[:, :, c * P:(c + 1) * P],
                                                     in_=w1_v[:, :, c * P:(c + 1) * P]),
                  w=[t_w1a[c]], key=t_w1a[c])
        for c in range(1, 8):
            s.dma("pool", lambda e, c=c: e.dma_start(out=w1_sb[:, :, c * 512:(c + 1) * 512],
                                                     in_=w1_v[:, :, c * 512:(c + 1) * 512]),
                  w=[t_w1[c]], key=t_w1[c])
        for c in range(8):
            s.dma("pool", lambda e, c=c: e.dma_start(out=w2_sb[:, 4 * c:4 * c + 4, :], in_=w2_v[:, 4 * c:4 * c + 4, :]),
                  w=[t_w2[c]], key=t_w2[c])
        gb, t_gb = load_gb(pg, st, g_ap, b_ap, tag)
        xT = [pg.sb(st, "xT%s%d" % (tag, i), [P, 8, TT], BF16) for i in range(2)]
        t_xTb = s.tiles("t_xTb", 2)
        hT = pg.sb(st, "hT" + tag, [P, 32, TT], BF16)
        t_hT = s.tiles("t_hT", 32)
        rl = [pg.sb(st, "rl%s%d" % (tag, i), [P, TT], F32) for i in range(2)]
        t_rl = s.tiles("t_rl", 2)
        xw = [pg.sb(st, "xw%s%d" % (tag, i), [P, D], F32) for i in range(3)]
        t_xw = s.tiles("t_xw", 3)
        ps_h = [pg.ps(st, "psh%s%d" % (tag, i), [P, TT], F32) for i in range(3)]
        t_psh = s.tiles("t_psh", 3)
        ps_y = [pg.ps(st, "psy%s%d" % (tag, i), [P, D], F32) for i in range(2)]
        t_psy = s.tiles("t_psy", 2)
        lnb = alloc_ln_bufs(pg, st, tag)
        xT_v = xT_dram.rearrange("(c p) t -> p c t", p=P)
        xTn_v = xT_new_dram.rearrange("(c p) t -> p c t", p=P)
        t_xT = pg.dram["xT"]
        t_xo = pg.dram["xo"]
        t_xi = pg.dram["xi"]

        def load_xT(t):
            s.dma("sp", lambda e: e.dma_start(out=xT[t % 2][:], in_=xT_v[:, :, t * TT:(t + 1) * TT]),
                  r=[t_xT[t]], w=[t_xTb[t % 2]], key=t_xTb[t % 2])

        def load_xold(b):
            s.dma("sp", lambda e: e.dma_start(out=xw[b % 3][:], in_=x_old_dram[b * P:(b + 1) * P, :]),
                  r=[t_xi[b // 4]], w=[t_xw[b % 3]], key=t_xw[b % 3])

        load_xT(0)
        if NT > 1:
            load_xT(1)
        hcnt = 0
        pend = []

        def flush():
            while pend:
                pend.pop(0)()

        sched_h = {}
        for t in range(NT):
            xTt, txT = xT[t % 2], t_xTb[t % 2]
            sched_h = {}
            slots = [0, 2, 4, 6, 8, 10, 12]
            for k, fn_ in enumerate(pend):
                sched_h.setdefault(slots[min(k, len(slots) - 1)], []).append(fn_)
            del pend[:]
            for c in range(32):
                ph, tph = ps_h[hcnt % 3], t_psh[hcnt % 3]
                r_, tr_ = rl[hcnt % 2], t_rl[hcnt % 2]
                hcnt += 1
                for kc in range(8):
                    s.op("pe", lambda e, c=c, kc=kc, ph=ph, xTt=xTt: e.matmul(
                        ph[:], w1_sb[:, kc, c * P:(c + 1) * P], xTt[:, kc, :], start=(kc == 0), stop=(kc == 7)),
                        r=[t_w1a[c] if c < 4 else t_w1[c // 4], txT], w=[tph])
                s.op("act", lambda e, ph=ph, r_=r_: e.activation(r_[:], ph[:], AF.Relu), r=[tph], w=[tr_])
                s.op("dve", lambda e, c=c, r_=r_: e.tensor_tensor(hT[:, c, :], r_[:], r_[:], ALU.mult),
                     r=[tr_], w=[t_hT[c]])
                if sched_h.get(c):
                    for fn_ in sched_h.pop(c):
                        fn_()
            for sub in range(4):
                b = t * 4 + sub
                load_xold(b)
                py, tpy = ps_y[b % 2], t_psy[b % 2]
                for c in range(32):
                    for half in range(2):
                        s.op("pe", lambda e, c=c, half=half, py=py, sub=sub: e.matmul(
                            py[:, half * 512:(half + 1) * 512], hT[:, c, sub * P:(sub + 1) * P],
                            w2_sb[:, c, half * 512:(half + 1) * 512], start=(c == 0), stop=(c == 31)),
                            r=[t_hT[c], t_w2[c // 4]], w=[tpy])
                xw_, txw_ = xw[b % 3], t_xw[b % 3]
                s.op("dve", lambda e, xw_=xw_, py=py: e.scalar_tensor_tensor(
                    xw_[:], xw_[:], float(ALPHA), py[:], ALU.mult, ALU.add), r=[txw_, tpy], w=[txw_])
                staged = (sub == 3 and t + 1 < NT)
                tail = ln_tail(pg, xw_, txw_, gb, t_gb, x_new_dram[b * P:(b + 1) * P, :], t_xo[t],
                               xTt, txT, sub * P, lnb, b, staged=staged)
                flush()
                pend.extend(tail)
                if sub == 3:
                    def fin(t=t, xTt=xTt, txT=txT):
                        s.dma("sp", lambda e: e.dma_start(out=xTn_v[:, :, t * TT:(t + 1) * TT], in_=xTt[:]),
                              r=[txT], w=[t_xT[t]], key=txT)
                        if t + 2 < NT:
                            load_xT(t + 2)
                    pend.append(fin)
        flush()
        s.barrier()
        s.emit()
    s.release_phase()


NEG = -30000.0
DA_H = 8


def slopes(n):
    return [2.0 ** (-8.0 * (h + 1) / n) for h in range(n)]


def dram_tile(pg, name):
    if name not in pg.dram:
        pg.dram[name] = Tile(name)
    return pg.dram[name]


def phase_proj(pg, tag, xT_dram, w_ap, ncols, jobs):
    s = pg.s
    with ExitStack() as st:
        xT = pg.sb(st, "pjx" + tag, [P, 8, S], BF16)
        xT_v = xT_dram.rearrange("(c p) t -> p c t", p=P)
        NXT = S // 512
        t_x = s.tiles("pj_tx", NXT)
        for c in range(NXT):
            s.dma("sp", lambda e, c=c: e.dma_start(out=xT[:, :, c * 512:(c + 1) * 512],
                                                   in_=xT_v[:, :, c * 512:(c + 1) * 512]),
                  r=pg.dram["xT"], w=[t_x[c]], key=t_x[c])
        w_sb = pg.sb(st, "pjw" + tag, [P, 8, ncols], BF16)
        NWB = (ncols + 511) // 512
        t_w = s.tiles("pj_tw", NWB)
        w_v = w_ap.rearrange("(c p) n -> p c n", p=P)
        worder = []
        for job in jobs:
            for wb in range(job["c0"] // 512, (job["c0"] + job["n"] + 511) // 512):
                if wb not in worder:
                    worder.append(wb)
        for wb in worder:
            hi = min(ncols, (wb + 1) * 512)
            s.dma("pool", lambda e, wb=wb, hi=hi: e.dma_start(out=w_sb[:, :, wb * 512:hi], in_=w_v[:, :, wb * 512:hi]),
                  w=[t_w[wb]], key=t_w[wb])
        ps = [pg.ps(st, "pjps%s%d" % (tag, i), [P, 512], F32) for i in range(4)]
        t_ps = s.tiles("pj_tps", 4)
        stg = {}
        cnt = {"ps": 0, "ev": 0}

        def staging(kind, dt, n):
            key = (kind, dt, n)
            if key not in stg:
                nm = "pjs%s%d" % (tag, len(stg))
                bufs = [pg.sb(st, nm + "_%d" % i, [P, n], dt) for i in range(2)]
                stg[key] = [bufs, s.tiles(nm, 2), 0]
            ent = stg[key]
            i = ent[2] % 2
            ent[2] += 1
            return ent[0][i], ent[1][i]

        def evac(dst_ap, src_ap, scale, rt, wt):
            i = cnt["ev"]
            cnt["ev"] += 1
            if i % 2 == 0:
                if scale == 1.0:
                    s.op("act", lambda e: e.copy(dst_ap, src_ap), r=[rt], w=[wt])
                else:
                    s.op("act", lambda e: e.mul(dst_ap, src_ap, float(scale)), r=[rt], w=[wt])
            else:
                if scale == 1.0:
                    s.op("dve", lambda e: e.tensor_copy(dst_ap, src_ap), r=[rt], w=[wt])
                else:
                    s.op("dve", lambda e: e.tensor_scalar(dst_ap, src_ap, float(scale), None, ALU.mult),
                         r=[rt], w=[wt])

        for job in jobs:
            c0, n, dst, dt = job["c0"], job["n"], job["dst"], job["dt"]
            scale = job.get("scale", 1.0)
            t_dst = dram_tile(pg, job["name"])
            if job["kind"] == "fm":
                dil = job.get("dil", 1)
                for cb in range((n + P - 1) // P):
                    m = min(P, n - cb * P)
                    stage, t_stage = staging("fm", dt, S)
                    for t in range(S // 512):
                        bank, tb = ps[cnt["ps"] % 4], t_ps[cnt["ps"] % 4]
                        cnt["ps"] += 1
                        for kc in range(8):
                            lhs = w_sb[:, kc, c0 + cb * P:c0 + cb * P + m]
                            rhs = xT[:, kc, t * 512:(t + 1) * 512]
                            s.op("pe", lambda e, kc=kc, bank=bank, m=m, lhs=lhs, rhs=rhs: e.matmul(
                                bank[:m, :], lhs, rhs, start=(kc == 0), stop=(kc == 7)),
                                r=[t_w[(c0 + cb * P) // 512], t_x[t]], w=[tb])
                        if dil == 1:
                            evac(stage[:m, t * 512:(t + 1) * 512], bank[:m, :], scale, tb, t_stage)
                        else:
                            wd = 512 // dil
                            src_ap = bank[:m, :].rearrange("p (l r) -> p r l", r=dil)
                            dst_ap = stage[:m, :].rearrange("p (r l) -> p r l", r=dil)[:, :, t * wd:(t + 1) * wd]
                            evac(dst_ap, src_ap, scale, tb, t_stage)
                    s.dma("sp", lambda e, o_=dst[cb * P:cb * P + m, :], i_=stage[:m, :]: e.dma_start(out=o_, in_=i_),
                          r=[t_stage], w=[t_dst], key=t_stage)
            else:
                blocks = job.get("blocks") or [(b * P, 1) for b in range(S // P)]
                for bi, (start, step) in enumerate(blocks):
                    stage, t_stage = staging("tm", dt, n)
                    xdeps = [t_x[j] for j in range(start // 512, (start + (P - 1) * step) // 512 + 1)]
                    for cg in range((n + 511) // 512):
                        wcol = min(512, n - cg * 512)
                        bank, tb = ps[cnt["ps"] % 4], t_ps[cnt["ps"] % 4]
                        cnt["ps"] += 1
                        for kc in range(8):
                            if step == 1:
                                lhs = xT[:, kc, start:start + P]
                            else:
                                lhs = xT[:, kc, start:start + (P - 1) * step + 1:step]
                            rhs = w_sb[:, kc, c0 + cg * 512:c0 + cg * 512 + wcol]
                            s.op("pe", lambda e, kc=kc, bank=bank, lhs=lhs, rhs=rhs, wcol=wcol: e.matmul(
                                bank[:, :wcol], lhs, rhs, start=(kc == 0), stop=(kc == 7)),
                                r=[t_w[(c0 + cg * 512) // 512]] + xdeps, w=[tb])
                        evac(stage[:, cg * 512:cg * 512 + wcol], bank[:, :wcol], scale, tb, t_stage)
                    s.dma("sp", lambda e, o_=dst[bi * P:(bi + 1) * P, :], i_=stage[:]: e.dma_start(out=o_, in_=i_),
                          r=[t_stage], w=[t_dst], key=t_stage)
        s.barrier()
        s.emit()
    s.release_phase()


def phase_outproj(pg, L, o_dram, w_out, x_old_dram, g_ap, b_ap, x_new_dram, xT_new_dram, nd_drams=None):
    s = pg.s
    tag = "o%d" % L
    NBX = 8
    with ExitStack() as st:
        w_sb = pg.sb(st, "wo" + tag, [P, 8, D], BF16)
        t_w = s.tile("t_wo")
        w_v = w_out.rearrange("(c p) n -> p c n", p=P)
        for c in range(2):
            s.dma("pool", lambda e, c=c: e.dma_start(out=w_sb[:, 4 * c:4 * c + 4, :], in_=w_v[:, 4 * c:4 * c + 4, :]),
                  w=[t_w], key=t_w)
        (g_bc, b_bc), t_gb = load_gb(pg, st, g_ap, b_ap, tag)
        NOB = 4
        ob = [pg.sb(st, "ob%s%d" % (tag, i), [P, D], BF16) for i in range(NOB)]
        t_ob = s.tiles("t_ob", NOB)
        if nd_drams is not None:
            ndt = [[pg.sb(st, "ndt%s%d%d" % (tag, i, g), [P, 8, 129], F32) for g in range(3)] for i in range(NOB)]
            t_ndt = [[s.tile("t_ndt") for g in range(3)] for i in range(NOB)]
            rec = [pg.sb(st, "rec%s%d" % (tag, i), [P, 8], F32) for i in range(NOB)]
            t_nd = dram_tile(pg, "nd")
        oT = [pg.sb(st, "oT%s%d" % (tag, i), [P, 8, P], BF16) for i in range(2)]
        t_oT = s.tiles("t_oT", 2)
        psT = [pg.ps(st, "psTo%s%d" % (tag, i), [P, 8, P], BF16) for i in range(2)]
        t_psT = s.tiles("t_psTo", 2)
        ps_y = [pg.ps(st, "psy%s%d" % (tag, i), [P, D], F32) for i in range(2)]
        t_psy = s.tiles("t_psy", 2)
        xw = [pg.sb(st, "xw%s%d" % (tag, i), [P, D], F32) for i in range(NBX)]
        t_xw = s.tiles("t_xw", NBX)
        st6 = [pg.sb(st, "st6%s%d" % (tag, i), [P, 12], F32) for i in range(NBX)]
        mv = [pg.sb(st, "mv%s%d" % (tag, i), [P, 8], F32) for i in range(NBX)]
        t_st = s.tiles("t_st", NBX)
        xbf = [pg.sb(st, "xbf%s%d" % (tag, i), [P, D], BF16) for i in range(3)]
        t_xbf = s.tiles("t_xbf", 3)
        psX = pg.ps(st, "psX" + tag, [P, 8, P], BF16)
        t_psX = s.tile("t_psX")
        xTo = [pg.sb(st, "xTo%s%d" % (tag, i), [P, 8, TT], BF16) for i in range(3)]
        t_xTo = s.tiles("t_xTo", 3)
        xTn_v = xT_new_dram.rearrange("(c p) t -> p c t", p=P)
        t_o = dram_tile(pg, "o_tm")
        t_xT = pg.dram["xT"]
        nblk = S // P

        def load(b):
            i = b % NOB
            if nd_drams is None:
                DMA(s, "sp", ob[i][:], o_dram[b * P:(b + 1) * P, :], r=[t_o], w=[t_ob[i]], key=t_ob[i])
            else:
                for g in range(3):
                    DMA(s, "sp", ndt[i][g][:], nd_drams[g][b * P:(b + 1) * P, :].rearrange("p (h c) -> p h c", h=8),
                        r=[t_nd], w=[t_ndt[i][g]], key=t_ndt[i][g])
            DMA(s, "sp", xw[b % NBX][:], x_old_dram[b * P:(b + 1) * P, :], r=[pg.dram["xi"][b // 4]],
                w=[t_xw[b % NBX]], key=t_xw[b % NBX])

        def stage_pre(b):
            if nd_drams is None:
                return
            i = b % NOB
            n0, n1, n2 = ndt[i]
            t0, t1, t2 = t_ndt[i]
            OP(s, "pool", "tensor_tensor", n0[:], n0[:], n1[:], ALU.add, r=[t0, t1], w=[t0])
            OP(s, "pool", "tensor_tensor", n0[:], n0[:], n2[:], ALU.add, r=[t0, t2], w=[t0])
            OP(s, "dve", "reciprocal", rec[i][:], n0[:, :, 128], r=[t0], w=[t0])
            OP(s, "dve", "tensor_tensor", ob[i][:].rearrange("p (h c) -> p h c", h=8), n0[:, :, 0:128],
               rec[i][:].unsqueeze(2).to_broadcast([P, 8, P]), ALU.mult, r=[t0], w=[t_ob[i]])

        def stage_a(b):
            pt, tpt = psT[b % 2], t_psT[b % 2]
            o_, to_ = ob[b % NOB], t_ob[b % NOB]
            for j in range(8):
                OP(s, "pe", "transpose", pt[:, j, :], o_[:, j * P:(j + 1) * P], pg.ident[:], r=[to_, pg.t_const],
                   w=[tpt])
            oT_, toT_ = oT[b % 2], t_oT[b % 2]
            OP(s, "act", "copy", oT_[:], pt[:], r=[tpt], w=[toT_])
            py, tpy = ps_y[b % 2], t_psy[b % 2]
            for kc in range(8):
                for half in range(2):
                    OP(s, "pe", "matmul", py[:, half * 512:(half + 1) * 512], oT_[:, kc, :],
                       w_sb[:, kc, half * 512:(half + 1) * 512], start=(kc == 0), stop=(kc == 7),
                       r=[toT_, t_w], w=[tpy])

        def stages(b):
            t, sub = divmod(b, 4)
            py, tpy = ps_y[b % 2], t_psy[b % 2]
            x_, tx_ = xw[b % NBX], t_xw[b % NBX]
            s6_, mv_, ts_ = st6[b % NBX], mv[b % NBX], t_st[b % NBX]
            xb_, txb_ = xbf[b % 3], t_xbf[b % 3]
            xo_, txo_ = xTo[t % 3], t_xTo[t % 3]

            def s1():
                OP(s, "dve", "scalar_tensor_tensor", x_[:], x_[:], float(ALPHA), py[:], ALU.mult, ALU.add,
                   r=[tx_, tpy], w=[tx_])
                OP(s, "dve", "bn_stats", s6_[:, 0:6], x_[:, 0:512], r=[tx_], w=[ts_])
                OP(s, "dve", "bn_stats", s6_[:, 6:12], x_[:, 512:1024], r=[tx_], w=[ts_])
                OP(s, "dve", "bn_aggr", mv_[:, 0:2], s6_[:, 0:12], r=[ts_], w=[ts_])

            def s2():
                OP(s, "act", "activation", mv_[:, 2:3], mv_[:, 1:2], AF.Sqrt, bias=pg.eps_ln[:, 0:1], scale=1.0,
                   r=[ts_, pg.t_const], w=[ts_])
                OP(s, "dve", "reciprocal", mv_[:, 3:4], mv_[:, 2:3], r=[ts_], w=[ts_])
                OP(s, "dve", "scalar_tensor_tensor", mv_[:, 4:5], mv_[:, 0:1], -1.0, mv_[:, 3:4], ALU.mult, ALU.mult,
                   r=[ts_], w=[ts_])

            def s3():
                OP(s, "act", "activation", x_[:], x_[:], AF.Identity, bias=mv_[:, 4:5], scale=mv_[:, 3:4],
                   r=[tx_, ts_], w=[tx_])

            def s4():
                OP(s, "dve" if nd_drams is not None else "pool", "tensor_tensor", x_[:], x_[:], g_bc[:], ALU.mult,
                   r=[tx_, t_gb], w=[tx_])
                OP(s, "dve", "tensor_tensor", x_[:], x_[:], b_bc[:], ALU.add, r=[tx_, t_gb], w=[tx_])

            def s5():
                DMA(s, "sp", x_new_dram[b * P:(b + 1) * P, :], x_[:], r=[tx_], w=[pg.dram["xo"][t]], key=tx_)
                OP(s, "act", "copy", xb_[:], x_[:], r=[tx_], w=[txb_])

            def s6():
                for j in range(8):
                    OP(s, "pe", "transpose", psX[:, j, :], xb_[:, j * P:(j + 1) * P], pg.ident[:],
                       r=[txb_, pg.t_const], w=[t_psX])
                OP(s, "act", "copy", xo_[:, :, sub * P:(sub + 1) * P], psX[:], r=[t_psX], w=[txo_])
                if sub == 3:
                    DMA(s, "sp", xTn_v[:, :, t * TT:(t + 1) * TT], xo_[:], r=[txo_], w=[t_xT[t]], key=txo_)
            return [s1, s2, s3, s4, s5, s6]

        for b in range(min(3, nblk)):
            load(b)
        for b in range(min(2, nblk)):
            stage_pre(b)
        stage_a(0)
        NST = 6
        all_st = {}
        for i in range(nblk + NST - 1):
            if i + 3 < nblk:
                load(i + 3)
            if i + 2 < nblk:
                stage_pre(i + 2)
            if i + 1 < nblk:
                stage_a(i + 1)
            if i < nblk:
                all_st[i] = stages(i)
            for k in range(NST):
                b = i - k
                if 0 <= b < nblk:
                    all_st[b][k]()
        s.barrier()
        s.emit()
    s.release_phase()


def build_aux(pg, specs):
    s = pg.s
    with ExitStack() as st:
        ia = pg.sb(st, "aux_ia", [1, S], I32)
        bB = pg.sb(st, "aux_bB", [1, S], BF16)
        bO = pg.sb(st, "aux_bO", [1, S], BF16)
        t_b = s.tile("t_auxb")
        OP(s, "pool", "iota", ia[:], [[0, S // P], [1, P]], base=0, channel_multiplier=0, w=[t_b])
        OP(s, "pool", "tensor_copy", bB[:], ia[:], r=[t_b], w=[t_b])
        OP(s, "pool", "memset", bO[:], 1.0, w=[t_b])
        ps = [pg.ps(st, "aux_ps%d" % i, [64, 512], F32) for i in range(2)]
        t_ps = s.tiles("t_auxps", 2)
        t_aq = dram_tile(pg, "auxda")
        cnt = 0
        for si, (aux_dram, mults, Lseq) in enumerate(specs):
            ib = pg.sb(st, "aux_ib%d" % si, [1, S], I32)
            bA = pg.sb(st, "aux_bA%d" % si, [1, S], BF16)
            mm = pg.sb(st, "aux_m%d" % si, [1, 3, 64], BF16)
            outb = pg.sb(st, "aux_o%d" % si, [64, S], BF16)
            t_s = s.tile("t_auxs%d" % si)
            t_o = s.tile("t_auxo%d" % si)
            nrep = S // Lseq
            OP(s, "pool", "iota", ib[:], [[0, nrep], [128, Lseq // P], [0, P]], base=0, channel_multiplier=0, w=[t_s])
            OP(s, "pool", "tensor_copy", bA[:], ib[:], r=[t_s], w=[t_s])
            OP(s, "pool", "memset", mm[:], 0.0, w=[t_s])
            mo = mm[:, 2, :].rearrange("p (h c) -> p h c", c=8)
            OP(s, "pool", "memset", mo[:, :, 2:6], 1.0, w=[t_s])
            for h in range(8):
                m = float(mults[h])
                OP(s, "pool", "memset", mm[:, 0, h * 8 + 0:h * 8 + 1], -m, w=[t_s])
                OP(s, "pool", "memset", mm[:, 0, h * 8 + 6:h * 8 + 7], m, w=[t_s])
                OP(s, "pool", "memset", mm[:, 1, h * 8 + 1:h * 8 + 2], -m, w=[t_s])
                OP(s, "pool", "memset", mm[:, 1, h * 8 + 7:h * 8 + 8], m, w=[t_s])
            for j in range(S // 512):
                p_, tp_ = ps[cnt % 2], t_ps[cnt % 2]
                cnt += 1
                cs = slice(j * 512, (j + 1) * 512)
                OP(s, "pe", "matmul", p_[:], mm[:, 0, :], bA[:, cs], start=True, stop=False, r=[t_s], w=[tp_])
                OP(s, "pe", "matmul", p_[:], mm[:, 1, :], bB[:, cs], start=False, stop=False, r=[t_s, t_b], w=[tp_])
                OP(s, "pe", "matmul", p_[:], mm[:, 2, :], bO[:, cs], start=False, stop=True, r=[t_s, t_b], w=[tp_])
                OP(s, "dve" if j % 2 else "act", "tensor_copy" if j % 2 else "copy", outb[:, cs], p_[:],
                   r=[tp_], w=[t_o])
            DMA(s, "sp", aux_dram.rearrange("h q r s -> (h q r) s"), outb[:], r=[t_o], w=[t_aq], key=t_o)
        s.barrier()
        s.emit()
    s.release_phase()


def phase_diffattn(pg, L, qT_dram, kT_dram, v_dram, aux, lam_aps, subln_g, lambda_init, o_dram):
    s = pg.s
    tag = "a%d" % L
    H = DA_H
    with ExitStack() as st:
        lamt = pg.sb(st, "lamt" + tag, [1, 4, 64], F32)
        lams = pg.sb(st, "lams" + tag, [1, 8], F32)
        junk = pg.sb(st, "lamj" + tag, [1, 64], F32)
        neglam = pg.sb(st, "neglam" + tag, [P, 1], F32)
        t_lam = s.tile("t_lam")
        t_nl = s.tile("t_nl")
        for i, ap in enumerate(lam_aps):
            s.dma("sp", lambda e, i=i, ap=ap: e.dma_start(out=lamt[:, i, :], in_=ap.rearrange("(o n) -> o n", o=1)),
                  w=[t_lam], key=t_lam)
        s.op("dve", lambda e: e.scalar_tensor_tensor(junk[:], lamt[:, 0, :], 1.0, lamt[:, 1, :], ALU.mult, ALU.mult,
                                                     accum_out=lams[:, 0:1]), r=[t_lam], w=[t_nl])
        s.op("dve", lambda e: e.scalar_tensor_tensor(junk[:], lamt[:, 2, :], 1.0, lamt[:, 3, :], ALU.mult, ALU.mult,
                                                     accum_out=lams[:, 1:2]), r=[t_lam], w=[t_nl])
        s.op("act", lambda e: e.activation(lams[:, 2:4], lams[:, 0:2], AF.Exp), r=[t_nl], w=[t_nl])
        s.op("dve", lambda e: e.scalar_tensor_tensor(lams[:, 4:5], lams[:, 3:4], float(-lambda_init), lams[:, 2:3],
                                                     ALU.add, ALU.subtract), r=[t_nl], w=[t_nl])
        t_lamd = dram_tile(pg, "lam_d")
        lam_d = pg.lam_dram
        s.dma("sp", lambda e: e.dma_start(out=lam_d[L:L + 1, :], in_=lams[:, 4:5]), r=[t_nl], w=[t_lamd], key=t_nl)
        s.dma("sp", lambda e: e.dma_start(out=neglam[:], in_=lam_d[L, :].partition_broadcast(P)),
              r=[t_lamd], w=[t_nl], key=t_nl)
        g_bc = pg.sb(st, "sg" + tag, [P, P], F32)
        t_g = s.tile("t_sg")
        s.dma("sp", lambda e: e.dma_start(out=g_bc[:], in_=subln_g.partition_broadcast(P)), w=[t_g], key=t_g)
        s.op("pool", lambda e: e.tensor_scalar(g_bc[:], g_bc[:], float(1.0 - lambda_init), None, ALU.mult),
             r=[t_g], w=[t_g])
        qa = [[pg.sb(st, "qa%s%d%d" % (tag, i, m), [P, S], BF16) for m in range(2)] for i in range(2)]
        ka = [[pg.sb(st, "ka%s%d%d" % (tag, i, m), [P, S], BF16) for m in range(2)] for i in range(2)]
        vaug = [pg.sb(st, "va%s%d" % (tag, i), [P, S // P, 129], BF16) for i in range(2)]
        t_qkv = s.tiles("t_qkv", 2)
        for i in range(2):
            for m in range(2):
                s.op("pool", lambda e, i=i, m=m: e.memset(qa[i][m][64:128, :], 0.0), w=[t_qkv[i]])
                s.op("pool", lambda e, i=i, m=m: e.memset(ka[i][m][64:128, :], 0.0), w=[t_qkv[i]])
            s.op("pool", lambda e, i=i: e.memset(vaug[i][:, :, 128:129], 1.0), w=[t_qkv[i]])
        t_q = dram_tile(pg, "qT")
        t_aux = dram_tile(pg, "auxda")
        v_v = v_dram.rearrange("(n p) d -> p n d", p=P)

        def load_head(h):
            i = h % 2
            for m in range(2):
                r0 = h * 128 + m * 64
                s.dma("sp", lambda e, m=m, r0=r0: e.dma_start(out=qa[i][m][0:64, :], in_=qT_dram[r0:r0 + 64, :]),
                      r=[t_q], w=[t_qkv[i]], key=t_qkv[i])
                s.dma("sp", lambda e, m=m, r0=r0: e.dma_start(out=ka[i][m][0:64, :], in_=kT_dram[r0:r0 + 64, :]),
                      r=[t_q], w=[t_qkv[i]], key=t_qkv[i])
                s.dma("sp", lambda e, m=m: e.dma_start(out=qa[i][m][64:68, :], in_=aux[h, 0, :, :]),
                      r=[t_aux], w=[t_qkv[i]], key=t_qkv[i])
                s.dma("sp", lambda e, m=m: e.dma_start(out=ka[i][m][64:68, :], in_=aux[h, 1, :, :]),
                      r=[t_aux], w=[t_qkv[i]], key=t_qkv[i])
            s.dma("sp", lambda e: e.dma_start(out=vaug[i][:, :, 0:128], in_=v_v[:, :, h * 128:(h + 1) * 128]),
                  r=[t_q], w=[t_qkv[i]], key=t_qkv[i])

        oacc = [pg.sb(st, "oacc%s%d" % (tag, i), [P, 2, 4, 129], F32) for i in range(2)]
        t_oacc = s.tiles("t_oacc", 2)
        sm = [pg.sb(st, "sm%s%d" % (tag, i), [P, 32], F32) for i in range(2)]
        t_sm = s.tiles("t_sm", 2)
        otmp = [pg.sb(st, "otmp%s%d" % (tag, i), [P, 2, 4, P], F32) for i in range(2)]
        t_otmp = s.tiles("t_otmp", 2)
        ost = [pg.sb(st, "ost%s%d" % (tag, i), [P, 4, P], BF16) for i in range(2)]
        t_ost = s.tiles("t_ost", 2)
        t_o = dram_tile(pg, "o_tm")
        o_v = o_dram.rearrange("(n p) d -> p n d", p=P)
        NTL = S // 512
        pending = []
        AX = mybir.AxisListType

        def post(h, t, ob):
            oa, sm_, ot = oacc[ob], sm[ob], otmp[ob]
            toa, tsm, tot = t_oacc[ob], t_sm[ob], t_otmp[ob]
            bc = lambda ap: ap.unsqueeze(2).to_broadcast([P, 4, P])
            OP(s, "dve", "reciprocal", sm_[:, 0:8].rearrange("p (a b) -> p a b", a=2), oa[:, :, :, 128],
               r=[toa], w=[tsm])
            OP(s, "dve", "tensor_scalar", sm_[:, 8:12], sm_[:, 4:8], neglam[:, 0:1], None, ALU.mult,
               r=[tsm, t_nl], w=[tsm])
            OP(s, "dve", "tensor_tensor", ot[:, 0], oa[:, 0, :, 0:128], bc(sm_[:, 0:4]), ALU.mult,
               r=[toa, tsm], w=[tot])
            OP(s, "dve", "tensor_tensor", ot[:, 1], oa[:, 1, :, 0:128], bc(sm_[:, 8:12]), ALU.mult,
               r=[toa, tsm], w=[tot])
            OP(s, "dve", "tensor_tensor", ot[:, 0], ot[:, 0], ot[:, 1], ALU.add, r=[tot], w=[tot])
            OP(s, "dve", "tensor_tensor", ot[:, 1], ot[:, 0], ot[:, 0], ALU.mult, r=[tot], w=[tot])
            OP(s, "dve", "tensor_reduce", sm_[:, 12:16], ot[:, 1], AX.X, ALU.add, r=[tot], w=[tsm])
            OP(s, "dve", "tensor_scalar", sm_[:, 16:20], sm_[:, 12:16], 1.0 / 128.0, float(RMS_EPS), ALU.mult,
               ALU.add, r=[tsm], w=[tsm])
            OP(s, "pool", "tensor_tensor", sm_[:, 20:24], sm_[:, 16:20], pg.neg_half[:, 0:4], ALU.pow,
               r=[tsm, pg.t_const], w=[tsm])
            OP(s, "dve", "tensor_tensor", ot[:, 0], ot[:, 0], bc(sm_[:, 20:24]), ALU.mult, r=[tot, tsm], w=[tot])
            OP(s, "dve", "tensor_tensor", ost[ob][:], ot[:, 0], g_bc[:].unsqueeze(1).to_broadcast([P, 4, P]),
               ALU.mult, r=[tot, t_g], w=[t_ost[ob]])
            DMA(s, "sp", o_v[:, t * 4:(t + 1) * 4, h * 128:(h + 1) * 128], ost[ob][:], r=[t_ost[ob]], w=[t_o],
                key=t_ost[ob])

        def make_copy(h, t, m, qb):
            ob = (h * NTL + t) % 2

            def cp(acc, t_acc):
                OP(s, "dve", "tensor_copy", oacc[ob][:, m, qb, :], acc[qb][:, 0:129], r=[t_acc[qb]],
                   w=[t_oacc[ob]])
            return cp

        def make_end(h, t, m):
            ob = (h * NTL + t) % 2

            def end(acc, t_acc):
                if m == 0:
                    while pending:
                        pending.pop(0)()
                else:
                    pending.append(lambda: post(h, t, ob))
            return end

        units = []
        head_first_unit = {}
        sl_h = slopes(H)
        for h in range(H):
            i = h % 2
            head_first_unit[h] = len(units)
            wkeep = 0
            while sl_h[h] * (128 * (wkeep + 1) - 127) < 88.0:
                wkeep += 1
            for t in range(NTL):
                for m in range(2):
                    q_, k_, v_ = qa[i][m], ka[i][m], vaug[i]
                    reads = [t_qkv[i]]
                    kb_first = max(0, 4 * t - wkeep)
                    nd = list(range(kb_first, 4 * t))
                    groups = []
                    if len(nd) % 2 == 1:
                        groups.append(nd[:1])
                        nd = nd[1:]
                    for x0 in range(0, len(nd), 2):
                        groups.append(nd[x0:x0 + 2])
                    first_kb = kb_first if 4 * t > kb_first else 4 * t
                    for grp in groups:
                        u = dict(reads=reads, mm=[], exp=[(0, len(grp), 0, 512)], pv=[])
                        for bi, kb in enumerate(grp):
                            u["mm"].append((bi, 0, 512, [(k_[:, kb * P:(kb + 1) * P], q_[:, t * 512:(t + 1) * 512])]))
                            for qb in range(4):
                                u["pv"].append((qb, bi, qb * P, v_[:, kb, :], kb == first_kb, False))
                        units.append(u)
                    tri2 = pg.tri_le3[:, 0, :].unsqueeze(1).to_broadcast([P, 2, P])
                    for j0 in (0, 2):
                        w0 = 512 - P * j0
                        u = dict(reads=reads, mm=[], exp=[(0, 2, 0, w0)], pv=[])
                        for bi in range(2):
                            j = j0 + bi
                            kb = 4 * t + j
                            u["mm"].append((bi, 0, 512 - P * j, [(k_[:, kb * P:(kb + 1) * P],
                                                                  q_[:, t * 512 + P * j:(t + 1) * 512]),
                                                                 (pg.ident[:], pg.tri_bf[:, 0, :], 0, P)]))
                            for qb in range(j, 4):
                                u["pv"].append((qb, bi, (qb - j) * P, v_[:, kb, :], kb == first_kb, qb == j,
                                                make_copy(h, t, m, qb) if qb == j else None))
                        if j0 == 2:
                            u["end"] = make_end(h, t, m)
                        units.append(u)
        load_head(0)
        bounds = [head_first_unit[h] for h in range(H)] + [len(units)]
        for h in range(H):
            if h + 1 < H:
                load_head(h + 1)
            attn_core_run(pg, st, tag, units[bounds[h]:bounds[h + 1]], h == 0)
        while pending:
            pending.pop(0)()
        s.barrier()
        s.emit()
    s.release_phase()


DIL_GROUPS = ((128, 1), (512, 4), (2048, 16))


def phase_dilattn(pg, L, g, dil, qT_dram, kT_dram, v_dram, nd_dram):
    s = pg.s
    tag = "d%d%d" % (L, g)
    H = 8
    Lg = S // dil
    nb = Lg // P
    NB = S // P
    sl8 = slopes(8)
    with ExitStack() as st:
        qa = [pg.sb(st, "qa%s%d" % (tag, i), [P, S], BF16) for i in range(2)]
        ka = [pg.sb(st, "ka%s%d" % (tag, i), [P, S], BF16) for i in range(2)]
        vaug = [pg.sb(st, "va%s%d" % (tag, i), [P, NB, 129], BF16) for i in range(2)]
        t_qkv = s.tiles("t_qkv", 2)
        for i in range(2):
            s.op("pool", lambda e, i=i: e.memset(vaug[i][:, :, 128:129], 1.0), w=[t_qkv[i]])
        idist = pg.sb(st, "idist" + tag, [P, 256], I32)
        dist = pg.sb(st, "dist" + tag, [P, 256], F32)
        mbase = pg.sb(st, "mbase" + tag, [P, 2, P], F32)
        t_dm = s.tile("t_dm")
        OP(s, "pool", "iota", idist[:], [[1, 256]], base=0, channel_multiplier=-1, w=[t_dm])
        OP(s, "pool", "tensor_copy", dist[:], idist[:], r=[t_dm], w=[t_dm])
        OP(s, "pool", "tensor_copy", mbase[:, 0:1, :], pg.tri_le3[:], r=[pg.t_const], w=[t_dm])
        OP(s, "pool", "tensor_copy", mbase[:, 1:2, :], pg.tri_ge3[:], r=[pg.t_const], w=[t_dm])
        tbl = [pg.sb(st, "tbl%s%d" % (tag, i), [P, 2, 512], F32) for i in range(2)]
        t_tbl = s.tiles("t_tbl", 2)
        t_q = dram_tile(pg, "qT")
        v_v = v_dram.rearrange("(n p) d -> p n d", p=P)

        def load_head(h):
            i = h % 2
            DMA(s, "sp", qa[i][:], qT_dram[h * P:(h + 1) * P, :], r=[t_q], w=[t_qkv[i]], key=t_qkv[i])
            DMA(s, "sp", ka[i][:], kT_dram[h * P:(h + 1) * P, :], r=[t_q], w=[t_qkv[i]], key=t_qkv[i])
            DMA(s, "sp", vaug[i][:, :, 0:128], v_v[:, :, h * P:(h + 1) * P], r=[t_q], w=[t_qkv[i]], key=t_qkv[i])
            m = -float(sl8[h] * dil)
            OP(s, "dve", "scalar_tensor_tensor", tbl[i][:].rearrange("p a (b c) -> p (a b) c", c=256),
               dist[:].unsqueeze(1).to_broadcast([P, 4, 256]), m,
               mbase[:].rearrange("p a b -> p (a b)").unsqueeze(1).to_broadcast([P, 4, 256]), ALU.mult, ALU.add,
               r=[t_dm], w=[t_tbl[i]])

        ndh = [pg.sb(st, "ndh%s%d" % (tag, i), [P, NB, 129], F32) for i in range(2)]
        t_ndh = s.tiles("t_ndh", 2)
        t_nd = dram_tile(pg, "nd")
        nd_v = nd_dram.rearrange("(n p r) c -> p r n c", p=P, r=dil)

        def make_evac(h, beta):
            a = beta % 4

            def evac(acc, t_acc):
                k = h % 2
                OP(s, "dve", "tensor_copy", ndh[k][:, beta, :], acc[a][:, 0:129], r=[t_acc[a]], w=[t_ndh[k]])
                if beta == NB - 1:
                    src_ap = ndh[k][:].rearrange("p (r n) c -> p r n c", r=dil)
                    dst_ap = nd_v[:, :, :, h * 129:(h + 1) * 129]
                    DMA(s, "sp", dst_ap, src_ap, r=[t_ndh[k]], w=[t_nd], key=t_ndh[k])
            return evac

        load_head(0)
        for h in range(H):
            if h + 1 < H:
                load_head(h + 1)
            i = h % 2
            q_, k_, v_ = qa[i], ka[i], vaug[i]
            units = []
            for b0 in range(0, NB, 4):
                u = dict(reads=[t_qkv[i]], mm=[], mask=[(0, 2, 0, 512, tbl[i][:], t_tbl[i])],
                         exp=[(0, 2, 0, 512)], pv=[])
                for sl in range(4):
                    beta = b0 + sl
                    r_, n_ = divmod(beta, nb)
                    has_next = (n_ + 1 < nb)
                    width = 256 if has_next else 128
                    bank, off = sl // 2, (sl % 2) * 256
                    u["mm"].append((bank, off, width, [
                        (k_[:, beta * P:(beta + 1) * P], q_[:, beta * P:beta * P + width])]))
                    u["pv"].append((beta % 4, bank, off, v_[:, beta, :], n_ == 0, True, make_evac(h, beta)))
                    if has_next:
                        u["pv"].append(((beta + 1) % 4, bank, off + P, v_[:, beta, :], True, False, None))
                units.append(u)
            attn_core_run(pg, st, tag, units, h == 0)
        s.barrier()
        s.emit()
    s.release_phase()


def phase_dilcombine(pg, L, nd_drams, o_dram):
    s = pg.s
    tag = "c%d" % L
    with ExitStack() as st:
        nd = [[pg.sb(st, "nd%s%d%d" % (tag, i, g), [P, 8, 129], F32) for g in range(3)] for i in range(2)]
        t_ndb = [[s.tile("t_ndb") for g in range(3)] for i in range(2)]
        rec = [pg.sb(st, "rec%s%d" % (tag, i), [P, 8], F32) for i in range(2)]
        ob = [pg.sb(st, "ob%s%d" % (tag, i), [P, D], BF16) for i in range(2)]
        t_ob = s.tiles("t_ob", 2)
        t_nd = dram_tile(pg, "nd")
        t_o = dram_tile(pg, "o_tm")
        nblk = S // P

        def load(b):
            for g in range(3):
                s.dma("sp", lambda e, g=g: e.dma_start(
                    out=nd[b % 2][g][:], in_=nd_drams[g][b * P:(b + 1) * P, :].rearrange("p (h c) -> p h c", h=8)),
                    r=[t_nd], w=[t_ndb[b % 2][g]], key=t_ndb[b % 2][g])
        load(0)
        for b in range(nblk):
            if b + 1 < nblk:
                load(b + 1)
            n0, n1, n2 = nd[b % 2]
            t0, t1, t2 = t_ndb[b % 2]
            s.op("pool", lambda e, n0=n0, n1=n1: e.tensor_tensor(n0[:], n0[:], n1[:], ALU.add), r=[t0, t1], w=[t0])
            s.op("pool", lambda e, n0=n0, n2=n2: e.tensor_tensor(n0[:], n0[:], n2[:], ALU.add), r=[t0, t2], w=[t0])
            rc = rec[b % 2]
            s.op("dve", lambda e, n0=n0, rc=rc: e.reciprocal(rc[:], n0[:, :, 128]), r=[t0], w=[t0])
            for h in range(8):
                eng = "dve" if h % 2 == 0 else "act"
                if eng == "dve":
                    s.op("dve", lambda e, h=h, n0=n0, rc=rc, b=b: e.tensor_scalar(
                        ob[b % 2][:, h * P:(h + 1) * P], n0[:, h, 0:128], rc[:, h:h + 1], None, ALU.mult),
                        r=[t0], w=[t_ob[b % 2]])
                else:
                    s.op("act", lambda e, h=h, n0=n0, rc=rc, b=b: e.activation(
                        ob[b % 2][:, h * P:(h + 1) * P], n0[:, h, 0:128], AF.Copy, scale=rc[:, h:h + 1]),
                        r=[t0], w=[t_ob[b % 2]])
            s.dma("sp", lambda e, b=b: e.dma_start(out=o_dram[b * P:(b + 1) * P, :], in_=ob[b % 2][:]),
                  r=[t_ob[b % 2]], w=[t_o], key=t_ob[b % 2])
        s.barrier()
        s.emit()
    s.release_phase()


def OP(s, eng, name, *args, r=(), w=(), **kw):
    return s.op(eng, lambda e: getattr(e, name)(*args, **kw), r=list(r), w=list(w))


def DMA(s, q, out, in_, r=(), w=(), key=None):
    return s.dma(q, lambda e: e.dma_start(out=out, in_=in_), r=list(r), w=list(w), key=key)


def phase_gla(pg, L, gqT, gkT, gk_tm, v_dram, gr_tm, glT_dram, w_gate2, b_gate, gnorm_g, o_dram):
    s = pg.s
    tag = "g%d" % L
    HG, DK, DV = 4, 128, 256
    NCH = S // P
    I16 = 1.0 / 16.0
    with ExitStack() as st:
        sb = lambda name, shape, dt: pg.sb(st, name + tag, shape, dt)
        tri_incl = sb("tri_incl", [P, 4, P], F32)
        sgt = sb("sgt", [P, P], F32)
        ones_row = sb("ones_row", [1, P], F32)
        bg = sb("bg", [1, 512], F32)
        wg2 = sb("wg2", [16, 512], F32)
        glT = sb("glT", [16, S], F32)
        gn_bc = sb("gn_bc", [P, DV], F32)
        t_c = s.tile("t_glac")
        OP(s, "pool", "memset", tri_incl[:], 1.0, w=[t_c])
        OP(s, "pool", "affine_select", tri_incl[:], tri_incl[:], [[0, 4], [1, P]], ALU.is_ge, 0.0,
           base=0, channel_multiplier=-1, r=[t_c], w=[t_c])
        OP(s, "pool", "memset", sgt[:], 1.0, w=[t_c])
        OP(s, "pool", "affine_select", sgt[:], sgt[:], [[-1, P]], ALU.is_gt, 0.0,
           base=0, channel_multiplier=1, r=[t_c], w=[t_c])
        OP(s, "pool", "memset", ones_row[:], 1.0, w=[t_c])
        t_ld = s.tile("t_glald")
        DMA(s, "sp", bg[:], b_gate.rearrange("(o n) -> o n", o=1), w=[t_ld], key=t_ld)
        DMA(s, "sp", wg2[:], w_gate2, w=[t_ld], key=t_ld)
        DMA(s, "sp", glT[:], glT_dram, r=[dram_tile(pg, "qT")], w=[t_ld], key=t_ld)
        DMA(s, "sp", gn_bc[:], gnorm_g.partition_broadcast(P), w=[t_ld], key=t_ld)
        def dbl(name, shape, dt):
            return [sb("%s%d" % (name, i), shape, dt) for i in range(2)], s.tiles("t_" + name, 2)
        qT, t_qT = dbl("qT", [P, 4, P], F32)
        kT, t_kT = dbl("kT", [P, 4, P], F32)
        ktm, t_ktm = dbl("ktm", [P, 512], F32)
        vv, t_vv = dbl("vv", [P, D], BF16)
        rr, t_rr = dbl("rr", [P, D], F32)
        e1, t_e1 = dbl("e1", [P, 512], F32)
        eq, t_eq = dbl("eq", [P, 4, P], F32)
        ek, t_ek = dbl("ek", [P, 4, P], F32)
        es, t_es = dbl("es", [P, 512], F32)
        qd, t_qd = dbl("qd", [P, 4, P], BF16)
        ki, t_ki = dbl("ki", [P, 4, P], BF16)
        kst, t_kst = dbl("kst", [P, 512], BF16)
        sT, t_sT = dbl("sT", [P, 4, P], BF16)
        sg, t_sg = dbl("sg", [P, D], F32)
        ot, t_ot = dbl("ot", [P, D], F32)
        ost, t_ost = dbl("ost", [P, D], BF16)
        sm, t_sm = dbl("sm", [P, 16], F32)
        junk, t_junk = dbl("junk", [P, DV], F32)
        state = sb("state", [P, 4, DV], F32)
        state_bf = sb("state_bf", [P, 4, DV], BF16)
        t_state = s.tiles("t_state", 4)
        t_sbf = s.tiles("t_sbf", 4)
        ps_z = pg.ps(st, "psz" + tag, [P, 512], F32)
        ps_c = pg.ps(st, "psc" + tag, [P, 4, P], F32)
        ps_r = ps_z
        ps_s = ps_c
        ps_o2 = [pg.ps(st, "pso%s%d" % (tag, i), [P, 4, DV], F32) for i in range(2)]
        ps_kv = pg.ps(st, "pskv" + tag, [P, 4, DV], F32)
        t_psz, t_psc = s.tile("t_psz"), s.tile("t_psc")
        t_psr, t_pss = t_psz, t_psc
        t_pso2 = [s.tiles("t_pso", 2) for i in range(2)]
        t_pskv = [t for t in s.tiles("t_pskv", 2) for _ in range(2)]
        t_q = dram_tile(pg, "qT")
        t_o = dram_tile(pg, "o_tm")
        gq_v = gqT.rearrange("(h p) t -> p h t", p=P)
        gk_v = gkT.rearrange("(h p) t -> p h t", p=P)

        def load(c):
            i = c % 2
            cs = slice(c * P, (c + 1) * P)
            DMA(s, "sp", qT[i][:], gq_v[:, :, cs], r=[t_q], w=[t_qT[i]], key=t_qT[i])
            DMA(s, "sp", kT[i][:], gk_v[:, :, cs], r=[t_q], w=[t_kT[i]], key=t_kT[i])
            DMA(s, "sp", ktm[i][:], gk_tm[cs, :], r=[t_q], w=[t_ktm[i]], key=t_ktm[i])
            DMA(s, "sp", vv[i][:], v_dram[cs, :], r=[t_q], w=[t_vv[i]], key=t_vv[i])
            DMA(s, "sp", rr[i][:], gr_tm[cs, :], r=[t_q], w=[t_rr[i]], key=t_rr[i])

        def prep(c):
            i = c % 2
            cs = slice(c * P, (c + 1) * P)
            OP(s, "pe", "matmul", ps_z[:], glT[:, cs], wg2[:], start=True, stop=False, r=[t_ld], w=[t_psz])
            OP(s, "pe", "matmul", ps_z[:], ones_row[:], bg[:], start=False, stop=True, r=[t_ld, t_c], w=[t_psz])
            OP(s, "act", "activation", e1[i][:], ps_z[:], AF.Exp, scale=-1.0, r=[t_psz], w=[t_e1[i]])
            OP(s, "act", "activation", e1[i][:], e1[i][:], AF.Ln, bias=pg.one_col[:, 0:1], scale=1.0,
               r=[t_e1[i], pg.t_const], w=[t_e1[i]])
            for h in range(HG):
                OP(s, "pe", "matmul", ps_c[:, h, :], e1[i][:, h * P:(h + 1) * P], tri_incl[:, 0, :],
                   start=True, stop=True, r=[t_e1[i], t_c], w=[t_psc])
            OP(s, "pe", "matmul", ps_r[:], sgt[:], e1[i][:], start=True, stop=True, r=[t_e1[i], t_c], w=[t_psr])
            OP(s, "act", "activation", eq[i][:], ps_c[:], AF.Exp, scale=-I16, r=[t_psc], w=[t_eq[i]])
            OP(s, "act", "activation", ek[i][:], ps_c[:], AF.Exp, scale=I16, r=[t_psc], w=[t_ek[i]])
            OP(s, "act", "activation", es[i][:], ps_r[:], AF.Exp, scale=-I16, r=[t_psr], w=[t_es[i]])
            OP(s, "dve", "tensor_tensor", qd[i][:], qT[i][:], eq[i][:], ALU.mult, r=[t_qT[i], t_eq[i]], w=[t_qd[i]])
            OP(s, "pool", "tensor_tensor", ki[i][:], kT[i][:], ek[i][:], ALU.mult, r=[t_kT[i], t_ek[i]], w=[t_ki[i]])
            OP(s, "pool", "tensor_tensor", kst[i][:], ktm[i][:], es[i][:], ALU.mult, r=[t_ktm[i], t_es[i]],
               w=[t_kst[i]])
            for h in range(HG):
                OP(s, "pe", "matmul", ps_s[:, h, :], ki[i][:, h, :], qd[i][:, h, :], start=True, stop=True,
                   r=[t_ki[i], t_qd[i]], w=[t_pss])
            OP(s, "dve", "tensor_tensor", sT[i][:], ps_s[:], tri_incl[:], ALU.mult, r=[t_pss, t_c], w=[t_sT[i]])
            OP(s, "act", "activation", sg[i][:], rr[i][:], AF.Exp, scale=-1.0, r=[t_rr[i]], w=[t_sg[i]])
            OP(s, "act", "activation", sg[i][:], sg[i][:], AF.Ln, bias=pg.one_col[:, 0:1], scale=1.0,
               r=[t_sg[i], pg.t_const], w=[t_sg[i]])
            OP(s, "act", "activation", sg[i][:], sg[i][:], AF.Exp, scale=-1.0, r=[t_sg[i]], w=[t_sg[i]])
            OP(s, "pool", "tensor_tensor", sg[i][:], sg[i][:], rr[i][:], ALU.mult, r=[t_sg[i], t_rr[i]], w=[t_sg[i]])

        def recur(c):
            i = c % 2
            ps_o = ps_o2[c % 2]
            t_pso = t_pso2[c % 2]
            for h in range(HG):
                tpo = t_pso[h // 2]
                vs = vv[i][:, h * DV:(h + 1) * DV]
                if c > 0:
                    OP(s, "pe", "matmul", ps_o[:, h, :], qd[i][:, h, :], state_bf[:, h, :], start=True, stop=False,
                       r=[t_qd[i], t_sbf[h]], w=[tpo])
                OP(s, "pe", "matmul", ps_o[:, h, :], sT[i][:, h, :], vs, start=(c == 0), stop=True,
                   r=[t_sT[i], t_vv[i]], w=[tpo])
            for h in range(HG):
                vs = vv[i][:, h * DV:(h + 1) * DV]
                OP(s, "pe", "matmul", ps_kv[:, h, :], kst[i][:, h * P:(h + 1) * P], vs, start=True, stop=True,
                   r=[t_kst[i], t_vv[i]], w=[t_pskv[h]])
                if c == 0:
                    OP(s, "dve", "tensor_copy", state[:, h, :], ps_kv[:, h, :], r=[t_pskv[h]], w=[t_state[h]])
                else:
                    OP(s, "dve", "scalar_tensor_tensor", state[:, h, :], state[:, h, :], eq[i][:, h, P - 1:P],
                       ps_kv[:, h, :], ALU.mult, ALU.add, r=[t_pskv[h], t_eq[i], t_state[h]], w=[t_state[h]])
                if c + 1 < NCH:
                    OP(s, "act", "copy", state_bf[:, h, :], state[:, h, :], r=[t_state[h]], w=[t_sbf[h]])
            for h in range(HG):
                OP(s, "act", "activation", junk[i][:], ps_o[:, h, :], AF.Square, accum_out=sm[i][:, h:h + 1],
                   r=[t_pso[h // 2]], w=[t_junk[i], t_sm[i]])
            OP(s, "dve", "tensor_scalar", sm[i][:, 4:8], sm[i][:, 0:4], 1.0 / DV, float(RMS_EPS), ALU.mult, ALU.add,
               r=[t_sm[i]], w=[t_sm[i]])
            OP(s, "pool", "tensor_tensor", sm[i][:, 8:12], sm[i][:, 4:8], pg.neg_half[:, 0:4], ALU.pow,
               r=[t_sm[i], pg.t_const], w=[t_sm[i]])
            for h in range(HG):
                OP(s, "dve", "scalar_tensor_tensor", ot[i][:, h * DV:(h + 1) * DV], ps_o[:, h, :],
                   sm[i][:, 8 + h:9 + h], gn_bc[:], ALU.mult, ALU.mult, r=[t_pso[h // 2], t_sm[i], t_ld],
                   w=[t_ot[i]])
            OP(s, "pool", "tensor_tensor", ost[i][:], ot[i][:], sg[i][:], ALU.mult, r=[t_ot[i], t_sg[i]],
               w=[t_ost[i]])
            DMA(s, "sp", o_dram[c * P:(c + 1) * P, :], ost[i][:], r=[t_ost[i]], w=[t_o], key=t_ost[i])

        load(0)
        prep(0)
        for c in range(NCH):
            if c + 1 < NCH:
                load(c + 1)
                prep(c + 1)
            recur(c)
        s.barrier()
        s.emit()
    s.release_phase()


_CORE_STATE = {}


def attn_core_run(pg, st, tag, units, first):
    s = pg.s
    if first:
        cs = {}
        cs["ps_s"] = [pg.ps(st, "pss%s%d" % (tag, i), [P, 2, 512], F32) for i in range(2)]
        cs["t_pss"] = s.tiles("t_pss", 2)
        cs["pT"] = [pg.sb(st, "pT%s%d" % (tag, i), [P, 2, 512], BF16) for i in range(3)]
        cs["t_pT"] = s.tiles("t_pT", 3)
        cs["acc"] = [pg.ps(st, "acc%s%d" % (tag, i), [P, 512], F32) for i in range(4)]
        cs["t_acc"] = s.tiles("t_acc", 4)
        cs["cnt"] = 0
        for i in range(2):
            s.op("dve", lambda e, i=i: e.memset(cs["ps_s"][i][:], 0.0), w=[cs["t_pss"][i]])
        _CORE_STATE[tag] = cs
    cs = _CORE_STATE[tag]
    ps_s, t_pss, pT, t_pT, acc, t_acc = cs["ps_s"], cs["t_pss"], cs["pT"], cs["t_pT"], cs["acc"], cs["t_acc"]

    def s_stage(u, i):
        ps, tps = ps_s[i % 2], t_pss[i % 2]
        p_, tp_ = pT[i % 3], t_pT[i % 3]
        for (bank, off, width, pairs) in u["mm"]:
            for pi, pr in enumerate(pairs):
                lhsT, rhs = pr[0], pr[1]
                o0, wd = (off + pr[2], pr[3]) if len(pr) > 2 else (off, width)
                s.op("pe", lambda e, bank=bank, o0=o0, wd=wd, lhsT=lhsT, rhs=rhs, ps=ps, pi=pi,
                     np_=len(pairs): e.matmul(ps[:, bank, o0:o0 + wd], lhsT, rhs, start=(pi == 0),
                                              stop=(pi == np_ - 1)), r=u["reads"] + [pg.t_const], w=[tps])
        for (bank0, nb, off, width, tab, ttab) in u.get("mask", []):
            s.op("dve", lambda e, bank0=bank0, nb=nb, off=off, width=width, tab=tab, ps=ps: e.tensor_tensor(
                ps[:, bank0:bank0 + nb, off:off + width], ps[:, bank0:bank0 + nb, off:off + width], tab, ALU.add),
                r=[tps, ttab], w=[tps])
        for (bank0, nb, off, width) in u["exp"]:
            s.op("act", lambda e, bank0=bank0, nb=nb, off=off, width=width, ps=ps, p_=p_: e.activation(
                p_[:, bank0:bank0 + nb, off:off + width], ps[:, bank0:bank0 + nb, off:off + width], AF.Exp),
                r=[tps], w=[tp_])

    def pv_stage(u, i):
        p_, tp_ = pT[i % 3], t_pT[i % 3]
        for ent in u["pv"]:
            (a, bank, off, vap, start, stop) = ent[:6]
            s.op("pe", lambda e, a=a, bank=bank, off=off, vap=vap, start=start, stop=stop, p_=p_: e.matmul(
                acc[a][:, 0:129], p_[:, bank, off:off + P], vap, start=start, stop=stop),
                r=[tp_] + u["reads"], w=[t_acc[a]])
            if len(ent) > 6 and ent[6] is not None:
                ent[6](acc, t_acc)
        if u.get("end") is not None:
            u["end"](acc, t_acc)

    n = len(units)
    base = cs["cnt"]
    for i in range(min(2, n)):
        s_stage(units[i], base + i)
    for i in range(n):
        if i + 2 < n:
            s_stage(units[i + 2], base + i + 2)
        pv_stage(units[i], base + i)
    cs["cnt"] = base + n


import math


def diff_lambda_init(layer_idx):
    return 0.8 - 0.6 * math.exp(-0.3 * layer_idx)


DIFF_IN = ["w_in", "lam_q1", "lam_k1", "lam_q2", "lam_k2", "subln_g", "w_out"]
DIFF_SHAPES = {"w_in": [D, 3072], "lam_q1": [64], "lam_k1": [64], "lam_q2": [64], "lam_k2": [64],
               "subln_g": [128], "w_out": [D, D]}
DIL_SHAPES = {"w_in": [D, 9216], "w_out": [D, D]}
GLA_SHAPES = {"w_in": [D, 3088], "w_gate2": [16, 512], "b_gate": [512], "gnorm_g": [256], "w_out": [D, D]}
FFN_SHAPES = {"ln1_g": [D], "ln1_b": [D], "w_ff1": [D, DFF], "w_ff2": [DFF, D], "ln2_g": [D], "ln2_b": [D]}


def layer_shapes(L):
    kind = L % 3
    d = dict([DIFF_SHAPES, DIL_SHAPES, GLA_SHAPES][kind])
    d.update(FFN_SHAPES)
    return d


def build_program(mode="full"):
    nc = bass.Bass("TRN2", target_bir_lowering=False)
    ins = {}

    def din(name, shape):
        ins[name] = nc.dram_tensor(name, list(shape), F32, kind="ExternalInput").ap()
        return ins[name]

    if mode == "full":
        layers = list(range(DEPTH))
    else:
        layers = [int(mode[-1])]
    x = din("x", [S, D])
    for L in layers:
        for k, shp in layer_shapes(L).items():
            din("l%d_%s" % (L, k), shp)
    out = nc.dram_tensor("out", [S, D], F32, kind="ExternalOutput").ap()

    def scratch(name, shape, dt):
        return nc.dram_tensor(name, list(shape), dt, kind="Internal").ap()

    xTa = scratch("xTa", [D, S], BF16)
    xa = scratch("xa", [S, D], F32)
    xb = scratch("xb", [S, D], F32)
    qT = scratch("qT", [D, S], BF16)
    kT = scratch("kT", [D, S], BF16)
    v_tm = scratch("v_tm", [S, D], BF16)
    o_tm = scratch("o_tm", [S, D], BF16)
    aux_da = scratch("aux_da", [8, 2, 4, S], BF16)

    with ExitStack() as stack:
        sched = Sched(nc, stack)
        pg = Prog(nc, sched, stack)
        pg.lam_dram = scratch("lam_d", [DEPTH, 1], F32)
        pg.eps_ln = pg.sb(stack, "eps_ln", [P, 1], F32)
        setup_consts(pg)
        sched.op("pool", lambda e: e.memset(pg.eps_ln[:], LN_EPS), w=[pg.t_const])
        pg.dram["xT"] = [Tile("xT%d" % i) for i in range(NT)]
        pg.dram["xo"] = [Tile("xo%d" % i) for i in range(NT)]
        pg.dram["xi"] = [Tile("xi%d" % i) for i in range(NT)]
        phase_pre(pg, x, xTa)
        if mode.startswith("ffn"):
            L = layers[0]
            p = "l%d_" % L
            phase_ffn(pg, L, x, xTa, ins[p + "w_ff1"], ins[p + "w_ff2"], ins[p + "ln2_g"], ins[p + "ln2_b"],
                      out, xTa)
            return nc, list(ins.keys())
        specs = []
        if any(L % 3 == 0 for L in layers):
            specs.append((aux_da, slopes(DA_H), S))
        if specs:
            build_aux(pg, specs)
        x_cur = x
        for li, L in enumerate(layers):
            p = "l%d_" % L
            kind = L % 3
            last = (li == len(layers) - 1)
            if kind == 0:
                jobs = [dict(kind="fm", name="qT", c0=0, n=1024, dst=qT, dt=BF16, scale=0.125),
                        dict(kind="fm", name="qT", c0=1024, n=1024, dst=kT, dt=BF16),
                        dict(kind="tm", name="qT", c0=2048, n=1024, dst=v_tm, dt=BF16)]
                phase_proj(pg, "p%d" % L, xTa, ins[p + "w_in"], 3072, jobs)
                phase_diffattn(pg, L, qT, kT, v_tm, aux_da,
                               [ins[p + "lam_q1"], ins[p + "lam_k1"], ins[p + "lam_q2"], ins[p + "lam_k2"]],
                               ins[p + "subln_g"], diff_lambda_init(L), o_tm)
            elif kind == 1:
                nds = []
                for g, (window, dil) in enumerate(DIL_GROUPS):
                    nd_g = scratch("nd%d" % g, [S, 8 * 129], F32)
                    nds.append(nd_g)
                    Lg = S // dil
                    blocks = [(r + dil * P * n, dil) for r in range(dil) for n in range(Lg // P)]
                    jobs = [dict(kind="fm", name="qT", c0=0, n=1024, dst=qT, dt=BF16, scale=128.0 ** -0.5, dil=dil),
                            dict(kind="fm", name="qT", c0=1024, n=1024, dst=kT, dt=BF16, dil=dil),
                            dict(kind="tm", name="qT", c0=2048, n=1024, dst=v_tm, dt=BF16, blocks=blocks)]
                    phase_proj(pg, "p%d%d" % (L, g), xTa, ins[p + "w_in"][:, g * 3072:(g + 1) * 3072], 3072, jobs)
                    phase_dilattn(pg, L, g, dil, qT, kT, v_tm, nd_g)
            else:
                gqT = scratch("gqT", [512, S], F32)
                gkT = scratch("gkT", [512, S], F32)
                gk_tm = scratch("gk_tm", [S, 512], F32)
                gr_tm = scratch("gr_tm", [S, D], F32)
                glT = scratch("glT", [16, S], F32)
                jobs = [dict(kind="fm", name="qT", c0=0, n=512, dst=gqT, dt=F32, scale=128.0 ** -0.5),
                        dict(kind="fm", name="qT", c0=512, n=512, dst=gkT, dt=F32),
                        dict(kind="fm", name="qT", c0=3072, n=16, dst=glT, dt=F32),
                        dict(kind="tm", name="qT", c0=512, n=512, dst=gk_tm, dt=F32),
                        dict(kind="tm", name="qT", c0=1024, n=1024, dst=v_tm, dt=BF16),
                        dict(kind="tm", name="qT", c0=2048, n=1024, dst=gr_tm, dt=F32)]
                phase_proj(pg, "p%d" % L, xTa, ins[p + "w_in"], 3088, jobs)
                phase_gla(pg, L, gqT, gkT, gk_tm, v_tm, gr_tm, glT, ins[p + "w_gate2"], ins[p + "b_gate"],
                          ins[p + "gnorm_g"], o_tm)
            mix_only = mode.startswith("mix")
            phase_outproj(pg, L, o_tm, ins[p + "w_out"], x_cur, ins[p + "ln1_g"], ins[p + "ln1_b"],
                          out if mix_only else xa, xTa, nd_drams=(nds if kind == 1 else None))
            if mix_only:
                break
            phase_ffn(pg, L, xa, xTa, ins[p + "w_ff1"], ins[p + "w_ff2"], ins[p + "ln2_g"], ins[p + "ln2_b"],
                      out if last else xb, xTa)
            x_cur = xb
    return nc, list(ins.keys())


INPUT_NAMES = (
    "x",
    "l0_w_in",
    "l0_lam_q1",
    "l0_lam_k1",
    "l0_lam_q2",
    "l0_lam_k2",
    "l0_subln_g",
    "l0_w_out",
    "l0_ln1_g",
    "l0_ln1_b",
    "l0_w_ff1",
    "l0_w_ff2",
    "l0_ln2_g",
    "l0_ln2_b",
    "l1_w_in",
    "l1_w_out",
    "l1_ln1_g",
    "l1_ln1_b",
    "l1_w_ff1",
    "l1_w_ff2",
    "l1_ln2_g",
    "l1_ln2_b",
    "l2_w_in",
    "l2_w_gate2",
    "l2_b_gate",
    "l2_gnorm_g",
    "l2_w_out",
    "l2_ln1_g",
    "l2_ln1_b",
    "l2_w_ff1",
    "l2_w_ff2",
    "l2_ln2_g",
    "l2_ln2_b",
    "l3_w_in",
    "l3_lam_q1",
    "l3_lam_k1",
    "l3_lam_q2",
    "l3_lam_k2",
    "l3_subln_g",
    "l3_w_out",
    "l3_ln1_g",
    "l3_ln1_b",
    "l3_w_ff1",
    "l3_w_ff2",
    "l3_ln2_g",
    "l3_ln2_b",
)


_CACHE = {}
LAST_EXEC_NS = None


def kernel(**inputs):
    mode = inputs.pop("_mode", "full")
    if mode not in _CACHE:
        _CACHE[mode] = build_program(mode)
    nc, names = _CACHE[mode]
    x = np.ascontiguousarray(inputs["x"], dtype=np.float32)
    in_maps = []
    for c in range(NCORES):
        m = {}
        for n in INPUT_NAMES:
            if n not in names:
                continue
            if n == "x":
                m[n] = np.ascontiguousarray(x[c])
            else:
                m[n] = np.ascontiguousarray(inputs[n], dtype=np.float32)
        in_maps.append(m)
    import os
    tr = bool(os.environ.get("KTRACE"))
    res = run_bass_kernel_spmd(nc, in_maps, core_ids=list(range(NCORES)), trace=tr)
    global LAST_EXEC_NS
    LAST_EXEC_NS = res.exec_time_ns
    return np.stack([r["out"] for r in res.results], axis=0)
```
